# Optimizing a Trainium2 kernel written in Bass

```python
import jax, jax.numpy as jnp
from jax import lax
import numpy as np

D_MODEL = 1024
BATCH = 4
SEQ = 4096
DEPTH = 1
DEC_BATCH = 16
DEC_SEQ = 16
PAST_LEN = 2048

CHUNK = 64
D_MIX = D_MODEL
D_POOL = D_MIX // 2
D_CONV = D_MIX - D_POOL
POOL_WINDOWS = (2, 4, 8, 16)
N_POOL_GROUPS = len(POOL_WINDOWS)
POOL_GROUP = D_POOL // N_POOL_GROUPS
POOL_STATE = max(POOL_WINDOWS) - 1
N_CONV_HEADS = 8
CONV_WIDTH = 3
CONV_STATE = CONV_WIDTH - 1
D_IN = D_POOL + 3 * D_CONV
N_MEM = 256
N_XHEADS = 4
XHEAD_DIM = D_MODEL // N_XHEADS
D_FF = 4 * D_MODEL
EPS = 1e-6

kernel_name = 'hybrid_pool_conv_stream_encoder_step'


def rmsnorm(x, g):
    x32 = x.astype(jnp.float32)
    y = x32 * lax.rsqrt(jnp.mean(jnp.square(x32), axis=-1, keepdims=True) + EPS)
    return (y * g.astype(jnp.float32)).astype(x.dtype)


def group_rmsnorm(y, g, n_groups):
    b, t, d = y.shape
    y32 = y.astype(jnp.float32).reshape(b, t, n_groups, d // n_groups)
    y32 = y32 * lax.rsqrt(jnp.mean(jnp.square(y32), axis=-1, keepdims=True) + EPS)
    return (y32.reshape(b, t, d) * g.astype(jnp.float32)).astype(y.dtype)


def pool_mix(u_ext, start_pos, w_pool, pool_scale):
    b, l, _ = u_ext.shape
    t = l - POOL_STATE
    u32 = u_ext.astype(jnp.float32)
    cs = jnp.cumsum(jnp.pad(u32, ((0, 0), (1, 0), (0, 0))), axis=1)
    pos = start_pos + jnp.arange(t, dtype=jnp.int32)
    lo = POOL_STATE + 1
    outs = []
    for gi, w in enumerate(POOL_WINDOWS):
        sl = slice(gi * POOL_GROUP, (gi + 1) * POOL_GROUP)
        win_sum = cs[:, lo:lo + t, sl] - cs[:, lo - w:lo - w + t, sl]
        cnt = jnp.minimum(pos + 1, w).astype(jnp.float32)[None, :, None]
        outs.append(win_sum / cnt - u32[:, POOL_STATE:, sl])
    pooled = jnp.stack(outs, axis=2)
    mapped = jnp.einsum('btgc,gcd->btgd', pooled, w_pool.astype(jnp.float32))
    return (mapped.reshape(b, t, D_POOL) * pool_scale.astype(jnp.float32)).astype(u_ext.dtype)


def conv_mix(bg, cg, h, conv_prev, w_conv):
    v = cg * h
    v_ext = jnp.concatenate([conv_prev.astype(v.dtype), v], axis=1)
    t = v.shape[1]
    z = sum(w_conv[k] * v_ext[:, k:k + t] for k in range(CONV_WIDTH))
    return bg * z, v_ext[:, -CONV_STATE:]


def memory_kv(mem, g_mem, w_k, w_v):
    m = rmsnorm(mem, g_mem)
    k = jnp.einsum('bmd,dhe->bmhe', m, w_k)
    v = jnp.einsum('bmd,dhe->bmhe', m, w_v)
    return k, v


def cross_attn(xn, mem_k, mem_v, w_q, w_co):
    q = jnp.einsum('btd,dhe->bthe', xn, w_q)
    s = jnp.einsum('bthe,bmhe->bhtm', q, mem_k).astype(jnp.float32) * (XHEAD_DIM ** -0.5)
    p = jax.nn.softmax(s, axis=-1).astype(mem_v.dtype)
    o = jnp.einsum('bhtm,bmhe->bthe', p, mem_v)
    return jnp.einsum('bthe,hed->btd', o, w_co)


def encoder_layer(x, pool_prev, conv_prev, start_pos, mem_k, mem_v,
                  g_mix_pre, w_in, w_pool, pool_scale, w_conv, g_pool_out, g_conv_out, w_out,
                  g_mix_post, g_x_pre, w_q, w_co, g_x_post, g_ff_pre, w_up, w_down, g_ff_post):
    xn = rmsnorm(x, g_mix_pre)
    proj = xn @ w_in
    u, bg, cg, h = jnp.split(proj, [D_POOL, D_POOL + D_CONV, D_POOL + 2 * D_CONV], axis=-1)
    u_ext = jnp.concatenate([pool_prev.astype(u.dtype), u], axis=1)
    y_pool = pool_mix(u_ext, start_pos, w_pool, pool_scale)
    y_conv, conv_state = conv_mix(bg, cg, h, conv_prev, w_conv)
    merged = jnp.concatenate([group_rmsnorm(y_pool, g_pool_out, N_POOL_GROUPS),
                              group_rmsnorm(y_conv, g_conv_out, N_CONV_HEADS)], axis=-1)
    x = x + rmsnorm(merged @ w_out, g_mix_post)
    x = x + rmsnorm(cross_attn(rmsnorm(x, g_x_pre), mem_k, mem_v, w_q, w_co), g_x_post)
    hid = jnp.square(jax.nn.relu(rmsnorm(x, g_ff_pre) @ w_up))
    x = x + rmsnorm(hid @ w_down, g_ff_post)
    return x, u_ext[:, -POOL_STATE:], conv_state


def setup_inputs(seed: int = 0) -> dict:
    key = jax.random.key(seed)
    ks = iter(jax.random.split(key, 40))

    def nrm(shape, scale):
        return jax.random.normal(next(ks), shape, jnp.float32) * scale

    def gain(shape):
        return 1.0 + nrm(shape, 0.05)

    L = DEPTH
    return {
        'x_prompt': nrm((BATCH, SEQ, D_MODEL), 1.0),
        'x_sample': nrm((DEC_BATCH, DEC_SEQ, D_MODEL), 1.0),
        'state_pool': nrm((L, DEC_BATCH, POOL_STATE, D_POOL), 1.0),
        'state_conv': nrm((L, DEC_BATCH, CONV_STATE, D_CONV), 1.0),
        'cache_mem_k': nrm((L, DEC_BATCH, N_MEM, N_XHEADS, XHEAD_DIM), 1.0),
        'cache_mem_v': nrm((L, DEC_BATCH, N_MEM, N_XHEADS, XHEAD_DIM), 1.0),
        'mem_prompt': nrm((BATCH, N_MEM, D_MODEL), 1.0),
        'g_mix_pre': gain((L, D_MODEL)),
        'w_in': nrm((L, D_MODEL, D_IN), D_MODEL ** -0.5),
        'w_pool': nrm((L, N_POOL_GROUPS, POOL_GROUP, POOL_GROUP), POOL_GROUP ** -0.5),
        'pool_scale': 0.5 + nrm((L, D_POOL), 0.1),
        'w_conv': nrm((L, CONV_WIDTH, D_CONV), CONV_WIDTH ** -0.5),
        'g_pool_out': gain((L, D_POOL)),
        'g_conv_out': gain((L, D_CONV)),
        'w_out': nrm((L, D_MIX, D_MODEL), D_MIX ** -0.5),
        'g_mix_post': gain((L, D_MODEL)),
        'g_mem': gain((L, D_MODEL)),
        'w_k': nrm((L, D_MODEL, N_XHEADS, XHEAD_DIM), D_MODEL ** -0.5),
        'w_v': nrm((L, D_MODEL, N_XHEADS, XHEAD_DIM), D_MODEL ** -0.5),
        'g_x_pre': gain((L, D_MODEL)),
        'w_q': nrm((L, D_MODEL, N_XHEADS, XHEAD_DIM), D_MODEL ** -0.5),
        'w_co': nrm((L, N_XHEADS, XHEAD_DIM, D_MODEL), D_MODEL ** -0.5),
        'g_x_post': gain((L, D_MODEL)),
        'g_ff_pre': gain((L, D_MODEL)),
        'w_up': nrm((L, D_MODEL, D_FF), D_MODEL ** -0.5),
        'w_down': nrm((L, D_FF, D_MODEL), D_FF ** -0.5),
        'g_ff_post': gain((L, D_MODEL)),
    }


def reference(x_prompt, x_sample, state_pool, state_conv, cache_mem_k, cache_mem_v, mem_prompt,
              g_mix_pre, w_in, w_pool, pool_scale, w_conv, g_pool_out, g_conv_out, w_out,
              g_mix_post, g_mem, w_k, w_v, g_x_pre, w_q, w_co, g_x_post,
              g_ff_pre, w_up, w_down, g_ff_post):
    yp, ys = x_prompt, x_sample
    bp = x_prompt.shape[0]
    pool_p, conv_p, mk_p, mv_p, pool_s, conv_s = [], [], [], [], [], []
    for l in range(DEPTH):
        lw = (g_mix_pre[l], w_in[l], w_pool[l], pool_scale[l], w_conv[l], g_pool_out[l],
              g_conv_out[l], w_out[l], g_mix_post[l], g_x_pre[l], w_q[l], w_co[l], g_x_post[l],
              g_ff_pre[l], w_up[l], w_down[l], g_ff_post[l])
        mk, mv = memory_kv(mem_prompt, g_mem[l], w_k[l], w_v[l])
        zero_pool = jnp.zeros((bp, POOL_STATE, D_POOL), yp.dtype)
        zero_conv = jnp.zeros((bp, CONV_STATE, D_CONV), yp.dtype)
        yp, sp, cp = encoder_layer(yp, zero_pool, zero_conv, 0, mk, mv, *lw)
        ys, ss, cs = encoder_layer(ys, state_pool[l], state_conv[l], PAST_LEN,
                                   cache_mem_k[l], cache_mem_v[l], *lw)
        pool_p.append(sp); conv_p.append(cp); mk_p.append(mk); mv_p.append(mv)
        pool_s.append(ss); conv_s.append(cs)
    return (yp, ys, jnp.stack(pool_p), jnp.stack(conv_p), jnp.stack(mk_p), jnp.stack(mv_p),
            jnp.stack(pool_s), jnp.stack(conv_s))
```

```python
import numpy as np
import concourse.bass as bass
import concourse.mybir as mybir
from concourse.bass_utils import run_bass_kernel_spmd

F32 = mybir.dt.float32
BF16 = mybir.dt.bfloat16
AF = mybir.ActivationFunctionType
ALU = mybir.AluOpType

_ESZ = {F32: 4, BF16: 2, mybir.dt.int32: 4, mybir.dt.uint8: 1}


def _esz(dt):
    return _ESZ[dt]


def ap_rects(ap):
    t = ap.tensor
    name = t.name
    esz = _esz(ap.dtype)
    a = list(ap.ap)
    off = int(ap.offset)
    space = str(t.space)
    if "DRAM" in space.upper() or "HBM" in space.upper():
        lo = off
        hi = off
        for st, cnt in a:
            if st >= 0:
                hi += st * (cnt - 1)
            else:
                lo += st * (cnt - 1)
        return name, [(0, 1, lo * esz, (hi + 1) * esz)]
    rowlen = 1
    for s in list(t.shape)[1:]:
        rowlen *= int(s)
    if "PSUM" in space.upper():
        p0 = off // rowlen
        pst, pcnt = a[0]
        p1 = p0 + (pcnt if pst != 0 else 1)
        return name, [((p0 // 32) * 32, ((p1 + 31) // 32) * 32, 0, rowlen * esz)]
    p0 = off // rowlen
    c0 = off % rowlen
    pst, pcnt = a[0]
    p1 = p0 + (pcnt if pst != 0 else 1)
    free = a[1:]
    if not free:
        return name, [(p0, p1, c0 * esz, (c0 + 1) * esz)]
    inner_st, inner_cnt = free[-1]
    outer = free[:-1]
    nouter = 1
    for st, cnt in outer:
        nouter *= cnt
    inner_len = (abs(inner_st) * (inner_cnt - 1) + 1)
    if nouter > 64:
        lo = c0
        hi = c0
        for st, cnt in free:
            if st >= 0:
                hi += st * (cnt - 1)
            else:
                lo += st * (cnt - 1)
        return name, [(p0, p1, lo * esz, (hi + 1) * esz)]
    starts = [c0]
    for st, cnt in outer:
        starts = [s + i * st for s in starts for i in range(cnt)]
    rs = sorted((s * esz, (s + inner_len) * esz) for s in starts)
    merged = []
    for lo, hi in rs:
        if merged and lo <= merged[-1][1]:
            merged[-1][1] = max(merged[-1][1], hi)
        else:
            merged.append([lo, hi])
    return name, [(p0, p1, lo, hi) for lo, hi in merged]


def _ov(r, q):
    return r[0] < q[1] and q[0] < r[1] and r[2] < q[3] and q[2] < r[3]


def _covers(big, small):
    return big[0] <= small[0] and big[1] >= small[1] and big[2] <= small[2] and big[3] >= small[3]


class Op:
    __slots__ = ("eng", "idx", "fn", "is_dma", "deps", "needs_signal", "sigval",
                 "sem", "semval", "name")

    def __init__(self, eng, fn, is_dma, name):
        self.eng = eng
        self.fn = fn
        self.is_dma = is_dma
        self.deps = {}
        self.needs_signal = False
        self.sigval = None
        self.sem = None
        self.semval = None
        self.name = name


class Rec:
    __slots__ = ("rect", "writer", "readers")

    def __init__(self, rect, writer):
        self.rect = rect
        self.writer = writer
        self.readers = {}


class Prog:
    ENGS = ["pe", "act", "dve", "pool", "sp"]

    def __init__(self, nc, n_dma_sems=None):
        self.nc = nc
        self.ops = {e: [] for e in self.ENGS}
        self.state = {}
        self.n_dma_sems = n_dma_sems or {"sp": 20, "pool": 10, "act": 6}
        self.dma_rr = {e: 0 for e in self.ENGS}
        self.dma_last = {}

    def add(self, eng, fn, reads=(), writes=(), dma=False, name=""):
        op = Op(eng, fn, dma, name)
        op.idx = len(self.ops[eng])
        self.ops[eng].append(op)
        rkey = ("dma", id(op)) if dma else eng
        for ap in reads:
            tn, rects = ap_rects(ap)
            recs = self.state.setdefault(tn, [])
            for rect in rects:
                for rec in recs:
                    if _ov(rec.rect, rect):
                        if rec.writer is not None and rec.writer is not op:
                            op.deps.setdefault(rec.writer, "raw")
                            if op.deps[rec.writer] != "raw":
                                op.deps[rec.writer] = "raw"
                        rec.readers[rkey] = op
        for ap in writes:
            tn, rects = ap_rects(ap)
            recs = self.state.setdefault(tn, [])
            for rect in rects:
                keep = []
                for rec in recs:
                    if _ov(rec.rect, rect):
                        if rec.writer is not None and rec.writer is not op:
                            op.deps.setdefault(rec.writer, "war")
                        for rd in rec.readers.values():
                            if rd is not op:
                                op.deps.setdefault(rd, "war")
                        if _covers(rect, rec.rect):
                            continue
                    keep.append(rec)
                keep.append(Rec(rect, op))
                recs[:] = keep
        if dma:
            n = self.n_dma_sems[eng]
            slot = self.dma_rr[eng] % n
            self.dma_rr[eng] += 1
            prev = self.dma_last.get((eng, slot))
            op.sem = (eng, slot)
            op.semval = 16 if prev is None else prev.semval + 16
            if prev is not None:
                op.deps.setdefault(prev, "war")
            self.dma_last[(eng, slot)] = op
        return op

    def emit(self, final_wait_eng="sp"):
        nc = self.nc
        plan = {}
        for e in self.ENGS:
            for op in self.ops[e]:
                need = []
                for p, kind in op.deps.items():
                    if p.is_dma:
                        need.append(p)
                    elif p.eng == op.eng:
                        if op.eng in ("act", "dve", "pool"):
                            p.needs_signal = True
                            need.append(p)
                    else:
                        p.needs_signal = True
                        need.append(p)
                plan[op] = need
        for e in self.ENGS:
            c = 0
            for op in self.ops[e]:
                if op.needs_signal and not op.is_dma:
                    c += 1
                    op.sigval = c
        import contextlib
        stack = contextlib.ExitStack()
        esem = {}
        for e in ("pe", "act", "dve", "pool"):
            esem[e] = stack.enter_context(nc.semaphore("s_" + e))
        dsem = {}
        for e, n in self.n_dma_sems.items():
            for s in range(n):
                if (e, s) in self.dma_last:
                    dsem[(e, s)] = stack.enter_context(nc.semaphore("d_%s_%d" % (e, s)))

        def run(e, eng):
            seen = {}
            for op in self.ops[e]:
                waits = {}
                for p in plan[op]:
                    if p.is_dma:
                        k = dsem[p.sem]
                        v = p.semval
                    else:
                        k = esem[p.eng]
                        v = p.sigval
                    kk = id(k)
                    if seen.get(kk, 0) >= v:
                        continue
                    if kk not in waits or waits[kk][1] < v:
                        waits[kk] = (k, v)
                for kk, (k, v) in waits.items():
                    eng.wait_ge(k, v)
                    seen[kk] = v
                ins = op.fn(eng)
                if op.is_dma:
                    ins.then_inc(dsem[op.sem], 16)
                elif op.needs_signal:
                    ins.then_inc(esem[e], 1)
            if e == final_wait_eng:
                for key, last in self.dma_last.items():
                    k = dsem[key]
                    if seen.get(id(k), 0) < last.semval:
                        eng.wait_ge(k, last.semval)

        with nc.Block() as block:
            @block.tensor
            def _(eng):
                run("pe", eng)

            @block.scalar
            def _(eng):
                run("act", eng)

            @block.vector
            def _(eng):
                run("dve", eng)

            @block.gpsimd
            def _(eng):
                run("pool", eng)

            @block.sync
            def _(eng):
                run("sp", eng)
        stack.close()

    def stats(self):
        return {e: len(self.ops[e]) for e in self.ENGS}


NCORES = 8
D = 1024
TP = 2048
NT = 4
TN = 512
EPS = 1e-6
RING = 4
USE_POOL_ADDS = False
XH = 16


def _blocks():
    kv = [("w_k", 0, 0), ("w_k", 0, 1), ("w_v", 0, 0), ("w_v", 0, 1)]
    tile = [("w_in", 0, 0), ("w_in", 0, 2), ("w_in", 0, 3), ("w_in", 0, 1),
            ("w_out", 0, 0), ("w_out", 0, 1), ("w_q", 0, 0), ("w_q", 0, 1),
            ("w_co", 0, 0), ("w_co", 0, 1)]
    tile += [("w_up", 0, c) for c in range(8)]
    tile += [("w_down", q, h) for h in range(2) for q in range(4)]
    tile0 = tile[:6] + kv + tile[6:]
    return tile0, tile


class Seg:
    pass


class _Stop(Exception):
    pass


def ckpt(k):
    return None


def build_program():
    nc = bass.Bass("TRN2", target_bir_lowering=False)
    P = Prog(nc, n_dma_sems={"sp": 24, "pool": 12, "act": 2})

    def din(name, shape, dt=F32):
        return nc.dram_tensor(name, list(shape), dt, kind="ExternalInput")

    def dout(name, shape, dt=F32):
        return nc.dram_tensor(name, list(shape), dt, kind="ExternalOutput")

    xp = din("xp", [TP, D]); xh = din("xh", [XH, D]); xsm = din("xsm", [32, D])
    spool = din("spool", [2, 15, 512]); sconv = din("sconv", [2, 2, 512])
    ck = din("ck", [2, 256, D]); cv = din("cv", [2, 256, D]); mem = din("mem", [256, D])
    invc = din("invc", [128, 64]); pvec = din("pvec", [128, 24]); ident = din("ident", [128, 128])
    gnames = ["g_mix_pre", "g_mix_post", "g_x_pre", "g_x_post", "g_ff_pre", "g_ff_post", "g_mem"]
    gd = {n: din(n, [1, D]) for n in gnames}
    gcol = din("gcol", [128, 32])
    wd = {"w_in": din("w_in", [D, 2048]), "w_out": din("w_out", [D, D]), "w_k": din("w_k", [D, D]),
          "w_v": din("w_v", [D, D]), "w_q": din("w_q", [D, D]), "w_co": din("w_co", [D, D]),
          "w_up": din("w_up", [D, 4096]), "w_down": din("w_down", [4096, D])}
    w_pool = din("w_pool", [4, 128, 128])
    yp = dout("yp", [TP, D]); ysm = dout("ysm", [32, D])
    o_pool_p = dout("pool_p", [16, 512]); o_conv_p = dout("conv_p", [2, 512])
    o_mk = dout("mk", [128, D]); o_mv = dout("mv", [128, D])
    o_pool_s = dout("pool_s", [32, 512]); o_conv_s = dout("conv_s", [4, 512])

    tile0_blocks, tile_blocks = _blocks()
    uniq = list(tile0_blocks)
    wsc = nc.dram_tensor("wsc", [len(uniq), 128, 4096], BF16, kind="Internal")
    bidx = {b: i for i, b in enumerate(uniq)}

    def sb(name, shape, dt=F32):
        return nc.alloc_sbuf_tensor(name, list(shape), dt)

    idf = sb("idf", [128, 128]); idb = sb("idb", [128, 128], BF16)
    ones = sb("ones", [128, 128], BF16); bd64 = sb("bd64", [128, 128], BF16)
    epsb = sb("epsb", [128, 1])
    gpost = {n: sb("bc_" + n, [128, D]) for n in ("g_mix_post", "g_x_post", "g_ff_post")}
    gcs = sb("gcs", [128, 32]); gcb = sb("gcb", [128, 32], BF16)
    pv = sb("pv", [128, 24]); invcs = sb("invcs", [128, 64])
    wpool = sb("wpool", [128, 4 * 128], BF16)
    junk = sb("junk", [128, D], BF16)
    xs_g = sb("xs_g", [128, 2 * D], BF16)
    misc = sb("misc", [128, 512])
    kT_p = sb("kT_p", [128, 8 * 256], BF16); vS_p = sb("vS_p", [128, 2 * D], BF16)
    memX = sb("memX", [128, 2 * D]); mTt = sb("mTt", [128, 8 * 256], BF16)
    ring = [sb("ring%d" % i, [128, 4096], BF16) for i in range(RING)]
    banks = [nc.alloc_psum_tensor("bank%d" % i, [128, 512], F32) for i in range(8)]
    TB = banks[0:4]
    FB = banks[4:7]
    RB = banks[7]
    RBb = RB.bitcast(BF16)
    fb_rr = [0]

    ALLB = [banks[4], banks[5], banks[6], banks[0], banks[1], banks[2], banks[3]]
    fb_wide = [True]

    def fbank():
        pool = ALLB if fb_wide[0] else FB
        b = pool[fb_rr[0] % len(pool)]
        fb_rr[0] += 1
        return b

    def mm(out, lhsT, rhs, start, stop):
        P.add("pe", lambda e: e.matmul(out, lhsT=lhsT, rhs=rhs, start=start, stop=stop),
              reads=[lhsT, rhs], writes=[out])

    def tr(out, in_, idn):
        P.add("pe", lambda e: e.transpose(out=out, in_=in_, identity=idn), reads=[in_, idn], writes=[out])

    def act(out, in_, func, scale=1.0, bias=None, accum=None):
        rd = [in_]
        wr = [out]
        kw = {}
        if bias is not None:
            kw["bias"] = bias
            rd.append(bias)
        if not isinstance(scale, float):
            rd.append(scale)
        if accum is not None:
            kw["accum_out"] = accum
            wr.append(accum)
        P.add("act", lambda e: e.activation(out=out, in_=in_, func=func, scale=scale, **kw), reads=rd, writes=wr)

    def tt(out, in0, in1, op):
        P.add("dve", lambda e: e.tensor_tensor(out=out, in0=in0, in1=in1, op=op), reads=[in0, in1], writes=[out])

    def ptt(out, in0, in1, op):
        P.add("pool", lambda e: e.tensor_tensor(out=out, in0=in0, in1=in1, op=op), reads=[in0, in1], writes=[out])

    def pts(out, in0, s1, op0):
        P.add("pool", lambda e: e.tensor_scalar(out=out, in0=in0, scalar1=s1, scalar2=None, op0=op0),
              reads=[in0, s1], writes=[out])

    def ts(out, in0, s1, op0, s2=None, op1=None):
        rd = [in0] + [s for s in (s1, s2) if s is not None and not isinstance(s, float)]
        if op1 is None:
            P.add("dve", lambda e: e.tensor_scalar(out=out, in0=in0, scalar1=s1, scalar2=None, op0=op0),
                  reads=rd, writes=[out])
        else:
            P.add("dve", lambda e: e.tensor_scalar(out=out, in0=in0, scalar1=s1, scalar2=s2, op0=op0, op1=op1),
                  reads=rd, writes=[out])

    def stt(out, in0, scalar, in1, op0, op1):
        rd = [in0, in1] + ([] if isinstance(scalar, float) else [scalar])
        P.add("dve", lambda e: e.scalar_tensor_tensor(out=out, in0=in0, scalar=scalar, in1=in1, op0=op0, op1=op1),
              reads=rd, writes=[out])

    def vcopy(out, in_):
        P.add("dve", lambda e: e.tensor_copy(out=out, in_=in_), reads=[in_], writes=[out])

    def vrecip(out, in_):
        P.add("dve", lambda e: e.reciprocal(out=out, in_=in_), reads=[in_], writes=[out])

    def vmemset(ap, v):
        P.add("dve", lambda e: e.memset(ap, v), writes=[ap])

    def dma(q, out, in_, rd=True, wr=True):
        P.add(q, lambda e: e.dma_start(out=out, in_=in_), reads=[in_] if rd else [], writes=[out] if wr else [],
              dma=True)

    try:
        dma("sp", idf[:, :], ident[:, :], rd=False)
        vcopy(idb[:, :], idf[:, :])
        vmemset(ones[:, :], 1.0)
        vmemset(bd64[:, :], 0.0)
        vmemset(bd64[0:64, 0:64], 1.0)
        vmemset(bd64[64:128, 64:128], 1.0)
        vmemset(epsb[:, :], EPS)
        def load_gpost():
            for n in gpost:
                dma("sp", gpost[n][:, :], gd[n][0:1, :].broadcast_to([128, D]), rd=False)
        dma("sp", gcs[:, :], gcol[:, :], rd=False)
        vcopy(gcb[:, :], gcs[:, :])
        dma("sp", pv[:, :], pvec[:, :], rd=False)
        dma("sp", invcs[:, :], invc[:, :], rd=False)
        PS = lambda g: pv[:, g:g + 1]
        WC = lambda k, cc: pv[:, 4 + k * 4 + cc: 5 + k * 4 + cc]
        GPO = lambda g: pv[:, 16 + g:17 + g]
        GCO = lambda cc: pv[:, 20 + cc:21 + cc]
        GCOL = {"g_mix_pre": 0, "g_x_pre": 1, "g_ff_pre": 2, "g_mem": 3}

        def cast_wpool():
            P.add("pool", lambda e: e.dma_start(out=wpool[:, :].rearrange("p (g d) -> p g d", g=4),
                                                in_=w_pool.rearrange("g c d -> c g d")),
                  writes=[wpool[:, :]], dma=True)

        ckpt(1)
        wseq = [0]
        wb_pending = {}

        def load_block(blk):
            k = wseq[0]
            slot = ring[k % RING]
            wseq[0] += 1
            if k < len(tile0_blocks):
                wn, r, c = blk
                src = wd[wn][r * 1024:(r + 1) * 1024, c * 512:(c + 1) * 512].rearrange("(kc p) n -> p kc n", p=128)
                dst = slot[:, :].rearrange("p (kc n) -> p kc n", kc=8)
                P.add("pool", lambda e, src=src, dst=dst: e.dma_start(out=dst, in_=src), writes=[slot[:, :]], dma=True)
                if wn not in ("w_k", "w_v"):
                    wb_pending[k] = (lambda blk=blk, slot=slot: dma("sp", wsc[bidx[blk]], slot[:, :]))
            else:
                dma("sp", slot[:, :], wsc[bidx[blk]])
            return slot

        class Stream:
            def __init__(self, blocks):
                self.blocks = list(blocks)
                self.loaded = []
                self.pos = 0
                self.issued = 0

            def prefetch(self, depth):
                while self.issued < len(self.blocks) and self.issued - self.pos < depth:
                    self.loaded.append(load_block(self.blocks[self.issued]))
                    self.issued += 1

            def next(self):
                self.prefetch(RING - 1)
                s = self.loaded[self.pos]
                if self.pos in wb_pending:
                    wb_pending.pop(self.pos)()
                self.pos += 1
                return s

            def done_one(self):
                self.prefetch(RING - 1)

        all_blocks = list(tile0_blocks)
        for t in range(NT - 1):
            all_blocks += tile_blocks
        stream = Stream(all_blocks)

        def W3(slot):
            return slot[:, :].rearrange("p (kc n) -> p kc n", kc=8)

        def make_seg(name, nseq, L, rows, nsub, share=None):
            s = Seg()
            s.name = name; s.nseq = nseq; s.L = L; s.N = nseq * L; s.rows = rows; s.nsub = nsub
            n = s.N
            s.WU = 16 + L; s.WV = 2 + L
            s.xs = xs_g
            s.cur = 0
            if share is None:
                s.X = sb(name + "_X", [128, nsub * D])
                s.Xb = [s.X] + ([sb(name + "_X1", [128, nsub * D])] if name == "pr" else [])
                s.xT = sb(name + "_xT", [128, 8 * n], BF16)
                s.U = sb(name + "_U", [128, 4 * nseq * s.WU])
                s.V = sb(name + "_V", [128, 4 * nseq * s.WV])
                s.st = sb(name + "_st", [128, 16 * nsub])
            else:
                s.X = share.X; s.Xb = share.Xb; s.xT = sb(name + "_xT", [128, 8 * n], BF16)
                s.U = share.U; s.V = share.V; s.st = share.st
            szO = nsub * 4096 if share is None else 0
            offA = 16 * n
            o = offA
            lay = {}
            for nm, sz in (("Cs", 16 * n), ("Z", 16 * n), ("Y", 16 * n), ("Ym", 16 * n),
                           ("sA", 4 * nseq * s.WU), ("sB", 4 * nseq * s.WU), ("pooled", 8 * n), ("sq", 4 * n),
                           ("rstdF", 8 * n)):
                lay[nm] = o
                o += sz
            endB = o
            lay["O"] = offA
            oD = offA + szO
            lay["qT"] = oD; lay["pT"] = oD + 16 * n; lay["rden"] = oD + 24 * n
            endD = oD + 32 * n
            lay["hid"] = oD; lay["rtmp"] = oD + 64 * n
            endE = oD + 68 * n
            tot = max(endB, endD, endE, offA + szO)
            tot = (tot + 63) // 64 * 64
            s.un = sb(name + "_un", [128, tot // 4])
            s.unb = s.un.bitcast(BF16)
            s.lay = lay

            def f32v(nm, off_elems, ncols):
                b = lay[nm] // 4 + off_elems
                return s.un[:, b:b + ncols]

            def bfv(nm, off_elems, ncols):
                b = lay[nm] // 2 + off_elems
                return s.unb[:, b:b + ncols]
            s.f32v = f32v; s.bfv = bfv
            s.merged = lambda kc: s.unb[:, kc * n:(kc + 1) * n]
            return s

        prm = make_seg("pr", 1, TN, 128, 4)
        smp = make_seg("sm", 2, 16, 32, 1)
        hal = make_seg("ha", 1, 16, 16, 1, share=smp)

        def seq3(ap2d, nseq, width, lo, hi):
            if nseq == 1:
                return ap2d[:, lo:hi]
            return ap2d.rearrange("p (b c) -> p b c", b=nseq)[:, :, lo:hi]

        def tok3(ap2d, nseq, L):
            if nseq == 1:
                return ap2d
            return ap2d.rearrange("p (b c) -> p b c", b=nseq)

        def Uv(s, g, lo, hi):
            return seq3(s.U[:, g * s.nseq * s.WU:(g + 1) * s.nseq * s.WU], s.nseq, s.WU, lo, hi)

        def Vv(s, cc, lo, hi):
            return seq3(s.V[:, cc * s.nseq * s.WV:(cc + 1) * s.nseq * s.WV], s.nseq, s.WV, lo, hi)

        def rstd_rows(s, i, ss_ap, r, tc=6, oc=8):
            t = s.st[0:r, i * 16 + tc:i * 16 + tc + 1]
            o = s.st[0:r, i * 16 + oc:i * 16 + oc + 1]
            act(t, ss_ap, AF.Ln, scale=1.0 / D, bias=epsb[0:r, 0:1])
            act(o, t, AF.Exp, scale=-0.5)
            return o

        def Xs(s, i):
            return s.Xb[s.cur][0:s.rows, i * D:(i + 1) * D]

        xs_rr = [0]
        xs_slot = {}

        def pre1(s, i, gname):
            r = s.rows
            X = Xs(s, i)
            ss = s.st[0:r, i * 16:i * 16 + 1]
            act(junk[0:r, :], X, AF.Square, accum=ss)
            rs = rstd_rows(s, i, ss, r, 7, 9)
            k = xs_rr[0] % 2
            xs_rr[0] += 1
            xs_slot[(s.name, i)] = k
            xs = s.xs[0:r, k * D:(k + 1) * D]
            ts(xs, X, rs, ALU.mult)

        def pre2(s, i, gname):
            r = s.rows
            gc = GCOL[gname]
            k = xs_slot[(s.name, i)]
            xs = s.xs[0:r, k * D:(k + 1) * D]
            for c in range(8):
                tr(RBb[:, c * 128:c * 128 + r], xs[:, c * 128:(c + 1) * 128], idb[0:r, 0:r])
            src = RBb[:, :].rearrange("p (c t) -> p c t", c=8)[:, :, 0:r]
            dst = s.xT[:, :].rearrange("p (c t) -> p c t", c=8)[:, :, i * 128:i * 128 + r]
            gb = gcs[:, gc * 8:(gc + 1) * 8].unsqueeze(2).to_broadcast([128, 8, r])
            tt(dst, src, gb, ALU.mult)

        def prenorm(s, gname):
            for i in range(s.nsub):
                pre1(s, i, gname)
                pre2(s, i, gname)

        def xTc(s, kc):
            return s.xT[:, kc * s.N:(kc + 1) * s.N]

        def proj_chunk(s, slot, jj, evac):
            w = W3(slot)
            b = fbank()
            out = b[:, 0:s.N]
            for kc in range(8):
                mm(out, w[:, kc, jj * 128:(jj + 1) * 128], xTc(s, kc), kc == 0, kc == 7)
            evac(out)

        SPL = 384

        def proj_split_begin(s, slot):
            w = W3(slot)
            outs = []
            for jj in range(4):
                b = fbank()
                out = b[:, 0:s.N]
                for kc in range(8):
                    mm(out[:, 0:SPL], w[:, kc, jj * 128:(jj + 1) * 128], xTc(s, kc)[:, 0:SPL], kc == 0, kc == 7)
                outs.append(out)
            return outs

        def proj_split_end(s, slot, outs, evacs):
            w = W3(slot)
            for jj in range(4):
                for kc in range(8):
                    mm(outs[jj][:, SPL:s.N], w[:, kc, jj * 128:(jj + 1) * 128], xTc(s, kc)[:, SPL:s.N],
                       kc == 0, kc == 7)
                evacs[jj](outs[jj])

        def load_x(s, src_rows):
            for i in range(s.nsub):
                dma("sp", Xs(s, i), src_rows(i), rd=False)

        def mix_u(s, slot, hist_to=None):
            for g in range(4):
                if hist_to is None:
                    dst = Uv(s, g, 16, 16 + s.L)
                    proj_chunk(s, slot, g, lambda ps, dst=dst: act(dst, tok3(ps, s.nseq, s.L), AF.Copy))
                else:
                    dst = Uv(hist_to, g, 0, 16)
                    proj_chunk(s, slot, g, lambda ps, dst=dst: act(dst, ps, AF.Copy))

        def mix_C(s, slot):
            for cc in range(4):
                dst = s.f32v("Cs", cc * s.N, s.N)
                proj_chunk(s, slot, cc, lambda ps, dst=dst: act(dst, ps, AF.Copy))

        def mix_h(s, slot, hist_to=None):
            for cc in range(4):
                cs = s.f32v("Cs", cc * s.N, s.N)
                if hist_to is None:
                    dst = Vv(s, cc, 2, 2 + s.L)
                    proj_chunk(s, slot, cc, lambda ps, dst=dst, cs=cs: tt(dst, tok3(ps, s.nseq, s.L),
                                                                         tok3(cs, s.nseq, s.L), ALU.mult))
                else:
                    dst = Vv(hist_to, cc, 0, 2)
                    proj_chunk(s, slot, cc, lambda ps, dst=dst, cs=cs: tt(dst, ps[:, 14:16], cs[:, 14:16], ALU.mult))

        def pool_ops(s, first_tile_fix):
            W_ = s.WU
            L = s.L
            for g, w in enumerate((2, 4, 8, 16)):
                Ug = s.U[:, g * s.nseq * W_:(g + 1) * s.nseq * W_]
                sA = s.f32v("sA", 0, s.nseq * W_)
                sB = s.f32v("sB", 0, s.nseq * W_)
                cur = Ug
                bufs = [sA, sB]
                k = 1
                bi = 0
                while k < w:
                    nxt = bufs[bi]
                    lo = 2 * k - 1 if k > 1 else 1
                    (ptt if USE_POOL_ADDS else tt)(seq3(nxt, s.nseq, W_, lo, W_), seq3(cur, s.nseq, W_, lo, W_),
                       seq3(cur, s.nseq, W_, lo - k, W_ - k), ALU.add)
                    cur = nxt
                    bi ^= 1
                    k *= 2
                pooled = s.bfv("pooled", g * s.N, s.N)
                stt(tok3(pooled, s.nseq, L), seq3(cur, s.nseq, W_, 16, 16 + L), 1.0 / w,
                    seq3(Ug, s.nseq, W_, 16, 16 + L), ALU.mult, ALU.subtract)
                if first_tile_fix:
                    tmp = s.f32v("rstdF", 0, 16)
                    tt(tmp, cur[:, 16:32], invcs[:, g * 16:(g + 1) * 16], ALU.mult)
                    tt(pooled[:, 0:16], tmp, Ug[:, 16:32], ALU.subtract)

        def pool_map(s):
            pend = None
            for g in range(4):
                b = fbank()
                out = b[:, 0:s.N]
                mm(out, wpool[:, g * 128:(g + 1) * 128], s.bfv("pooled", g * s.N, s.N), True, True)
                ym = s.f32v("Ym", g * s.N, s.N)
                sq = s.bfv("sq", (g % 2) * s.N, s.N)
                act(ym, out, AF.Copy, scale=PS(g))
                act(sq, out, AF.Square, scale=PS(g))
                if pend is not None:
                    pend()

                def fin(g=g, ym=ym, sq=sq):
                    b2 = fbank()
                    o2 = b2[:, 0:s.N]
                    mm(o2, ones[:, :], sq, True, True)
                    rf = s.f32v("rstdF", (g % 2) * s.N, s.N)
                    act(rf, o2, AF.Ln, scale=1.0 / 128, bias=epsb[:, 0:1])
                    act(rf, rf, AF.Exp, scale=-0.5)
                    stt(s.merged(g), ym, GPO(g), rf, ALU.mult, ALU.mult)
                pend = fin
            pend()

        pool_ok = [False]

        def conv_ops(s):
            L = s.L
            for cc in range(4):
                z = s.f32v("Z", cc * s.N, s.N)
                z3 = tok3(z, s.nseq, L)
                if pool_ok[0]:
                    pts(z3, Vv(s, cc, 0, L), WC(0, cc), ALU.mult)
                else:
                    act(z3, Vv(s, cc, 0, L), AF.Copy, scale=WC(0, cc))
                stt(z3, Vv(s, cc, 1, 1 + L), WC(1, cc), z3, ALU.mult, ALU.add)
                stt(z3, Vv(s, cc, 2, 2 + L), WC(2, cc), z3, ALU.mult, ALU.add)

        def mix_B(s, slot):
            pend = None
            for cc in range(4):
                z = s.f32v("Z", cc * s.N, s.N)
                y = s.f32v("Y", cc * s.N, s.N)
                sq = s.bfv("sq", (cc % 2) * s.N, s.N)
                rf = s.f32v("rstdF", (cc % 2) * s.N, s.N)

                def ev(ps, z=z, y=y, sq=sq):
                    tt(y, ps, z, ALU.mult)
                    if pool_ok[0]:
                        ptt(sq, y, y, ALU.mult)
                    else:
                        act(sq, y, AF.Square)
                proj_chunk(s, slot, cc, ev)
                if pend is not None:
                    pend()

                def fin(y=y, sq=sq, rf=rf, cc=cc):
                    b2 = fbank()
                    o2 = b2[:, 0:s.N]
                    mm(o2, bd64[:, :], sq, True, True)
                    act(rf, o2, AF.Ln, scale=1.0 / 64, bias=epsb[:, 0:1])
                    act(rf, rf, AF.Exp, scale=-0.5)
                    stt(s.merged(4 + cc), y, GCO(cc), rf, ALU.mult, ALU.mult)
                pend = fin
            pend()

        def evac_tok(s, i, h, ps, gp):
            r = s.rows
            act(junk[0:r, 0:512], ps, AF.Square, accum=s.st[0:r, i * 16 + 1 + h:i * 16 + 2 + h])
            O = s.f32v("O", i * D + h * 512, 512)[0:r, :]
            act(O, ps, AF.Copy)
            tt(O, O, gp[0:r, h * 512:(h + 1) * 512], ALU.mult)

        def post(s, i):
            r = s.rows
            ss = s.st[0:r, i * 16 + 3:i * 16 + 4]
            act(ss, s.st[0:r, i * 16 + 1:i * 16 + 2], AF.Identity, bias=s.st[0:r, i * 16 + 2:i * 16 + 3])
            rs = rstd_rows(s, i, ss, r, 6, 8)
            X = Xs(s, i)
            O = s.f32v("O", i * D, D)[0:r, :]
            stt(X, O, rs, X, ALU.mult, ALU.add)

        tb_rr = [0]

        def tok_out1(segs, src_chunks, gpost_name, next_pre, mid_tail=None):
            gp = gpost[gpost_name]
            slots = [stream.next(), stream.next()]
            pend = None
            pend1 = None
            pend2 = []
            for s in reversed(segs):
                for i in range(s.nsub):
                    r = s.rows
                    for h in range(2):
                        w = W3(slots[h])
                        bk = TB[tb_rr[0] % 4]
                        tb_rr[0] += 1
                        out = bk[0:r, :]
                        for kc in range(8):
                            mm(out, src_chunks(s, kc)[:, i * 128:i * 128 + r], w[:, kc, :], kc == 0, kc == 7)
                        evac_tok(s, i, h, out, gp)
                    post(s, i)
                    if pend1 is not None:
                        pend1()
                    if pend is not None:
                        pend()
                        pend = None
                    pend_next = (lambda s=s, i=i: pre2(s, i, next_pre))

                    def p1(s=s, i=i, pn=pend_next):
                        pre1(s, i, next_pre)
                        pend2.append(pn)
                    pend1 = p1
                    if pend2:
                        pend = pend2.pop(0)
            if pend1 is not None:
                pend1()
            if pend is not None:
                pend()
            if mid_tail is not None:
                mid_tail()
            while pend2:
                pend2.pop(0)()
            stream.done_one()
            stream.done_one()

        def tok_out4(segs, src_chunks, gpost_name, hooks):
            gp = gpost[gpost_name]
            nq = 4
            fb_wide[0] = False
            for h in range(2):
                banks_used = {}
                for q in range(nq):
                    slot = stream.next()
                    w = W3(slot)
                    for s in segs:
                        for i in range(s.nsub):
                            if s is prm:
                                bk = TB[i]
                            else:
                                key = (s.name, i)
                                if key not in banks_used:
                                    banks_used[key] = fbank()
                                bk = banks_used[key]
                            out = bk[0:s.rows, :]
                            for kc in range(8):
                                lhsT = src_chunks(s, q * 8 + kc)[:, i * 128:i * 128 + s.rows]
                                mm(out, lhsT, w[:, kc, :], q == 0 and kc == 0, q == nq - 1 and kc == 7)
                    stream.done_one()
                    if (h, q) in hooks:
                        hooks[(h, q)]()
                for s in segs:
                    for i in range(s.nsub):
                        bk = TB[i] if s is prm else banks_used[(s.name, i)]
                        evac_tok(s, i, h, bk[0:s.rows, :], gp)
            for s in segs:
                for i in range(s.nsub):
                    post(s, i)
            fb_wide[0] = True

        def attention(s, kTs, vSs):
            nb = len(kTs)
            cols = s.N // nb
            steps = [(hh, bi) for hh in range(4) for bi in range(nb)]
            pend = None
            for si, (hh, bi) in enumerate(steps):
                par = si % 2
                kT = kTs[bi]; vS = vSs[bi]
                c0 = bi * cols
                q = [s.bfv("qT", (2 * hh + e2) * s.N + c0, cols) for e2 in range(2)]
                pts = []
                for mc in range(2):
                    b = fbank()
                    out = b[:, 0:cols]
                    for e2 in range(2):
                        j = 2 * hh + e2
                        mm(out, kT[:, j * 256 + mc * 128: j * 256 + (mc + 1) * 128], q[e2], e2 == 0, e2 == 1)
                    pt = s.bfv("pT", (mc * 2 + par) * s.N + c0, cols)
                    act(pt, out, AF.Exp, scale=1.0 / 16.0)
                    pts.append(pt)
                if pend is not None:
                    pend()

                def fin(hh=hh, c0=c0, vS=vS, pts=pts, par=par):
                    b = fbank()
                    den = b[:, 0:cols]
                    for mc in range(2):
                        mm(den, ones[:, :], pts[mc], mc == 0, mc == 1)
                    rd = s.f32v("rden", par * s.N + c0, cols)
                    act(rd, den, AF.Ln)
                    act(rd, rd, AF.Exp, scale=-1.0)
                    for e2 in range(2):
                        j = 2 * hh + e2
                        b = fbank()
                        out = b[:, 0:cols]
                        for mc in range(2):
                            mm(out, vS[:, mc * D + j * 128: mc * D + (j + 1) * 128], pts[mc], mc == 0, mc == 1)
                        tt(s.merged(j)[:, c0:c0 + cols], out, rd, ALU.mult)
                pend = fin
            pend()

        Xb1b = prm.Xb[1].bitcast(BF16)
        kT_s = [Xb1b[:, b * 2048:(b + 1) * 2048] for b in range(2)]
        vS_s = [Xb1b[:, 4096 + b * 2048:4096 + (b + 1) * 2048] for b in range(2)]
        memXb = memX.bitcast(BF16)
        ckbs = [memXb[:, 0:2048], memXb[:, 2048:4096]]

        def cast_block(blk):
            wn, r, c = blk
            src = wd[wn][r * 1024:(r + 1) * 1024, c * 512:(c + 1) * 512].rearrange("(kc p) n -> p kc n", p=128)
            i = bidx[blk]
            dst = wsc[i].rearrange("p (kc n) -> p kc n", kc=8)
            P.add("pool", lambda e, src=src, dst=dst: e.dma_start(out=dst, in_=src), writes=[wsc[i]], dma=True)

        memseg = Seg()
        memseg.rows = 128; memseg.nsub = 2; memseg.N = 256; memseg.name = 'mem'
        memseg.X = memX; memseg.Xb = [memX]; memseg.cur = 0
        memseg.xs = xs_g; memseg.xT = mTt; memseg.st = sb("mst", [128, 32])
        stream.prefetch(RING - 1)
        cast_wpool()
        def sample_cache_casts():
            for b in range(2):
                ckb = ckbs[b]
                P.add("pool", lambda e, b=b, ckb=ckb: e.dma_start(out=ckb.rearrange("p (mc n) -> p mc n", mc=2),
                                                         in_=ck[b].rearrange("(mc p) n -> p mc n", p=128)),
                      writes=[ckb], dma=True)
                P.add("pool", lambda e, b=b: e.dma_start(out=vS_s[b].rearrange("p (mc n) -> p mc n", mc=2),
                                                         in_=cv[b].rearrange("(mc p) n -> p mc n", p=128)),
                      writes=[vS_s[b]], dma=True)
        stT = prm.un[0:32, 6144:6656]; scT = prm.un[0:4, 6656:7168]
        vmemset(stT[:, :], 0.0)
        for b in range(2):
            dma("sp", stT[b * 16 + 1:b * 16 + 16, :], spool[b], rd=False)
        dma("sp", scT[:, :], sconv.rearrange("b r c -> (b r) c"), rd=False)
        for g in range(4):
            tr(RB[:, 0:32], stT[0:32, g * 128:(g + 1) * 128], idf[0:32, 0:32])
            vcopy(Uv(smp, g, 0, 16), RB[:, 0:32].rearrange("p (b c) -> p b c", b=2))
            tr(RB[:, 32:36], scT[0:4, g * 128:(g + 1) * 128], idf[0:4, 0:4])
            vcopy(Vv(smp, g, 0, 2), RB[:, 32:36].rearrange("p (b c) -> p b c", b=2))

        def kv_phase():
            for b in range(2):
                ckb = ckbs[b]
                for mc in range(2):
                    for half in range(2):
                        for jj in range(4):
                            j = half * 4 + jj
                            tr(RBb[:, jj * 128:(jj + 1) * 128], ckb[:, mc * D + j * 128: mc * D + (j + 1) * 128], idb[:, :])
                        dst = kT_s[b].rearrange("p (j m) -> p j m", j=8)[:, half * 4:(half + 1) * 4, mc * 128:(mc + 1) * 128]
                        src = RBb[:, 0:512].rearrange("p (j m) -> p j m", j=4)
                        vcopy(dst, src)
            KO = prm.un[:, 2 * D:3 * D]
            VO = prm.un[:, 3 * D:4 * D]
            mT = lambda kc: memseg.xT[:, kc * 256:(kc + 1) * 256]
            for half in range(2):
                slot = stream.next()
                w = W3(slot)
                for jj in range(4):
                    j = half * 4 + jj
                    b = fbank()
                    out = b[:, 0:256]
                    for kc in range(8):
                        mm(out, w[:, kc, jj * 128:(jj + 1) * 128], mT(kc), kc == 0, kc == 7)
                    act(kT_p[:, j * 256:(j + 1) * 256], out, AF.Copy)
                b = fbank()
                for kc in range(8):
                    mm(b[:, :], mT(kc)[:, 0:128], w[:, kc, :], kc == 0, kc == 7)
                vcopy(KO[:, half * 512:(half + 1) * 512], b[:, :])
                stream.done_one()
            dma("sp", o_mk[:, :], KO, wr=False)
            for half in range(2):
                slot = stream.next()
                w = W3(slot)
                for mc in range(2):
                    b = fbank()
                    for kc in range(8):
                        mm(b[:, :], mT(kc)[:, mc * 128:(mc + 1) * 128], w[:, kc, :], kc == 0, kc == 7)
                    if mc == 0:
                        act(VO[:, half * 512:(half + 1) * 512], b[:, :], AF.Copy)
                        vcopy(vS_p[:, mc * D + half * 512: mc * D + (half + 1) * 512], VO[:, half * 512:(half + 1) * 512])
                    else:
                        vcopy(vS_p[:, mc * D + half * 512: mc * D + (half + 1) * 512], b[:, :])
                stream.done_one()
            dma("sp", o_mv[:, :], VO, wr=False)

        ckpt(3)
        for s in (prm, smp):
            vmemset(s.U[:, :], 0.0) if s is prm else None
        vmemset(prm.V[:, :], 0.0)

        tp_rr = [0]
        def stage_A_loads(t):
            prm.cur = t % 2
            load_x(prm, lambda i, t=t: xp[t * TN + i * 128: t * TN + (i + 1) * 128, :])
            if t == 0:
                load_x(hal, lambda i: xh[:, :])

        for t in range(NT):
            segs = [prm] + ([smp] if t == 0 else [])
            pool_ok[0] = (t > 0)
            prm.cur = t % 2
            if t == 0:
                stage_A_loads(0)
                prenorm(hal, "g_mix_pre")
                load_x(smp, lambda i: xsm[:, :])
                load_gpost()
                prenorm(prm, "g_mix_pre")
                prenorm(smp, "g_mix_pre")
                load_x(memseg, lambda i: mem[i * 128:(i + 1) * 128, :])
            else:
                for g in range(4):
                    vcopy(Uv(prm, g, 0, 16), Uv(prm, g, TN, TN + 16))
                    vcopy(Vv(prm, g, 0, 2), Vv(prm, g, TN, TN + 2))
            ckpt(4)
            slot = stream.next()
            if t == 0:
                mix_u(hal, slot, hist_to=prm)
            for s in segs:
                mix_u(s, slot)
            stream.done_one()
            for s in segs:
                pool_ops(s, first_tile_fix=(t == 0 and s is prm))
            slot = stream.next()
            if t == 0:
                mix_C(hal, slot)
            for s in segs:
                mix_C(s, slot)
            stream.done_one()
            slot = stream.next()
            if t == 0:
                mix_h(hal, slot, hist_to=prm)
            for s in segs:
                mix_h(s, slot)
            stream.done_one()
            for s in segs:
                conv_ops(s)
            for s in segs:
                pool_map(s)
            slot = stream.next()
            for s in segs:
                mix_B(s, slot)
            stream.done_one()
            ckpt(5)
            if t == 0:
                prenorm(memseg, "g_mem")
                sample_cache_casts()
            if t == NT - 1:
                stp = misc[0:16, :]; scp = misc[0:2, :]
                for g in range(4):
                    tr(RB[0:16, g * 128:(g + 1) * 128], Uv(prm, g, TN, TN + 16), idf[:, :])
                vcopy(stp[:, :], RB[0:16, :])
                dma("sp", o_pool_p[:, :], stp[:, :], wr=False)
                for g in range(4):
                    tr(RB[0:2, g * 128:(g + 1) * 128], Vv(prm, g, TN, TN + 2), idf[:, :])
                vcopy(scp[:, :], RB[0:2, :])
                dma("sp", o_conv_p[:, :], scp[:, :], wr=False)
            if t == 0:
                sts = misc[0:32, :]; scs = misc[0:4, :]
                utmp = sb("utmp", [128, 4 * 32]); vtmp = sb("vtmp", [128, 4 * 4])
                for g in range(4):
                    vcopy(utmp[:, g * 32:(g + 1) * 32].rearrange("p (b c) -> p b c", b=2), Uv(smp, g, 16, 32))
                    vcopy(vtmp[:, g * 4:(g + 1) * 4].rearrange("p (b c) -> p b c", b=2), Vv(smp, g, 16, 18))
                for g in range(4):
                    tr(RB[0:32, g * 128:(g + 1) * 128], utmp[:, g * 32:(g + 1) * 32], idf[:, :])
                vcopy(sts[:, :], RB[0:32, :])
                dma("sp", o_pool_s[:, :], sts[:, :], wr=False)
                for g in range(4):
                    tr(RB[0:4, g * 128:(g + 1) * 128], vtmp[:, g * 4:(g + 1) * 4], idf[:, :])
                vcopy(scs[:, :], RB[0:4, :])
                dma("sp", o_conv_s[:, :], scs[:, :], wr=False)
            split = {}

            def mid_first_block():
                split["slot"] = stream.next()
                split["outs"] = proj_split_begin(prm, split["slot"])
            tok_out1(segs, lambda s, kc: s.merged(kc), "g_mix_post", "g_x_pre",
                     mid_tail=(mid_first_block if t > 0 else None))
            ckpt(6)
            if t == 0:
                kv_phase()
            for half in range(2):
                pre = split.pop("slot", None) if half == 0 else None
                slot = pre if pre is not None else stream.next()
                for s in segs:
                    if pre is not None and s is prm:
                        evs = [(lambda ps, dst=s.bfv("qT", (half * 4 + jj) * s.N, s.N): act(dst, ps, AF.Copy))
                               for jj in range(4)]
                        proj_split_end(s, slot, split.pop("outs"), evs)
                        continue
                    for jj in range(4):
                        dst = s.bfv("qT", (half * 4 + jj) * s.N, s.N)
                        proj_chunk(s, slot, jj, lambda ps, dst=dst: act(dst, ps, AF.Copy))
                stream.done_one()
            if smp in segs:
                attention(smp, kT_s, vS_s)
            attention(prm, [kT_p], [vS_p])
            tok_out1(segs, lambda s, kc: s.merged(kc), "g_x_post", "g_ff_pre", mid_tail=mid_first_block)
            ckpt(7)
            if t + 1 < NT:
                stage_A_loads(t + 1)
                prm.cur = t % 2
            for c in range(8):
                pre = split.pop("slot", None) if c == 0 else None
                slot = pre if pre is not None else stream.next()
                for s in segs:
                    evs = []
                    for jj in range(4):
                        ch = c * 4 + jj
                        hid = s.bfv("hid", ch * s.N, s.N)
                        rt = s.bfv("rtmp", (ch % 2) * s.N, s.N)

                        def ev(ps, hid=hid, rt=rt):
                            act(rt, ps, AF.Relu)
                            tt(hid, rt, rt, ALU.mult)
                        evs.append(ev)
                    if pre is not None and s is prm:
                        proj_split_end(s, slot, split.pop("outs"), evs)
                    else:
                        for jj in range(4):
                            proj_chunk(s, slot, jj, evs[jj])
                stream.done_one()
            hooks = {}
            cur_t = t
            if t + 1 < NT:
                nt_ = t + 1

                def mk_hooks(nt_=nt_):
                    def h00():
                        prm.cur = nt_ % 2
                        pre1(prm, 0, "g_mix_pre")
                        prm.cur = cur_t % 2

                    def mid(i):
                        def f():
                            prm.cur = nt_ % 2
                            pre2(prm, i - 1, "g_mix_pre")
                            if i < 4:
                                pre1(prm, i, "g_mix_pre")
                            prm.cur = cur_t % 2
                        return f
                    return {(0, 0): h00, (0, 1): mid(1), (0, 2): mid(2), (0, 3): mid(3), (1, 0): mid(4)}
                hooks = mk_hooks()
            prm.cur = t % 2
            tok_out4(segs, lambda s, kc: s.bfv("hid", kc * s.N, s.N), "g_ff_post", hooks)
            ckpt(8)
            prm.cur = t % 2
            for i in range(4):
                dma("sp", yp[t * TN + i * 128: t * TN + (i + 1) * 128, :], Xs(prm, i), wr=False)
            if smp in segs:
                dma("sp", ysm[:, :], smp.X[0:32, 0:D], wr=False)
    except _Stop:
        pass
    with nc.allow_low_precision("bf16 matmul operands, fp32 accumulation"):
        P.emit()
    return nc, P


_CACHE = {}


def _get_program():
    if "nc" not in _CACHE:
        _CACHE["nc"], _CACHE["P"] = build_program()
    return _CACHE["nc"]


def kernel(x_prompt, x_sample, state_pool, state_conv, cache_mem_k, cache_mem_v, mem_prompt,
           g_mix_pre, w_in, w_pool, pool_scale, w_conv, g_pool_out, g_conv_out, w_out,
           g_mix_post, g_mem, w_k, w_v, g_x_pre, w_q, w_co, g_x_post,
           g_ff_pre, w_up, w_down, g_ff_post):
    f = lambda a: np.ascontiguousarray(np.asarray(a, dtype=np.float32))
    x_prompt = f(x_prompt); x_sample = f(x_sample)
    nc = _get_program()
    col = lambda v: f(v).reshape(-1, 128).T
    pvec = np.concatenate([col(pool_scale[0]), col(w_conv[0, 0]), col(w_conv[0, 1]), col(w_conv[0, 2]),
                           col(g_pool_out[0]), col(g_conv_out[0])], axis=1)
    gcolv = np.concatenate([col(g_mix_pre[0]), col(g_x_pre[0]), col(g_ff_pre[0]), col(g_mem[0])], axis=1)
    shared = {
        "pvec": f(pvec), "gcol": f(gcolv), "ident": np.eye(128, dtype=np.float32),
        "g_mix_pre": f(g_mix_pre), "g_mix_post": f(g_mix_post), "g_x_pre": f(g_x_pre), "g_x_post": f(g_x_post),
        "g_ff_pre": f(g_ff_pre), "g_ff_post": f(g_ff_post), "g_mem": f(g_mem),
        "w_in": f(w_in[0]), "w_out": f(w_out[0]), "w_k": f(w_k[0]).reshape(D, D), "w_v": f(w_v[0]).reshape(D, D),
        "w_q": f(w_q[0]).reshape(D, D), "w_co": f(w_co[0]).reshape(D, D), "w_up": f(w_up[0]),
        "w_down": f(w_down[0]), "w_pool": f(w_pool[0]),
    }
    pos = np.arange(16)
    inv_start = np.stack([1.0 / np.minimum(pos + 1, w) for w in (2, 4, 8, 16)]).astype(np.float32)
    inv_mid = np.stack([np.full(16, 1.0 / w) for w in (2, 4, 8, 16)]).astype(np.float32)
    in_maps = []
    for c in range(NCORES):
        b, half = c // 2, c % 2
        t0 = half * TP
        m = f(mem_prompt[b])
        if half == 1:
            m = np.concatenate([m[128:], m[:128]], axis=0)
        d = dict(shared)
        d["xp"] = f(x_prompt[b, t0:t0 + TP])
        d["xh"] = f(x_prompt[b, t0 - XH:t0]) if half == 1 else np.zeros((XH, D), np.float32)
        d["xsm"] = f(x_sample[2 * c:2 * c + 2]).reshape(32, D)
        d["spool"] = f(state_pool[0, 2 * c:2 * c + 2]); d["sconv"] = f(state_conv[0, 2 * c:2 * c + 2])
        d["ck"] = f(cache_mem_k[0, 2 * c:2 * c + 2]).reshape(2, 256, D)
        d["cv"] = f(cache_mem_v[0, 2 * c:2 * c + 2]).reshape(2, 256, D)
        d["mem"] = f(m)
        tab = inv_start if half == 0 else inv_mid
        d["invc"] = np.ascontiguousarray(np.broadcast_to(tab.reshape(1, 64), (128, 64)))
        in_maps.append(d)
    res = run_bass_kernel_spmd(nc, in_maps, core_ids=list(range(NCORES)))
    R = res.results
    yp = np.zeros((4, 4096, D), np.float32); ys = np.zeros((16, 16, D), np.float32)
    pool_p = np.zeros((1, 4, 15, 512), np.float32); conv_p = np.zeros((1, 4, 2, 512), np.float32)
    mk = np.zeros((1, 4, 256, 4, 256), np.float32); mv = np.zeros((1, 4, 256, 4, 256), np.float32)
    pool_s = np.zeros((1, 16, 15, 512), np.float32); conv_s = np.zeros((1, 16, 2, 512), np.float32)
    for c in range(NCORES):
        b, half = c // 2, c % 2
        r = R[c]
        yp[b, half * TP:(half + 1) * TP] = r["yp"]
        ys[2 * c:2 * c + 2] = r["ysm"].reshape(2, 16, D)
        if half == 1:
            pool_p[0, b] = r["pool_p"][1:16]
            conv_p[0, b] = r["conv_p"]
        mk[0, b, half * 128:(half + 1) * 128] = r["mk"].reshape(128, 4, 256)
        mv[0, b, half * 128:(half + 1) * 128] = r["mv"].reshape(128, 4, 256)
        pool_s[0, 2 * c:2 * c + 2] = r["pool_s"].reshape(2, 16, 512)[:, 1:16]
        conv_s[0, 2 * c:2 * c + 2] = r["conv_s"].reshape(2, 2, 512)
    return yp, ys, pool_p, conv_p, mk, mv, pool_s, conv_s
```

```python
import numpy as np
import concourse.bass as bass
import concourse.mybir as mybir
from concourse.bass_utils import run_bass_kernel_spmd

F32 = mybir.dt.float32
BF16 = mybir.dt.bfloat16
AF = mybir.ActivationFunctionType
ALU = mybir.AluOpType

_ESZ = {F32: 4, BF16: 2, mybir.dt.int32: 4, mybir.dt.uint8: 1}


def _esz(dt):
    return _ESZ[dt]


def ap_rects(ap):
    t = ap.tensor
    name = t.name
    esz = _esz(ap.dtype)
    a = list(ap.ap)
    off = int(ap.offset)
    space = str(t.space)
    if "DRAM" in space.upper() or "HBM" in space.upper():
        lo = off
        hi = off
        for st, cnt in a:
            if st >= 0:
                hi += st * (cnt - 1)
            else:
                lo += st * (cnt - 1)
        return name, [(0, 1, lo * esz, (hi + 1) * esz)]
    rowlen = 1
    for s in list(t.shape)[1:]:
        rowlen *= int(s)
    if "PSUM" in space.upper():
        p0 = off // rowlen
        pst, pcnt = a[0]
        p1 = p0 + (pcnt if pst != 0 else 1)
        return name, [((p0 // 32) * 32, ((p1 + 31) // 32) * 32, 0, rowlen * esz)]
    p0 = off // rowlen
    c0 = off % rowlen
    pst, pcnt = a[0]
    p1 = p0 + (pcnt if pst != 0 else 1)
    free = a[1:]
    if not free:
        return name, [(p0, p1, c0 * esz, (c0 + 1) * esz)]
    inner_st, inner_cnt = free[-1]
    outer = free[:-1]
    nouter = 1
    for st, cnt in outer:
        nouter *= cnt
    inner_len = (abs(inner_st) * (inner_cnt - 1) + 1)
    if nouter > 64:
        lo = c0
        hi = c0
        for st, cnt in free:
            if st >= 0:
                hi += st * (cnt - 1)
            else:
                lo += st * (cnt - 1)
        return name, [(p0, p1, lo * esz, (hi + 1) * esz)]
    starts = [c0]
    for st, cnt in outer:
        starts = [s + i * st for s in starts for i in range(cnt)]
    rs = sorted((s * esz, (s + inner_len) * esz) for s in starts)
    merged = []
    for lo, hi in rs:
        if merged and lo <= merged[-1][1]:
            merged[-1][1] = max(merged[-1][1], hi)
        else:
            merged.append([lo, hi])
    return name, [(p0, p1, lo, hi) for lo, hi in merged]


def _ov(r, q):
    return r[0] < q[1] and q[0] < r[1] and r[2] < q[3] and q[2] < r[3]


def _covers(big, small):
    return big[0] <= small[0] and big[1] >= small[1] and big[2] <= small[2] and big[3] >= small[3]


class Op:
    __slots__ = ("eng", "idx", "fn", "is_dma", "deps", "needs_signal", "sigval",
                 "sem", "semval", "name")

    def __init__(self, eng, fn, is_dma, name):
        self.eng = eng
        self.fn = fn
        self.is_dma = is_dma
        self.deps = {}
        self.needs_signal = False
        self.sigval = None
        self.sem = None
        self.semval = None
        self.name = name


class Rec:
    __slots__ = ("rect", "writer", "readers")

    def __init__(self, rect, writer):
        self.rect = rect
        self.writer = writer
        self.readers = {}


class Prog:
    ENGS = ["pe", "act", "dve", "pool", "sp"]

    def __init__(self, nc, n_dma_sems=None):
        self.nc = nc
        self.ops = {e: [] for e in self.ENGS}
        self.state = {}
        self.n_dma_sems = n_dma_sems or {"sp": 20, "pool": 10, "act": 6}
        self.dma_rr = {e: 0 for e in self.ENGS}
        self.dma_last = {}

    def add(self, eng, fn, reads=(), writes=(), dma=False, name=""):
        op = Op(eng, fn, dma, name)
        op.idx = len(self.ops[eng])
        self.ops[eng].append(op)
        rkey = ("dma", id(op)) if dma else eng
        for ap in reads:
            tn, rects = ap_rects(ap)
            recs = self.state.setdefault(tn, [])
            for rect in rects:
                for rec in recs:
                    if _ov(rec.rect, rect):
                        if rec.writer is not None and rec.writer is not op:
                            op.deps.setdefault(rec.writer, "raw")
                            if op.deps[rec.writer] != "raw":
                                op.deps[rec.writer] = "raw"
                        rec.readers[rkey] = op
        for ap in writes:
            tn, rects = ap_rects(ap)
            recs = self.state.setdefault(tn, [])
            for rect in rects:
                keep = []
                for rec in recs:
                    if _ov(rec.rect, rect):
                        if rec.writer is not None and rec.writer is not op:
                            op.deps.setdefault(rec.writer, "war")
                        for rd in rec.readers.values():
                            if rd is not op:
                                op.deps.setdefault(rd, "war")
                        if _covers(rect, rec.rect):
                            continue
                    keep.append(rec)
                keep.append(Rec(rect, op))
                recs[:] = keep
        if dma:
            n = self.n_dma_sems[eng]
            slot = self.dma_rr[eng] % n
            self.dma_rr[eng] += 1
            prev = self.dma_last.get((eng, slot))
            op.sem = (eng, slot)
            op.semval = 16 if prev is None else prev.semval + 16
            if prev is not None:
                op.deps.setdefault(prev, "war")
            self.dma_last[(eng, slot)] = op
        return op

    def emit(self, final_wait_eng="sp"):
        nc = self.nc
        plan = {}
        for e in self.ENGS:
            for op in self.ops[e]:
                need = []
                for p, kind in op.deps.items():
                    if p.is_dma:
                        need.append(p)
                    elif p.eng == op.eng:
                        if op.eng in ("act", "dve", "pool"):
                            p.needs_signal = True
                            need.append(p)
                    else:
                        p.needs_signal = True
                        need.append(p)
                plan[op] = need
        for e in self.ENGS:
            c = 0
            for op in self.ops[e]:
                if op.needs_signal and not op.is_dma:
                    c += 1
                    op.sigval = c
        import contextlib
        stack = contextlib.ExitStack()
        esem = {}
        for e in ("pe", "act", "dve", "pool"):
            esem[e] = stack.enter_context(nc.semaphore("s_" + e))
        dsem = {}
        for e, n in self.n_dma_sems.items():
            for s in range(n):
                if (e, s) in self.dma_last:
                    dsem[(e, s)] = stack.enter_context(nc.semaphore("d_%s_%d" % (e, s)))

        def run(e, eng):
            seen = {}
            for op in self.ops[e]:
                waits = {}
                for p in plan[op]:
                    if p.is_dma:
                        k = dsem[p.sem]
                        v = p.semval
                    else:
                        k = esem[p.eng]
                        v = p.sigval
                    kk = id(k)
                    if seen.get(kk, 0) >= v:
                        continue
                    if kk not in waits or waits[kk][1] < v:
                        waits[kk] = (k, v)
                for kk, (k, v) in waits.items():
                    eng.wait_ge(k, v)
                    seen[kk] = v
                ins = op.fn(eng)
                if op.is_dma:
                    ins.then_inc(dsem[op.sem], 16)
                elif op.needs_signal:
                    ins.then_inc(esem[e], 1)
            if e == final_wait_eng:
                for key, last in self.dma_last.items():
                    k = dsem[key]
                    if seen.get(id(k), 0) < last.semval:
                        eng.wait_ge(k, last.semval)

        with nc.Block() as block:
            @block.tensor
            def _(eng):
                run("pe", eng)

            @block.scalar
            def _(eng):
                run("act", eng)

            @block.vector
            def _(eng):
                run("dve", eng)

            @block.gpsimd
            def _(eng):
                run("pool", eng)

            @block.sync
            def _(eng):
                run("sp", eng)
        stack.close()

    def stats(self):
        return {e: len(self.ops[e]) for e in self.ENGS}


NCORES = 8
D = 1024
TP = 2048
NT = 4
TN = 512
EPS = 1e-6
RING = 4
USE_POOL_ADDS = False
XH = 16


def _blocks():
    kv = [("w_k", 0, 0), ("w_k", 0, 1), ("w_v", 0, 0), ("w_v", 0, 1)]
    tile = [("w_in", 0, 0), ("w_in", 0, 2), ("w_in", 0, 3), ("w_in", 0, 1),
            ("w_out", 0, 0), ("w_out", 0, 1), ("w_q", 0, 0), ("w_q", 0, 1),
            ("w_co", 0, 0), ("w_co", 0, 1)]
    tile += [("w_up", 0, c) for c in range(8)]
    tile += [("w_down", q, h) for h in range(2) for q in range(4)]
    tile0 = tile[:6] + kv + tile[6:]
    return tile0, tile


class Seg:
    pass


class _Stop(Exception):
    pass


def ckpt(k):
    return None


def build_program():
    nc = bass.Bass("TRN2", target_bir_lowering=False)
    P = Prog(nc, n_dma_sems={"sp": 24, "pool": 12, "act": 2})

    def din(name, shape, dt=F32):
        return nc.dram_tensor(name, list(shape), dt, kind="ExternalInput")

    def dout(name, shape, dt=F32):
        return nc.dram_tensor(name, list(shape), dt, kind="ExternalOutput")

    xp = din("xp", [TP, D]); xh = din("xh", [XH, D]); xsm = din("xsm", [32, D])
    spool = din("spool", [2, 15, 512]); sconv = din("sconv", [2, 2, 512])
    ck = din("ck", [2, 256, D]); cv = din("cv", [2, 256, D]); mem = din("mem", [256, D])
    invc = din("invc", [128, 64]); pvec = din("pvec", [128, 24]); ident = din("ident", [128, 128])
    gnames = ["g_mix_pre", "g_mix_post", "g_x_pre", "g_x_post", "g_ff_pre", "g_ff_post", "g_mem"]
    gd = {n: din(n, [1, D]) for n in gnames}
    gcol = din("gcol", [128, 32])
    wd = {"w_in": din("w_in", [D, 2048]), "w_out": din("w_out", [D, D]), "w_k": din("w_k", [D, D]),
          "w_v": din("w_v", [D, D]), "w_q": din("w_q", [D, D]), "w_co": din("w_co", [D, D]),
          "w_up": din("w_up", [D, 4096]), "w_down": din("w_down", [4096, D])}
    w_pool = din("w_pool", [4, 128, 128])
    yp = dout("yp", [TP, D]); ysm = dout("ysm", [32, D])
    o_pool_p = dout("pool_p", [16, 512]); o_conv_p = dout("conv_p", [2, 512])
    o_mk = dout("mk", [128, D]); o_mv = dout("mv", [128, D])
    o_pool_s = dout("pool_s", [32, 512]); o_conv_s = dout("conv_s", [4, 512])

    tile0_blocks, tile_blocks = _blocks()
    uniq = list(tile0_blocks)
    wsc = nc.dram_tensor("wsc", [len(uniq), 128, 4096], BF16, kind="Internal")
    bidx = {b: i for i, b in enumerate(uniq)}

    def sb(name, shape, dt=F32):
        return nc.alloc_sbuf_tensor(name, list(shape), dt)

    idf = sb("idf", [128, 128]); idb = sb("idb", [128, 128], BF16)
    ones = sb("ones", [128, 128], BF16); bd64 = sb("bd64", [128, 128], BF16)
    epsb = sb("epsb", [128, 1])
    gpost = {n: sb("bc_" + n, [128, D]) for n in ("g_mix_post", "g_x_post", "g_ff_post")}
    gcs = sb("gcs", [128, 32]); gcb = sb("gcb", [128, 32], BF16)
    pv = sb("pv", [128, 24]); invcs = sb("invcs", [128, 64])
    wpool = sb("wpool", [128, 4 * 128], BF16)
    junk = sb("junk", [128, D], BF16)
    xs_g = sb("xs_g", [128, 2 * D], BF16)
    misc = sb("misc", [128, 512])
    kT_p = sb("kT_p", [128, 8 * 256], BF16); vS_p = sb("vS_p", [128, 2 * D], BF16)
    memX = sb("memX", [128, 2 * D]); mTt = sb("mTt", [128, 8 * 256], BF16)
    ring = [sb("ring%d" % i, [128, 4096], BF16) for i in range(RING)]
    banks = [nc.alloc_psum_tensor("bank%d" % i, [128, 512], F32) for i in range(8)]
    TB = banks[0:4]
    FB = banks[4:7]
    RB = banks[7]
    RBb = RB.bitcast(BF16)
    fb_rr = [0]

    ALLB = [banks[4], banks[5], banks[6], banks[0], banks[1], banks[2], banks[3]]
    fb_wide = [True]

    def fbank():
        pool = ALLB if fb_wide[0] else FB
        b = pool[fb_rr[0] % len(pool)]
        fb_rr[0] += 1
        return b

    def mm(out, lhsT, rhs, start, stop):
        P.add("pe", lambda e: e.matmul(out, lhsT=lhsT, rhs=rhs, start=start, stop=stop),
              reads=[lhsT, rhs], writes=[out])

    def tr(out, in_, idn):
        P.add("pe", lambda e: e.transpose(out=out, in_=in_, identity=idn), reads=[in_, idn], writes=[out])

    def act(out, in_, func, scale=1.0, bias=None, accum=None):
        rd = [in_]
        wr = [out]
        kw = {}
        if bias is not None:
            kw["bias"] = bias
            rd.append(bias)
        if not isinstance(scale, float):
            rd.append(scale)
        if accum is not None:
            kw["accum_out"] = accum
            wr.append(accum)
        P.add("act", lambda e: e.activation(out=out, in_=in_, func=func, scale=scale, **kw), reads=rd, writes=wr)

    def tt(out, in0, in1, op):
        P.add("dve", lambda e: e.tensor_tensor(out=out, in0=in0, in1=in1, op=op), reads=[in0, in1], writes=[out])

    def ptt(out, in0, in1, op):
        P.add("pool", lambda e: e.tensor_tensor(out=out, in0=in0, in1=in1, op=op), reads=[in0, in1], writes=[out])

    def ts(out, in0, s1, op0, s2=None, op1=None):
        rd = [in0] + [s for s in (s1, s2) if s is not None and not isinstance(s, float)]
        if op1 is None:
            P.add("dve", lambda e: e.tensor_scalar(out=out, in0=in0, scalar1=s1, scalar2=None, op0=op0),
                  reads=rd, writes=[out])
        else:
            P.add("dve", lambda e: e.tensor_scalar(out=out, in0=in0, scalar1=s1, scalar2=s2, op0=op0, op1=op1),
                  reads=rd, writes=[out])

    def stt(out, in0, scalar, in1, op0, op1):
        rd = [in0, in1] + ([] if isinstance(scalar, float) else [scalar])
        P.add("dve", lambda e: e.scalar_tensor_tensor(out=out, in0=in0, scalar=scalar, in1=in1, op0=op0, op1=op1),
              reads=rd, writes=[out])

    def vcopy(out, in_):
        P.add("dve", lambda e: e.tensor_copy(out=out, in_=in_), reads=[in_], writes=[out])

    def vrecip(out, in_):
        P.add("dve", lambda e: e.reciprocal(out=out, in_=in_), reads=[in_], writes=[out])

    def vmemset(ap, v):
        P.add("dve", lambda e: e.memset(ap, v), writes=[ap])

    def dma(q, out, in_, rd=True, wr=True):
        P.add(q, lambda e: e.dma_start(out=out, in_=in_), reads=[in_] if rd else [], writes=[out] if wr else [],
              dma=True)

    try:
        dma("sp", idf[:, :], ident[:, :], rd=False)
        vcopy(idb[:, :], idf[:, :])
        vmemset(ones[:, :], 1.0)
        vmemset(bd64[:, :], 0.0)
        vmemset(bd64[0:64, 0:64], 1.0)
        vmemset(bd64[64:128, 64:128], 1.0)
        vmemset(epsb[:, :], EPS)
        def load_gpost():
            for n in gpost:
                dma("sp", gpost[n][:, :], gd[n][0:1, :].broadcast_to([128, D]), rd=False)
        dma("sp", gcs[:, :], gcol[:, :], rd=False)
        vcopy(gcb[:, :], gcs[:, :])
        dma("sp", pv[:, :], pvec[:, :], rd=False)
        dma("sp", invcs[:, :], invc[:, :], rd=False)
        PS = lambda g: pv[:, g:g + 1]
        WC = lambda k, cc: pv[:, 4 + k * 4 + cc: 5 + k * 4 + cc]
        GPO = lambda g: pv[:, 16 + g:17 + g]
        GCO = lambda cc: pv[:, 20 + cc:21 + cc]
        GCOL = {"g_mix_pre": 0, "g_x_pre": 1, "g_ff_pre": 2, "g_mem": 3}

        def cast_wpool():
            P.add("pool", lambda e: e.dma_start(out=wpool[:, :].rearrange("p (g d) -> p g d", g=4),
                                                in_=w_pool.rearrange("g c d -> c g d")),
                  writes=[wpool[:, :]], dma=True)

        ckpt(1)
        wseq = [0]
        wb_pending = {}

        def load_block(blk):
            k = wseq[0]
            slot = ring[k % RING]
            wseq[0] += 1
            if k < len(tile0_blocks):
                wn, r, c = blk
                src = wd[wn][r * 1024:(r + 1) * 1024, c * 512:(c + 1) * 512].rearrange("(kc p) n -> p kc n", p=128)
                dst = slot[:, :].rearrange("p (kc n) -> p kc n", kc=8)
                P.add("pool", lambda e, src=src, dst=dst: e.dma_start(out=dst, in_=src), writes=[slot[:, :]], dma=True)
                if wn not in ("w_k", "w_v"):
                    wb_pending[k] = (lambda blk=blk, slot=slot: dma("sp", wsc[bidx[blk]], slot[:, :]))
            else:
                dma("sp", slot[:, :], wsc[bidx[blk]])
            return slot

        class Stream:
            def __init__(self, blocks):
                self.blocks = list(blocks)
                self.loaded = []
                self.pos = 0
                self.issued = 0

            def prefetch(self, depth):
                while self.issued < len(self.blocks) and self.issued - self.pos < depth:
                    self.loaded.append(load_block(self.blocks[self.issued]))
                    self.issued += 1

            def next(self):
                self.prefetch(RING - 1)
                s = self.loaded[self.pos]
                if self.pos in wb_pending:
                    wb_pending.pop(self.pos)()
                self.pos += 1
                return s

            def done_one(self):
                self.prefetch(RING - 1)

        all_blocks = list(tile0_blocks)
        for t in range(NT - 1):
            all_blocks += tile_blocks
        stream = Stream(all_blocks)

        def W3(slot):
            return slot[:, :].rearrange("p (kc n) -> p kc n", kc=8)

        def make_seg(name, nseq, L, rows, nsub, share=None):
            s = Seg()
            s.name = name; s.nseq = nseq; s.L = L; s.N = nseq * L; s.rows = rows; s.nsub = nsub
            n = s.N
            s.WU = 16 + L; s.WV = 2 + L
            s.xs = xs_g
            s.cur = 0
            if share is None:
                s.X = sb(name + "_X", [128, nsub * D])
                s.Xb = [s.X] + ([sb(name + "_X1", [128, nsub * D])] if name == "pr" else [])
                s.xT = sb(name + "_xT", [128, 8 * n], BF16)
                s.U = sb(name + "_U", [128, 4 * nseq * s.WU])
                s.V = sb(name + "_V", [128, 4 * nseq * s.WV])
                s.st = sb(name + "_st", [128, 16 * nsub])
            else:
                s.X = share.X; s.Xb = share.Xb; s.xT = sb(name + "_xT", [128, 8 * n], BF16)
                s.U = share.U; s.V = share.V; s.st = share.st
            szO = nsub * 4096 if share is None else 0
            offA = 16 * n
            o = offA
            lay = {}
            for nm, sz in (("Cs", 16 * n), ("Z", 16 * n), ("Y", 16 * n), ("Ym", 16 * n),
                           ("sA", 4 * nseq * s.WU), ("sB", 4 * nseq * s.WU), ("pooled", 8 * n), ("sq", 4 * n),
                           ("rstdF", 8 * n)):
                lay[nm] = o
                o += sz
            endB = o
            lay["O"] = offA
            oD = offA + szO
            lay["qT"] = oD; lay["pT"] = oD + 16 * n; lay["rden"] = oD + 24 * n
            endD = oD + 32 * n
            lay["hid"] = oD; lay["rtmp"] = oD + 64 * n
            endE = oD + 68 * n
            tot = max(endB, endD, endE, offA + szO)
            tot = (tot + 63) // 64 * 64
            s.un = sb(name + "_un", [128, tot // 4])
            s.unb = s.un.bitcast(BF16)
            s.lay = lay

            def f32v(nm, off_elems, ncols):
                b = lay[nm] // 4 + off_elems
                return s.un[:, b:b + ncols]

            def bfv(nm, off_elems, ncols):
                b = lay[nm] // 2 + off_elems
                return s.unb[:, b:b + ncols]
            s.f32v = f32v; s.bfv = bfv
            s.merged = lambda kc: s.unb[:, kc * n:(kc + 1) * n]
            return s

        prm = make_seg("pr", 1, TN, 128, 4)
        smp = make_seg("sm", 2, 16, 32, 1)
        hal = make_seg("ha", 1, 16, 16, 1, share=smp)

        def seq3(ap2d, nseq, width, lo, hi):
            if nseq == 1:
                return ap2d[:, lo:hi]
            return ap2d.rearrange("p (b c) -> p b c", b=nseq)[:, :, lo:hi]

        def tok3(ap2d, nseq, L):
            if nseq == 1:
                return ap2d
            return ap2d.rearrange("p (b c) -> p b c", b=nseq)

        def Uv(s, g, lo, hi):
            return seq3(s.U[:, g * s.nseq * s.WU:(g + 1) * s.nseq * s.WU], s.nseq, s.WU, lo, hi)

        def Vv(s, cc, lo, hi):
            return seq3(s.V[:, cc * s.nseq * s.WV:(cc + 1) * s.nseq * s.WV], s.nseq, s.WV, lo, hi)

        def rstd_rows(s, i, ss_ap, r, tc=6, oc=8):
            t = s.st[0:r, i * 16 + tc:i * 16 + tc + 1]
            o = s.st[0:r, i * 16 + oc:i * 16 + oc + 1]
            act(t, ss_ap, AF.Ln, scale=1.0 / D, bias=epsb[0:r, 0:1])
            act(o, t, AF.Exp, scale=-0.5)
            return o

        def Xs(s, i):
            return s.Xb[s.cur][0:s.rows, i * D:(i + 1) * D]

        xs_rr = [0]
        xs_slot = {}

        def pre1(s, i, gname):
            r = s.rows
            X = Xs(s, i)
            ss = s.st[0:r, i * 16:i * 16 + 1]
            act(junk[0:r, :], X, AF.Square, accum=ss)
            rs = rstd_rows(s, i, ss, r, 7, 9)
            k = xs_rr[0] % 2
            xs_rr[0] += 1
            xs_slot[(s.name, i)] = k
            xs = s.xs[0:r, k * D:(k + 1) * D]
            ts(xs, X, rs, ALU.mult)

        def pre2(s, i, gname):
            r = s.rows
            gc = GCOL[gname]
            k = xs_slot[(s.name, i)]
            xs = s.xs[0:r, k * D:(k + 1) * D]
            for c in range(8):
                tr(RBb[:, c * 128:c * 128 + r], xs[:, c * 128:(c + 1) * 128], idb[0:r, 0:r])
            src = RBb[:, :].rearrange("p (c t) -> p c t", c=8)[:, :, 0:r]
            dst = s.xT[:, :].rearrange("p (c t) -> p c t", c=8)[:, :, i * 128:i * 128 + r]
            gb = gcs[:, gc * 8:(gc + 1) * 8].unsqueeze(2).to_broadcast([128, 8, r])
            tt(dst, src, gb, ALU.mult)

        def prenorm(s, gname):
            for i in range(s.nsub):
                pre1(s, i, gname)
                pre2(s, i, gname)

        def xTc(s, kc):
            return s.xT[:, kc * s.N:(kc + 1) * s.N]

        def proj_chunk(s, slot, jj, evac):
            w = W3(slot)
            b = fbank()
            out = b[:, 0:s.N]
            for kc in range(8):
                mm(out, w[:, kc, jj * 128:(jj + 1) * 128], xTc(s, kc), kc == 0, kc == 7)
            evac(out)

        SPL = 384

        def proj_split_begin(s, slot):
            w = W3(slot)
            outs = []
            for jj in range(4):
                b = fbank()
                out = b[:, 0:s.N]
                for kc in range(8):
                    mm(out[:, 0:SPL], w[:, kc, jj * 128:(jj + 1) * 128], xTc(s, kc)[:, 0:SPL], kc == 0, kc == 7)
                outs.append(out)
            return outs

        def proj_split_end(s, slot, outs, evacs):
            w = W3(slot)
            for jj in range(4):
                for kc in range(8):
                    mm(outs[jj][:, SPL:s.N], w[:, kc, jj * 128:(jj + 1) * 128], xTc(s, kc)[:, SPL:s.N],
                       kc == 0, kc == 7)
                evacs[jj](outs[jj])

        def load_x(s, src_rows):
            for i in range(s.nsub):
                dma("sp", Xs(s, i), src_rows(i), rd=False)

        def mix_u(s, slot, hist_to=None):
            for g in range(4):
                if hist_to is None:
                    dst = Uv(s, g, 16, 16 + s.L)
                    proj_chunk(s, slot, g, lambda ps, dst=dst: act(dst, tok3(ps, s.nseq, s.L), AF.Copy))
                else:
                    dst = Uv(hist_to, g, 0, 16)
                    proj_chunk(s, slot, g, lambda ps, dst=dst: act(dst, ps, AF.Copy))

        def mix_C(s, slot):
            for cc in range(4):
                dst = s.f32v("Cs", cc * s.N, s.N)
                proj_chunk(s, slot, cc, lambda ps, dst=dst: act(dst, ps, AF.Copy))

        def mix_h(s, slot, hist_to=None):
            for cc in range(4):
                cs = s.f32v("Cs", cc * s.N, s.N)
                if hist_to is None:
                    dst = Vv(s, cc, 2, 2 + s.L)
                    proj_chunk(s, slot, cc, lambda ps, dst=dst, cs=cs: tt(dst, tok3(ps, s.nseq, s.L),
                                                                         tok3(cs, s.nseq, s.L), ALU.mult))
                else:
                    dst = Vv(hist_to, cc, 0, 2)
                    proj_chunk(s, slot, cc, lambda ps, dst=dst, cs=cs: tt(dst, ps[:, 14:16], cs[:, 14:16], ALU.mult))

        def pool_ops(s, first_tile_fix):
            W_ = s.WU
            L = s.L
            for g, w in enumerate((2, 4, 8, 16)):
                Ug = s.U[:, g * s.nseq * W_:(g + 1) * s.nseq * W_]
                sA = s.f32v("sA", 0, s.nseq * W_)
                sB = s.f32v("sB", 0, s.nseq * W_)
                cur = Ug
                bufs = [sA, sB]
                k = 1
                bi = 0
                while k < w:
                    nxt = bufs[bi]
                    lo = 2 * k - 1 if k > 1 else 1
                    (ptt if USE_POOL_ADDS else tt)(seq3(nxt, s.nseq, W_, lo, W_), seq3(cur, s.nseq, W_, lo, W_),
                       seq3(cur, s.nseq, W_, lo - k, W_ - k), ALU.add)
                    cur = nxt
                    bi ^= 1
                    k *= 2
                pooled = s.bfv("pooled", g * s.N, s.N)
                stt(tok3(pooled, s.nseq, L), seq3(cur, s.nseq, W_, 16, 16 + L), 1.0 / w,
                    seq3(Ug, s.nseq, W_, 16, 16 + L), ALU.mult, ALU.subtract)
                if first_tile_fix:
                    tmp = s.f32v("rstdF", 0, 16)
                    tt(tmp, cur[:, 16:32], invcs[:, g * 16:(g + 1) * 16], ALU.mult)
                    tt(pooled[:, 0:16], tmp, Ug[:, 16:32], ALU.subtract)

        def pool_map(s):
            pend = None
            for g in range(4):
                b = fbank()
                out = b[:, 0:s.N]
                mm(out, wpool[:, g * 128:(g + 1) * 128], s.bfv("pooled", g * s.N, s.N), True, True)
                ym = s.f32v("Ym", g * s.N, s.N)
                sq = s.bfv("sq", (g % 2) * s.N, s.N)
                act(ym, out, AF.Copy, scale=PS(g))
                act(sq, out, AF.Square, scale=PS(g))
                if pend is not None:
                    pend()

                def fin(g=g, ym=ym, sq=sq):
                    b2 = fbank()
                    o2 = b2[:, 0:s.N]
                    mm(o2, ones[:, :], sq, True, True)
                    rf = s.f32v("rstdF", (g % 2) * s.N, s.N)
                    act(rf, o2, AF.Ln, scale=1.0 / 128, bias=epsb[:, 0:1])
                    act(rf, rf, AF.Exp, scale=-0.5)
                    stt(s.merged(g), ym, GPO(g), rf, ALU.mult, ALU.mult)
                pend = fin
            pend()

        def conv_ops(s):
            L = s.L
            for cc in range(4):
                z = s.f32v("Z", cc * s.N, s.N)
                z3 = tok3(z, s.nseq, L)
                act(z3, Vv(s, cc, 0, L), AF.Copy, scale=WC(0, cc))
                stt(z3, Vv(s, cc, 1, 1 + L), WC(1, cc), z3, ALU.mult, ALU.add)
                stt(z3, Vv(s, cc, 2, 2 + L), WC(2, cc), z3, ALU.mult, ALU.add)

        def mix_B(s, slot):
            pend = None
            for cc in range(4):
                z = s.f32v("Z", cc * s.N, s.N)
                y = s.f32v("Y", cc * s.N, s.N)
                sq = s.bfv("sq", (cc % 2) * s.N, s.N)
                rf = s.f32v("rstdF", (cc % 2) * s.N, s.N)

                def ev(ps, z=z, y=y, sq=sq):
                    tt(y, ps, z, ALU.mult)
                    act(sq, y, AF.Square)
                proj_chunk(s, slot, cc, ev)
                if pend is not None:
                    pend()

                def fin(y=y, sq=sq, rf=rf, cc=cc):
                    b2 = fbank()
                    o2 = b2[:, 0:s.N]
                    mm(o2, bd64[:, :], sq, True, True)
                    act(rf, o2, AF.Ln, scale=1.0 / 64, bias=epsb[:, 0:1])
                    act(rf, rf, AF.Exp, scale=-0.5)
                    stt(s.merged(4 + cc), y, GCO(cc), rf, ALU.mult, ALU.mult)
                pend = fin
            pend()

        def evac_tok(s, i, h, ps, gp):
            r = s.rows
            act(junk[0:r, 0:512], ps, AF.Square, accum=s.st[0:r, i * 16 + 1 + h:i * 16 + 2 + h])
            O = s.f32v("O", i * D + h * 512, 512)[0:r, :]
            act(O, ps, AF.Copy)
            tt(O, O, gp[0:r, h * 512:(h + 1) * 512], ALU.mult)

        def post(s, i):
            r = s.rows
            ss = s.st[0:r, i * 16 + 3:i * 16 + 4]
            act(ss, s.st[0:r, i * 16 + 1:i * 16 + 2], AF.Identity, bias=s.st[0:r, i * 16 + 2:i * 16 + 3])
            rs = rstd_rows(s, i, ss, r, 6, 8)
            X = Xs(s, i)
            O = s.f32v("O", i * D, D)[0:r, :]
            stt(X, O, rs, X, ALU.mult, ALU.add)

        tb_rr = [0]

        def tok_out1(segs, src_chunks, gpost_name, next_pre, mid_tail=None, early_k=0):
            gp = gpost[gpost_name]
            slots = [stream.next(), stream.next()]
            pend = None
            pend1 = None
            pend2 = []
            order = [(s, i) for s in reversed(segs) for i in range(s.nsub)]
            early = {}
            if early_k:
                for (s, i) in order[:2]:
                    for h in range(2):
                        bk = TB[tb_rr[0] % 4]
                        tb_rr[0] += 1
                        out = bk[0:s.rows, :]
                        early[(s.name, i, h)] = out
                        w = W3(slots[h])
                        for kc in range(early_k):
                            mm(out, src_chunks(s, kc)[:, i * 128:i * 128 + s.rows], w[:, kc, :], kc == 0, False)
            for (s, i) in order:
                if True:
                    r = s.rows
                    for h in range(2):
                        w = W3(slots[h])
                        if (s.name, i, h) in early:
                            out = early[(s.name, i, h)]
                            k0 = early_k
                        else:
                            bk = TB[tb_rr[0] % 4]
                            tb_rr[0] += 1
                            out = bk[0:r, :]
                            k0 = 0
                        for kc in range(k0, 8):
                            mm(out, src_chunks(s, kc)[:, i * 128:i * 128 + r], w[:, kc, :], kc == 0, kc == 7)
                        evac_tok(s, i, h, out, gp)
                    post(s, i)
                    if pend1 is not None:
                        pend1()
                    if pend is not None:
                        pend()
                        pend = None
                    pend_next = (lambda s=s, i=i: pre2(s, i, next_pre))

                    def p1(s=s, i=i, pn=pend_next):
                        pre1(s, i, next_pre)
                        pend2.append(pn)
                    pend1 = p1
                    if pend2:
                        pend = pend2.pop(0)
            if pend1 is not None:
                pend1()
            if pend is not None:
                pend()
            if mid_tail is not None:
                mid_tail()
            while pend2:
                pend2.pop(0)()
            stream.done_one()
            stream.done_one()

        def tok_out4(segs, src_chunks, gpost_name, hooks):
            gp = gpost[gpost_name]
            nq = 4
            fb_wide[0] = False
            for h in range(2):
                banks_used = {}
                for q in range(nq):
                    slot = stream.next()
                    w = W3(slot)
                    for s in segs:
                        for i in range(s.nsub):
                            if s is prm:
                                bk = TB[i]
                            else:
                                key = (s.name, i)
                                if key not in banks_used:
                                    banks_used[key] = fbank()
                                bk = banks_used[key]
                            out = bk[0:s.rows, :]
                            for kc in range(8):
                                lhsT = src_chunks(s, q * 8 + kc)[:, i * 128:i * 128 + s.rows]
                                mm(out, lhsT, w[:, kc, :], q == 0 and kc == 0, q == nq - 1 and kc == 7)
                    stream.done_one()
                    if (h, q) in hooks:
                        hooks[(h, q)]()
                for s in segs:
                    for i in range(s.nsub):
                        bk = TB[i] if s is prm else banks_used[(s.name, i)]
                        evac_tok(s, i, h, bk[0:s.rows, :], gp)
            for s in segs:
                for i in range(s.nsub):
                    post(s, i)
            fb_wide[0] = True

        def attention(s, kTs, vSs):
            nb = len(kTs)
            cols = s.N // nb
            steps = [(hh, bi) for hh in range(4) for bi in range(nb)]
            pend = None
            for si, (hh, bi) in enumerate(steps):
                par = si % 2
                kT = kTs[bi]; vS = vSs[bi]
                c0 = bi * cols
                q = [s.bfv("qT", (2 * hh + e2) * s.N + c0, cols) for e2 in range(2)]
                pts = []
                for mc in range(2):
                    b = fbank()
                    out = b[:, 0:cols]
                    for e2 in range(2):
                        j = 2 * hh + e2
                        mm(out, kT[:, j * 256 + mc * 128: j * 256 + (mc + 1) * 128], q[e2], e2 == 0, e2 == 1)
                    pt = s.bfv("pT", (mc * 2 + par) * s.N + c0, cols)
                    act(pt, out, AF.Exp, scale=1.0 / 16.0)
                    pts.append(pt)
                if pend is not None:
                    pend()

                def fin(hh=hh, c0=c0, vS=vS, pts=pts, par=par):
                    b = fbank()
                    den = b[:, 0:cols]
                    for mc in range(2):
                        mm(den, ones[:, :], pts[mc], mc == 0, mc == 1)
                    rd = s.f32v("rden", par * s.N + c0, cols)
                    act(rd, den, AF.Ln)
                    act(rd, rd, AF.Exp, scale=-1.0)
                    for e2 in range(2):
                        j = 2 * hh + e2
                        b = fbank()
                        out = b[:, 0:cols]
                        for mc in range(2):
                            mm(out, vS[:, mc * D + j * 128: mc * D + (j + 1) * 128], pts[mc], mc == 0, mc == 1)
                        tt(s.merged(j)[:, c0:c0 + cols], out, rd, ALU.mult)
                pend = fin
            pend()

        Xb1b = prm.Xb[1].bitcast(BF16)
        kT_s = [Xb1b[:, b * 2048:(b + 1) * 2048] for b in range(2)]
        vS_s = [Xb1b[:, 4096 + b * 2048:4096 + (b + 1) * 2048] for b in range(2)]
        memXb = memX.bitcast(BF16)
        ckbs = [memXb[:, 0:2048], memXb[:, 2048:4096]]

        def cast_block(blk):
            wn, r, c = blk
            src = wd[wn][r * 1024:(r + 1) * 1024, c * 512:(c + 1) * 512].rearrange("(kc p) n -> p kc n", p=128)
            i = bidx[blk]
            dst = wsc[i].rearrange("p (kc n) -> p kc n", kc=8)
            P.add("pool", lambda e, src=src, dst=dst: e.dma_start(out=dst, in_=src), writes=[wsc[i]], dma=True)

        memseg = Seg()
        memseg.rows = 128; memseg.nsub = 2; memseg.N = 256; memseg.name = 'mem'
        memseg.X = memX; memseg.Xb = [memX]; memseg.cur = 0
        memseg.xs = xs_g; memseg.xT = mTt; memseg.st = sb("mst", [128, 32])
        stream.prefetch(RING - 1)
        cast_wpool()
        def sample_cache_casts():
            for b in range(2):
                ckb = ckbs[b]
                P.add("pool", lambda e, b=b, ckb=ckb: e.dma_start(out=ckb.rearrange("p (mc n) -> p mc n", mc=2),
                                                         in_=ck[b].rearrange("(mc p) n -> p mc n", p=128)),
                      writes=[ckb], dma=True)
                P.add("pool", lambda e, b=b: e.dma_start(out=vS_s[b].rearrange("p (mc n) -> p mc n", mc=2),
                                                         in_=cv[b].rearrange("(mc p) n -> p mc n", p=128)),
                      writes=[vS_s[b]], dma=True)
        stT = prm.un[0:32, 6144:6656]; scT = prm.un[0:4, 6656:7168]
        vmemset(stT[:, :], 0.0)
        for b in range(2):
            dma("sp", stT[b * 16 + 1:b * 16 + 16, :], spool[b], rd=False)
        dma("sp", scT[:, :], sconv.rearrange("b r c -> (b r) c"), rd=False)
        for g in range(4):
            tr(RB[:, 0:32], stT[0:32, g * 128:(g + 1) * 128], idf[0:32, 0:32])
            vcopy(Uv(smp, g, 0, 16), RB[:, 0:32].rearrange("p (b c) -> p b c", b=2))
            tr(RB[:, 32:36], scT[0:4, g * 128:(g + 1) * 128], idf[0:4, 0:4])
            vcopy(Vv(smp, g, 0, 2), RB[:, 32:36].rearrange("p (b c) -> p b c", b=2))

        def kv_phase():
            for b in range(2):
                ckb = ckbs[b]
                for mc in range(2):
                    for half in range(2):
                        for jj in range(4):
                            j = half * 4 + jj
                            tr(RBb[:, jj * 128:(jj + 1) * 128], ckb[:, mc * D + j * 128: mc * D + (j + 1) * 128], idb[:, :])
                        dst = kT_s[b].rearrange("p (j m) -> p j m", j=8)[:, half * 4:(half + 1) * 4, mc * 128:(mc + 1) * 128]
                        src = RBb[:, 0:512].rearrange("p (j m) -> p j m", j=4)
                        vcopy(dst, src)
            KO = prm.un[:, 2 * D:3 * D]
            VO = prm.un[:, 3 * D:4 * D]
            mT = lambda kc: memseg.xT[:, kc * 256:(kc + 1) * 256]
            for half in range(2):
                slot = stream.next()
                w = W3(slot)
                for jj in range(4):
                    j = half * 4 + jj
                    b = fbank()
                    out = b[:, 0:256]
                    for kc in range(8):
                        mm(out, w[:, kc, jj * 128:(jj + 1) * 128], mT(kc), kc == 0, kc == 7)
                    act(kT_p[:, j * 256:(j + 1) * 256], out, AF.Copy)
                b = fbank()
                for kc in range(8):
                    mm(b[:, :], mT(kc)[:, 0:128], w[:, kc, :], kc == 0, kc == 7)
                vcopy(KO[:, half * 512:(half + 1) * 512], b[:, :])
                stream.done_one()
            dma("sp", o_mk[:, :], KO, wr=False)
            for half in range(2):
                slot = stream.next()
                w = W3(slot)
                for mc in range(2):
                    b = fbank()
                    for kc in range(8):
                        mm(b[:, :], mT(kc)[:, mc * 128:(mc + 1) * 128], w[:, kc, :], kc == 0, kc == 7)
                    if mc == 0:
                        act(VO[:, half * 512:(half + 1) * 512], b[:, :], AF.Copy)
                        vcopy(vS_p[:, mc * D + half * 512: mc * D + (half + 1) * 512], VO[:, half * 512:(half + 1) * 512])
                    else:
                        vcopy(vS_p[:, mc * D + half * 512: mc * D + (half + 1) * 512], b[:, :])
                stream.done_one()
            dma("sp", o_mv[:, :], VO, wr=False)

        ckpt(3)
        for s in (prm, smp):
            vmemset(s.U[:, :], 0.0) if s is prm else None
        vmemset(prm.V[:, :], 0.0)

        tp_rr = [0]
        def stage_A_loads(t):
            prm.cur = t % 2
            load_x(prm, lambda i, t=t: xp[t * TN + i * 128: t * TN + (i + 1) * 128, :])
            if t == 0:
                load_x(hal, lambda i: xh[:, :])

        for t in range(NT):
            segs = [prm] + ([smp] if t == 0 else [])
            prm.cur = t % 2
            if t == 0:
                stage_A_loads(0)
                prenorm(hal, "g_mix_pre")
                load_x(smp, lambda i: xsm[:, :])
                load_gpost()
                prenorm(prm, "g_mix_pre")
                prenorm(smp, "g_mix_pre")
                load_x(memseg, lambda i: mem[i * 128:(i + 1) * 128, :])
            else:
                for g in range(4):
                    vcopy(Uv(prm, g, 0, 16), Uv(prm, g, TN, TN + 16))
                    vcopy(Vv(prm, g, 0, 2), Vv(prm, g, TN, TN + 2))
            ckpt(4)
            slot = stream.next()
            if t == 0:
                mix_u(hal, slot, hist_to=prm)
            for s in segs:
                mix_u(s, slot)
            stream.done_one()
            for s in segs:
                pool_ops(s, first_tile_fix=(t == 0 and s is prm))
            slot = stream.next()
            if t == 0:
                mix_C(hal, slot)
            for s in segs:
                mix_C(s, slot)
            stream.done_one()
            slot = stream.next()
            if t == 0:
                mix_h(hal, slot, hist_to=prm)
            for s in segs:
                mix_h(s, slot)
            stream.done_one()
            for s in segs:
                conv_ops(s)
            for s in segs:
                pool_map(s)
            slot = stream.next()
            for s in segs:
                mix_B(s, slot)
            stream.done_one()
            ckpt(5)
            if t == 0:
                prenorm(memseg, "g_mem")
                sample_cache_casts()
            if t == NT - 1:
                stp = misc[0:16, :]; scp = misc[0:2, :]
                for g in range(4):
                    tr(RB[0:16, g * 128:(g + 1) * 128], Uv(prm, g, TN, TN + 16), idf[:, :])
                vcopy(stp[:, :], RB[0:16, :])
                dma("sp", o_pool_p[:, :], stp[:, :], wr=False)
                for g in range(4):
                    tr(RB[0:2, g * 128:(g + 1) * 128], Vv(prm, g, TN, TN + 2), idf[:, :])
                vcopy(scp[:, :], RB[0:2, :])
                dma("sp", o_conv_p[:, :], scp[:, :], wr=False)
            if t == 0:
                sts = misc[0:32, :]; scs = misc[0:4, :]
                utmp = sb("utmp", [128, 4 * 32]); vtmp = sb("vtmp", [128, 4 * 4])
                for g in range(4):
                    vcopy(utmp[:, g * 32:(g + 1) * 32].rearrange("p (b c) -> p b c", b=2), Uv(smp, g, 16, 32))
                    vcopy(vtmp[:, g * 4:(g + 1) * 4].rearrange("p (b c) -> p b c", b=2), Vv(smp, g, 16, 18))
                for g in range(4):
                    tr(RB[0:32, g * 128:(g + 1) * 128], utmp[:, g * 32:(g + 1) * 32], idf[:, :])
                vcopy(sts[:, :], RB[0:32, :])
                dma("sp", o_pool_s[:, :], sts[:, :], wr=False)
                for g in range(4):
                    tr(RB[0:4, g * 128:(g + 1) * 128], vtmp[:, g * 4:(g + 1) * 4], idf[:, :])
                vcopy(scs[:, :], RB[0:4, :])
                dma("sp", o_conv_s[:, :], scs[:, :], wr=False)
            split = {}

            def mid_first_block():
                split["slot"] = stream.next()
                split["outs"] = proj_split_begin(prm, split["slot"])
            tok_out1(segs, lambda s, kc: s.merged(kc), "g_mix_post", "g_x_pre",
                     mid_tail=(mid_first_block if t > 0 else None), early_k=4)
            ckpt(6)
            if t == 0:
                kv_phase()
            for half in range(2):
                pre = split.pop("slot", None) if half == 0 else None
                slot = pre if pre is not None else stream.next()
                for s in segs:
                    if pre is not None and s is prm:
                        evs = [(lambda ps, dst=s.bfv("qT", (half * 4 + jj) * s.N, s.N): act(dst, ps, AF.Copy))
                               for jj in range(4)]
                        proj_split_end(s, slot, split.pop("outs"), evs)
                        continue
                    for jj in range(4):
                        dst = s.bfv("qT", (half * 4 + jj) * s.N, s.N)
                        proj_chunk(s, slot, jj, lambda ps, dst=dst: act(dst, ps, AF.Copy))
                stream.done_one()
            if smp in segs:
                attention(smp, kT_s, vS_s)
            attention(prm, [kT_p], [vS_p])
            tok_out1(segs, lambda s, kc: s.merged(kc), "g_x_post", "g_ff_pre", mid_tail=mid_first_block,
                     early_k=6)
            ckpt(7)
            if t + 1 < NT:
                stage_A_loads(t + 1)
                prm.cur = t % 2
            for c in range(8):
                pre = split.pop("slot", None) if c == 0 else None
                slot = pre if pre is not None else stream.next()
                for s in segs:
                    evs = []
                    for jj in range(4):
                        ch = c * 4 + jj
                        hid = s.bfv("hid", ch * s.N, s.N)
                        rt = s.bfv("rtmp", (ch % 2) * s.N, s.N)

                        def ev(ps, hid=hid, rt=rt):
                            act(rt, ps, AF.Relu)
                            tt(hid, rt, rt, ALU.mult)
                        evs.append(ev)
                    if pre is not None and s is prm:
                        proj_split_end(s, slot, split.pop("outs"), evs)
                    else:
                        for jj in range(4):
                            proj_chunk(s, slot, jj, evs[jj])
                stream.done_one()
            hooks = {}
            cur_t = t
            if t + 1 < NT:
                nt_ = t + 1

                def mk_hooks(nt_=nt_):
                    def h00():
                        prm.cur = nt_ % 2
                        pre1(prm, 0, "g_mix_pre")
                        prm.cur = cur_t % 2

                    def mid(i):
                        def f():
                            prm.cur = nt_ % 2
                            pre2(prm, i - 1, "g_mix_pre")
                            if i < 4:
                                pre1(prm, i, "g_mix_pre")
                            prm.cur = cur_t % 2
                        return f
                    return {(0, 0): h00, (0, 1): mid(1), (0, 2): mid(2), (0, 3): mid(3), (1, 0): mid(4)}
                hooks = mk_hooks()
            prm.cur = t % 2
            tok_out4(segs, lambda s, kc: s.bfv("hid", kc * s.N, s.N), "g_ff_post", hooks)
            ckpt(8)
            prm.cur = t % 2
            for i in range(4):
                dma("sp", yp[t * TN + i * 128: t * TN + (i + 1) * 128, :], Xs(prm, i), wr=False)
            if smp in segs:
                dma("sp", ysm[:, :], smp.X[0:32, 0:D], wr=False)
    except _Stop:
        pass
    with nc.allow_low_precision("bf16 matmul operands, fp32 accumulation"):
        P.emit()
    return nc, P


_CACHE = {}


def _get_program():
    if "nc" not in _CACHE:
        _CACHE["nc"], _CACHE["P"] = build_program()
    return _CACHE["nc"]


def kernel(x_prompt, x_sample, state_pool, state_conv, cache_mem_k, cache_mem_v, mem_prompt,
           g_mix_pre, w_in, w_pool, pool_scale, w_conv, g_pool_out, g_conv_out, w_out,
           g_mix_post, g_mem, w_k, w_v, g_x_pre, w_q, w_co, g_x_post,
           g_ff_pre, w_up, w_down, g_ff_post):
    f = lambda a: np.ascontiguousarray(np.asarray(a, dtype=np.float32))
    x_prompt = f(x_prompt); x_sample = f(x_sample)
    nc = _get_program()
    col = lambda v: f(v).reshape(-1, 128).T
    pvec = np.concatenate([col(pool_scale[0]), col(w_conv[0, 0]), col(w_conv[0, 1]), col(w_conv[0, 2]),
                           col(g_pool_out[0]), col(g_conv_out[0])], axis=1)
    gcolv = np.concatenate([col(g_mix_pre[0]), col(g_x_pre[0]), col(g_ff_pre[0]), col(g_mem[0])], axis=1)
    shared = {
        "pvec": f(pvec), "gcol": f(gcolv), "ident": np.eye(128, dtype=np.float32),
        "g_mix_pre": f(g_mix_pre), "g_mix_post": f(g_mix_post), "g_x_pre": f(g_x_pre), "g_x_post": f(g_x_post),
        "g_ff_pre": f(g_ff_pre), "g_ff_post": f(g_ff_post), "g_mem": f(g_mem),
        "w_in": f(w_in[0]), "w_out": f(w_out[0]), "w_k": f(w_k[0]).reshape(D, D), "w_v": f(w_v[0]).reshape(D, D),
        "w_q": f(w_q[0]).reshape(D, D), "w_co": f(w_co[0]).reshape(D, D), "w_up": f(w_up[0]),
        "w_down": f(w_down[0]), "w_pool": f(w_pool[0]),
    }
    pos = np.arange(16)
    inv_start = np.stack([1.0 / np.minimum(pos + 1, w) for w in (2, 4, 8, 16)]).astype(np.float32)
    inv_mid = np.stack([np.full(16, 1.0 / w) for w in (2, 4, 8, 16)]).astype(np.float32)
    in_maps = []
    for c in range(NCORES):
        b, half = c // 2, c % 2
        t0 = half * TP
        m = f(mem_prompt[b])
        if half == 1:
            m = np.concatenate([m[128:], m[:128]], axis=0)
        d = dict(shared)
        d["xp"] = f(x_prompt[b, t0:t0 + TP])
        d["xh"] = f(x_prompt[b, t0 - XH:t0]) if half == 1 else np.zeros((XH, D), np.float32)
        d["xsm"] = f(x_sample[2 * c:2 * c + 2]).reshape(32, D)
        d["spool"] = f(state_pool[0, 2 * c:2 * c + 2]); d["sconv"] = f(state_conv[0, 2 * c:2 * c + 2])
        d["ck"] = f(cache_mem_k[0, 2 * c:2 * c + 2]).reshape(2, 256, D)
        d["cv"] = f(cache_mem_v[0, 2 * c:2 * c + 2]).reshape(2, 256, D)
        d["mem"] = f(m)
        tab = inv_start if half == 0 else inv_mid
        d["invc"] = np.ascontiguousarray(np.broadcast_to(tab.reshape(1, 64), (128, 64)))
        in_maps.append(d)
    res = run_bass_kernel_spmd(nc, in_maps, core_ids=list(range(NCORES)))
    R = res.results
    yp = np.zeros((4, 4096, D), np.float32); ys = np.zeros((16, 16, D), np.float32)
    pool_p = np.zeros((1, 4, 15, 512), np.float32); conv_p = np.zeros((1, 4, 2, 512), np.float32)
    mk = np.zeros((1, 4, 256, 4, 256), np.float32); mv = np.zeros((1, 4, 256, 4, 256), np.float32)
    pool_s = np.zeros((1, 16, 15, 512), np.float32); conv_s = np.zeros((1, 16, 2, 512), np.float32)
    for c in range(NCORES):
        b, half = c // 2, c % 2
        r = R[c]
        yp[b, half * TP:(half + 1) * TP] = r["yp"]
        ys[2 * c:2 * c + 2] = r["ysm"].reshape(2, 16, D)
        if half == 1:
            pool_p[0, b] = r["pool_p"][1:16]
            conv_p[0, b] = r["conv_p"]
        mk[0, b, half * 128:(half + 1) * 128] = r["mk"].reshape(128, 4, 256)
        mv[0, b, half * 128:(half + 1) * 128] = r["mv"].reshape(128, 4, 256)
        pool_s[0, 2 * c:2 * c + 2] = r["pool_s"].reshape(2, 16, 512)[:, 1:16]
        conv_s[0, 2 * c:2 * c + 2] = r["conv_s"].reshape(2, 2, 512)
    return yp, ys, pool_p, conv_p, mk, mv, pool_s, conv_s
```

```python
import numpy as np
import concourse.bass as bass
import concourse.mybir as mybir
from concourse.bass_utils import run_bass_kernel_spmd

F32 = mybir.dt.float32
BF16 = mybir.dt.bfloat16
AF = mybir.ActivationFunctionType
ALU = mybir.AluOpType

_ESZ = {F32: 4, BF16: 2, mybir.dt.int32: 4, mybir.dt.uint8: 1}


def _esz(dt):
    return _ESZ[dt]


def ap_rects(ap):
    t = ap.tensor
    name = t.name
    esz = _esz(ap.dtype)
    a = list(ap.ap)
    off = int(ap.offset)
    space = str(t.space)
    if "DRAM" in space.upper() or "HBM" in space.upper():
        lo = off
        hi = off
        for st, cnt in a:
            if st >= 0:
                hi += st * (cnt - 1)
            else:
                lo += st * (cnt - 1)
        return name, [(0, 1, lo * esz, (hi + 1) * esz)]
    rowlen = 1
    for s in list(t.shape)[1:]:
        rowlen *= int(s)
    if "PSUM" in space.upper():
        p0 = off // rowlen
        pst, pcnt = a[0]
        p1 = p0 + (pcnt if pst != 0 else 1)
        return name, [((p0 // 32) * 32, ((p1 + 31) // 32) * 32, 0, rowlen * esz)]
    p0 = off // rowlen
    c0 = off % rowlen
    pst, pcnt = a[0]
    p1 = p0 + (pcnt if pst != 0 else 1)
    free = a[1:]
    if not free:
        return name, [(p0, p1, c0 * esz, (c0 + 1) * esz)]
    inner_st, inner_cnt = free[-1]
    outer = free[:-1]
    nouter = 1
    for st, cnt in outer:
        nouter *= cnt
    inner_len = (abs(inner_st) * (inner_cnt - 1) + 1)
    if nouter > 64:
        lo = c0
        hi = c0
        for st, cnt in free:
            if st >= 0:
                hi += st * (cnt - 1)
            else:
                lo += st * (cnt - 1)
        return name, [(p0, p1, lo * esz, (hi + 1) * esz)]
    starts = [c0]
    for st, cnt in outer:
        starts = [s + i * st for s in starts for i in range(cnt)]
    rs = sorted((s * esz, (s + inner_len) * esz) for s in starts)
    merged = []
    for lo, hi in rs:
        if merged and lo <= merged[-1][1]:
            merged[-1][1] = max(merged[-1][1], hi)
        else:
            merged.append([lo, hi])
    return name, [(p0, p1, lo, hi) for lo, hi in merged]


def _ov(r, q):
    return r[0] < q[1] and q[0] < r[1] and r[2] < q[3] and q[2] < r[3]


def _covers(big, small):
    return big[0] <= small[0] and big[1] >= small[1] and big[2] <= small[2] and big[3] >= small[3]


class Op:
    __slots__ = ("eng", "idx", "fn", "is_dma", "deps", "needs_signal", "sigval",
                 "sem", "semval", "name")

    def __init__(self, eng, fn, is_dma, name):
        self.eng = eng
        self.fn = fn
        self.is_dma = is_dma
        self.deps = {}
        self.needs_signal = False
        self.sigval = None
        self.sem = None
        self.semval = None
        self.name = name


class Rec:
    __slots__ = ("rect", "writer", "readers")

    def __init__(self, rect, writer):
        self.rect = rect
        self.writer = writer
        self.readers = {}


class Prog:
    ENGS = ["pe", "act", "dve", "pool", "sp"]

    def __init__(self, nc, n_dma_sems=None):
        self.nc = nc
        self.ops = {e: [] for e in self.ENGS}
        self.state = {}
        self.n_dma_sems = n_dma_sems or {"sp": 20, "pool": 10, "act": 6}
        self.dma_rr = {e: 0 for e in self.ENGS}
        self.dma_last = {}

    def add(self, eng, fn, reads=(), writes=(), dma=False, name=""):
        op = Op(eng, fn, dma, name)
        op.idx = len(self.ops[eng])
        self.ops[eng].append(op)
        rkey = ("dma", id(op)) if dma else eng
        for ap in reads:
            tn, rects = ap_rects(ap)
            recs = self.state.setdefault(tn, [])
            for rect in rects:
                for rec in recs:
                    if _ov(rec.rect, rect):
                        if rec.writer is not None and rec.writer is not op:
                            op.deps.setdefault(rec.writer, "raw")
                            if op.deps[rec.writer] != "raw":
                                op.deps[rec.writer] = "raw"
                        rec.readers[rkey] = op
        for ap in writes:
            tn, rects = ap_rects(ap)
            recs = self.state.setdefault(tn, [])
            for rect in rects:
                keep = []
                for rec in recs:
                    if _ov(rec.rect, rect):
                        if rec.writer is not None and rec.writer is not op:
                            op.deps.setdefault(rec.writer, "war")
                        for rd in rec.readers.values():
                            if rd is not op:
                                op.deps.setdefault(rd, "war")
                        if _covers(rect, rec.rect):
                            continue
                    keep.append(rec)
                keep.append(Rec(rect, op))
                recs[:] = keep
        if dma:
            n = self.n_dma_sems[eng]
            slot = self.dma_rr[eng] % n
            self.dma_rr[eng] += 1
            prev = self.dma_last.get((eng, slot))
            op.sem = (eng, slot)
            op.semval = 16 if prev is None else prev.semval + 16
            if prev is not None:
                op.deps.setdefault(prev, "war")
            self.dma_last[(eng, slot)] = op
        return op

    def emit(self, final_wait_eng="sp"):
        nc = self.nc
        plan = {}
        for e in self.ENGS:
            for op in self.ops[e]:
                need = []
                for p, kind in op.deps.items():
                    if p.is_dma:
                        need.append(p)
                    elif p.eng == op.eng:
                        if op.eng in ("act", "dve", "pool"):
                            p.needs_signal = True
                            need.append(p)
                    else:
                        p.needs_signal = True
                        need.append(p)
                plan[op] = need
        for e in self.ENGS:
            c = 0
            for op in self.ops[e]:
                if op.needs_signal and not op.is_dma:
                    c += 1
                    op.sigval = c
        import contextlib
        stack = contextlib.ExitStack()
        esem = {}
        for e in ("pe", "act", "dve", "pool"):
            esem[e] = stack.enter_context(nc.semaphore("s_" + e))
        dsem = {}
        for e, n in self.n_dma_sems.items():
            for s in range(n):
                if (e, s) in self.dma_last:
                    dsem[(e, s)] = stack.enter_context(nc.semaphore("d_%s_%d" % (e, s)))

        def run(e, eng):
            seen = {}
            for op in self.ops[e]:
                waits = {}
                for p in plan[op]:
                    if p.is_dma:
                        k = dsem[p.sem]
                        v = p.semval
                    else:
                        k = esem[p.eng]
                        v = p.sigval
                    kk = id(k)
                    if seen.get(kk, 0) >= v:
                        continue
                    if kk not in waits or waits[kk][1] < v:
                        waits[kk] = (k, v)
                for kk, (k, v) in waits.items():
                    eng.wait_ge(k, v)
                    seen[kk] = v
                ins = op.fn(eng)
                if op.is_dma:
                    ins.then_inc(dsem[op.sem], 16)
                elif op.needs_signal:
                    ins.then_inc(esem[e], 1)
            if e == final_wait_eng:
                for key, last in self.dma_last.items():
                    k = dsem[key]
                    if seen.get(id(k), 0) < last.semval:
                        eng.wait_ge(k, last.semval)

        with nc.Block() as block:
            @block.tensor
            def _(eng):
                run("pe", eng)

            @block.scalar
            def _(eng):
                run("act", eng)

            @block.vector
            def _(eng):
                run("dve", eng)

            @block.gpsimd
            def _(eng):
                run("pool", eng)

            @block.sync
            def _(eng):
                run("sp", eng)
        stack.close()

    def stats(self):
        return {e: len(self.ops[e]) for e in self.ENGS}


NCORES = 8
D = 1024
TP = 2048
NT = 4
TN = 512
EPS = 1e-6
RING = 4
USE_POOL_ADDS = False
XH = 16


def _blocks():
    kv = [("w_k", 0, 0), ("w_k", 0, 1), ("w_v", 0, 0), ("w_v", 0, 1)]
    tile = [("w_in", 0, 0), ("w_in", 0, 2), ("w_in", 0, 3), ("w_in", 0, 1),
            ("w_out", 0, 0), ("w_out", 0, 1), ("w_q", 0, 0), ("w_q", 0, 1),
            ("w_co", 0, 0), ("w_co", 0, 1)]
    tile += [("w_up", 0, c) for c in range(8)]
    tile += [("w_down", q, h) for h in range(2) for q in range(4)]
    tile0 = tile[:6] + kv + tile[6:]
    return tile0, tile


class Seg:
    pass


class _Stop(Exception):
    pass


def ckpt(k):
    return None


def build_program():
    nc = bass.Bass("TRN2", target_bir_lowering=False)
    P = Prog(nc, n_dma_sems={"sp": 24, "pool": 12, "act": 2})

    def din(name, shape, dt=F32):
        return nc.dram_tensor(name, list(shape), dt, kind="ExternalInput")

    def dout(name, shape, dt=F32):
        return nc.dram_tensor(name, list(shape), dt, kind="ExternalOutput")

    xp = din("xp", [TP, D]); xh = din("xh", [XH, D]); xsm = din("xsm", [32, D])
    spool = din("spool", [2, 15, 512]); sconv = din("sconv", [2, 2, 512])
    ck = din("ck", [2, 256, D]); cv = din("cv", [2, 256, D]); mem = din("mem", [256, D])
    invc = din("invc", [128, 64]); pvec = din("pvec", [128, 24]); ident = din("ident", [128, 128])
    gnames = ["g_mix_pre", "g_mix_post", "g_x_pre", "g_x_post", "g_ff_pre", "g_ff_post", "g_mem"]
    gd = {n: din(n, [1, D]) for n in gnames}
    gcol = din("gcol", [128, 32])
    wd = {"w_in": din("w_in", [D, 2048]), "w_out": din("w_out", [D, D]), "w_k": din("w_k", [D, D]),
          "w_v": din("w_v", [D, D]), "w_q": din("w_q", [D, D]), "w_co": din("w_co", [D, D]),
          "w_up": din("w_up", [D, 4096]), "w_down": din("w_down", [4096, D])}
    w_pool = din("w_pool", [4, 128, 128])
    yp = dout("yp", [TP, D]); ysm = dout("ysm", [32, D])
    o_pool_p = dout("pool_p", [16, 512]); o_conv_p = dout("conv_p", [2, 512])
    o_mk = dout("mk", [128, D]); o_mv = dout("mv", [128, D])
    o_pool_s = dout("pool_s", [32, 512]); o_conv_s = dout("conv_s", [4, 512])

    tile0_blocks, tile_blocks = _blocks()
    uniq = list(tile0_blocks)
    wsc = nc.dram_tensor("wsc", [len(uniq), 128, 4096], BF16, kind="Internal")
    bidx = {b: i for i, b in enumerate(uniq)}

    def sb(name, shape, dt=F32):
        return nc.alloc_sbuf_tensor(name, list(shape), dt)

    idf = sb("idf", [128, 128]); idb = sb("idb", [128, 128], BF16)
    ones = sb("ones", [128, 128], BF16); bd64 = sb("bd64", [128, 128], BF16)
    epsb = sb("epsb", [128, 1])
    gpost = {n: sb("bc_" + n, [128, D]) for n in ("g_mix_post", "g_x_post", "g_ff_post")}
    gcs = sb("gcs", [128, 32]); gcb = sb("gcb", [128, 32], BF16)
    pv = sb("pv", [128, 24]); invcs = sb("invcs", [128, 64])
    wpool = sb("wpool", [128, 4 * 128], BF16)
    junk = sb("junk", [128, D], BF16)
    xs_g = sb("xs_g", [128, 2 * D], BF16)
    misc = sb("misc", [128, 512])
    kT_p = sb("kT_p", [128, 8 * 256], BF16); vS_p = sb("vS_p", [128, 2 * D], BF16)
    memX = sb("memX", [128, 2 * D]); mTt = sb("mTt", [128, 8 * 256], BF16)
    ring = [sb("ring%d" % i, [128, 4096], BF16) for i in range(RING)]
    banks = [nc.alloc_psum_tensor("bank%d" % i, [128, 512], F32) for i in range(8)]
    TB = banks[0:4]
    FB = banks[4:7]
    RB = banks[7]
    RBb = RB.bitcast(BF16)
    fb_rr = [0]

    ALLB = [banks[4], banks[5], banks[6], banks[0], banks[1], banks[2], banks[3]]
    fb_wide = [True]

    def fbank():
        pool = ALLB if fb_wide[0] else FB
        b = pool[fb_rr[0] % len(pool)]
        fb_rr[0] += 1
        return b

    def mm(out, lhsT, rhs, start, stop):
        P.add("pe", lambda e: e.matmul(out, lhsT=lhsT, rhs=rhs, start=start, stop=stop),
              reads=[lhsT, rhs], writes=[out])

    def tr(out, in_, idn):
        P.add("pe", lambda e: e.transpose(out=out, in_=in_, identity=idn), reads=[in_, idn], writes=[out])

    def act(out, in_, func, scale=1.0, bias=None, accum=None):
        rd = [in_]
        wr = [out]
        kw = {}
        if bias is not None:
            kw["bias"] = bias
            rd.append(bias)
        if not isinstance(scale, float):
            rd.append(scale)
        if accum is not None:
            kw["accum_out"] = accum
            wr.append(accum)
        P.add("act", lambda e: e.activation(out=out, in_=in_, func=func, scale=scale, **kw), reads=rd, writes=wr)

    def tt(out, in0, in1, op):
        P.add("dve", lambda e: e.tensor_tensor(out=out, in0=in0, in1=in1, op=op), reads=[in0, in1], writes=[out])

    def ptt(out, in0, in1, op):
        P.add("pool", lambda e: e.tensor_tensor(out=out, in0=in0, in1=in1, op=op), reads=[in0, in1], writes=[out])

    def ts(out, in0, s1, op0, s2=None, op1=None):
        rd = [in0] + [s for s in (s1, s2) if s is not None and not isinstance(s, float)]
        if op1 is None:
            P.add("dve", lambda e: e.tensor_scalar(out=out, in0=in0, scalar1=s1, scalar2=None, op0=op0),
                  reads=rd, writes=[out])
        else:
            P.add("dve", lambda e: e.tensor_scalar(out=out, in0=in0, scalar1=s1, scalar2=s2, op0=op0, op1=op1),
                  reads=rd, writes=[out])

    def stt(out, in0, scalar, in1, op0, op1):
        rd = [in0, in1] + ([] if isinstance(scalar, float) else [scalar])
        P.add("dve", lambda e: e.scalar_tensor_tensor(out=out, in0=in0, scalar=scalar, in1=in1, op0=op0, op1=op1),
              reads=rd, writes=[out])

    def vcopy(out, in_):
        P.add("dve", lambda e: e.tensor_copy(out=out, in_=in_), reads=[in_], writes=[out])

    def vrecip(out, in_):
        P.add("dve", lambda e: e.reciprocal(out=out, in_=in_), reads=[in_], writes=[out])

    def vmemset(ap, v):
        P.add("dve", lambda e: e.memset(ap, v), writes=[ap])

    def dma(q, out, in_, rd=True, wr=True):
        P.add(q, lambda e: e.dma_start(out=out, in_=in_), reads=[in_] if rd else [], writes=[out] if wr else [],
              dma=True)

    try:
        dma("sp", idf[:, :], ident[:, :], rd=False)
        vcopy(idb[:, :], idf[:, :])
        vmemset(ones[:, :], 1.0)
        vmemset(bd64[:, :], 0.0)
        vmemset(bd64[0:64, 0:64], 1.0)
        vmemset(bd64[64:128, 64:128], 1.0)
        vmemset(epsb[:, :], EPS)
        def load_gpost():
            for n in gpost:
                dma("sp", gpost[n][:, :], gd[n][0:1, :].broadcast_to([128, D]), rd=False)
        dma("sp", gcs[:, :], gcol[:, :], rd=False)
        vcopy(gcb[:, :], gcs[:, :])
        dma("sp", pv[:, :], pvec[:, :], rd=False)
        dma("sp", invcs[:, :], invc[:, :], rd=False)
        PS = lambda g: pv[:, g:g + 1]
        WC = lambda k, cc: pv[:, 4 + k * 4 + cc: 5 + k * 4 + cc]
        GPO = lambda g: pv[:, 16 + g:17 + g]
        GCO = lambda cc: pv[:, 20 + cc:21 + cc]
        GCOL = {"g_mix_pre": 0, "g_x_pre": 1, "g_ff_pre": 2, "g_mem": 3}

        def cast_wpool():
            P.add("pool", lambda e: e.dma_start(out=wpool[:, :].rearrange("p (g d) -> p g d", g=4),
                                                in_=w_pool.rearrange("g c d -> c g d")),
                  writes=[wpool[:, :]], dma=True)

        ckpt(1)
        wseq = [0]
        wb_pending = {}

        def load_block(blk):
            k = wseq[0]
            slot = ring[k % RING]
            wseq[0] += 1
            if k < len(tile0_blocks):
                wn, r, c = blk
                src = wd[wn][r * 1024:(r + 1) * 1024, c * 512:(c + 1) * 512].rearrange("(kc p) n -> p kc n", p=128)
                dst = slot[:, :].rearrange("p (kc n) -> p kc n", kc=8)
                P.add("pool", lambda e, src=src, dst=dst: e.dma_start(out=dst, in_=src), writes=[slot[:, :]], dma=True)
                if wn not in ("w_k", "w_v"):
                    wb_pending[k] = (lambda blk=blk, slot=slot: dma("sp", wsc[bidx[blk]], slot[:, :]))
            else:
                dma("sp", slot[:, :], wsc[bidx[blk]])
            return slot

        class Stream:
            def __init__(self, blocks):
                self.blocks = list(blocks)
                self.loaded = []
                self.pos = 0
                self.issued = 0

            def prefetch(self, depth):
                while self.issued < len(self.blocks) and self.issued - self.pos < depth:
                    self.loaded.append(load_block(self.blocks[self.issued]))
                    self.issued += 1

            def next(self):
                self.prefetch(RING - 1)
                s = self.loaded[self.pos]
                if self.pos in wb_pending:
                    wb_pending.pop(self.pos)()
                self.pos += 1
                return s

            def done_one(self):
                self.prefetch(RING - 1)

        all_blocks = list(tile0_blocks)
        for t in range(NT - 1):
            all_blocks += tile_blocks
        stream = Stream(all_blocks)

        def W3(slot):
            return slot[:, :].rearrange("p (kc n) -> p kc n", kc=8)

        def make_seg(name, nseq, L, rows, nsub, share=None):
            s = Seg()
            s.name = name; s.nseq = nseq; s.L = L; s.N = nseq * L; s.rows = rows; s.nsub = nsub
            n = s.N
            s.WU = 16 + L; s.WV = 2 + L
            s.xs = xs_g
            s.cur = 0
            if share is None:
                s.X = sb(name + "_X", [128, nsub * D])
                s.Xb = [s.X] + ([sb(name + "_X1", [128, nsub * D])] if name == "pr" else [])
                s.xT = sb(name + "_xT", [128, 8 * n], BF16)
                s.U = sb(name + "_U", [128, 4 * nseq * s.WU])
                s.V = sb(name + "_V", [128, 4 * nseq * s.WV])
                s.st = sb(name + "_st", [128, 16 * nsub])
            else:
                s.X = share.X; s.Xb = share.Xb; s.xT = sb(name + "_xT", [128, 8 * n], BF16)
                s.U = share.U; s.V = share.V; s.st = share.st
            szO = nsub * 4096 if share is None else 0
            offA = 16 * n
            o = offA
            lay = {}
            for nm, sz in (("Cs", 16 * n), ("Z", 16 * n), ("Y", 16 * n), ("Ym", 16 * n),
                           ("sA", 4 * nseq * s.WU), ("sB", 4 * nseq * s.WU), ("pooled", 8 * n), ("sq", 4 * n),
                           ("rstdF", 8 * n)):
                lay[nm] = o
                o += sz
            endB = o
            lay["O"] = offA
            oD = offA + szO
            lay["qT"] = oD; lay["pT"] = oD + 16 * n; lay["rden"] = oD + 24 * n
            endD = oD + 32 * n
            lay["hid"] = oD; lay["rtmp"] = oD + 64 * n
            endE = oD + 68 * n
            tot = max(endB, endD, endE, offA + szO)
            tot = (tot + 63) // 64 * 64
            s.un = sb(name + "_un", [128, tot // 4])
            s.unb = s.un.bitcast(BF16)
            s.lay = lay

            def f32v(nm, off_elems, ncols):
                b = lay[nm] // 4 + off_elems
                return s.un[:, b:b + ncols]

            def bfv(nm, off_elems, ncols):
                b = lay[nm] // 2 + off_elems
                return s.unb[:, b:b + ncols]
            s.f32v = f32v; s.bfv = bfv
            s.merged = lambda kc: s.unb[:, kc * n:(kc + 1) * n]
            return s

        prm = make_seg("pr", 1, TN, 128, 4)
        smp = make_seg("sm", 2, 16, 32, 1)
        hal = make_seg("ha", 1, 16, 16, 1, share=smp)

        def seq3(ap2d, nseq, width, lo, hi):
            if nseq == 1:
                return ap2d[:, lo:hi]
            return ap2d.rearrange("p (b c) -> p b c", b=nseq)[:, :, lo:hi]

        def tok3(ap2d, nseq, L):
            if nseq == 1:
                return ap2d
            return ap2d.rearrange("p (b c) -> p b c", b=nseq)

        def Uv(s, g, lo, hi):
            return seq3(s.U[:, g * s.nseq * s.WU:(g + 1) * s.nseq * s.WU], s.nseq, s.WU, lo, hi)

        def Vv(s, cc, lo, hi):
            return seq3(s.V[:, cc * s.nseq * s.WV:(cc + 1) * s.nseq * s.WV], s.nseq, s.WV, lo, hi)

        def rstd_rows(s, i, ss_ap, r, tc=6, oc=8):
            t = s.st[0:r, i * 16 + tc:i * 16 + tc + 1]
            o = s.st[0:r, i * 16 + oc:i * 16 + oc + 1]
            act(t, ss_ap, AF.Ln, scale=1.0 / D, bias=epsb[0:r, 0:1])
            act(o, t, AF.Exp, scale=-0.5)
            return o

        def Xs(s, i):
            return s.Xb[s.cur][0:s.rows, i * D:(i + 1) * D]

        xs_rr = [0]
        xs_slot = {}

        def pre1(s, i, gname):
            r = s.rows
            X = Xs(s, i)
            ss = s.st[0:r, i * 16:i * 16 + 1]
            act(junk[0:r, :], X, AF.Square, accum=ss)
            rs = rstd_rows(s, i, ss, r, 7, 9)
            k = xs_rr[0] % 2
            xs_rr[0] += 1
            xs_slot[(s.name, i)] = k
            xs = s.xs[0:r, k * D:(k + 1) * D]
            ts(xs, X, rs, ALU.mult)

        def pre2(s, i, gname):
            r = s.rows
            gc = GCOL[gname]
            k = xs_slot[(s.name, i)]
            xs = s.xs[0:r, k * D:(k + 1) * D]
            for c in range(8):
                tr(RBb[:, c * 128:c * 128 + r], xs[:, c * 128:(c + 1) * 128], idb[0:r, 0:r])
            src = RBb[:, :].rearrange("p (c t) -> p c t", c=8)[:, :, 0:r]
            dst = s.xT[:, :].rearrange("p (c t) -> p c t", c=8)[:, :, i * 128:i * 128 + r]
            gb = gcs[:, gc * 8:(gc + 1) * 8].unsqueeze(2).to_broadcast([128, 8, r])
            tt(dst, src, gb, ALU.mult)

        def prenorm(s, gname):
            for i in range(s.nsub):
                pre1(s, i, gname)
                pre2(s, i, gname)

        def xTc(s, kc):
            return s.xT[:, kc * s.N:(kc + 1) * s.N]

        def proj_chunk(s, slot, jj, evac):
            w = W3(slot)
            b = fbank()
            out = b[:, 0:s.N]
            for kc in range(8):
                mm(out, w[:, kc, jj * 128:(jj + 1) * 128], xTc(s, kc), kc == 0, kc == 7)
            evac(out)

        SPL = 256

        def proj_split_begin(s, slot):
            w = W3(slot)
            outs = []
            for jj in range(4):
                b = fbank()
                out = b[:, 0:s.N]
                for kc in range(8):
                    mm(out[:, 0:SPL], w[:, kc, jj * 128:(jj + 1) * 128], xTc(s, kc)[:, 0:SPL], kc == 0, kc == 7)
                outs.append(out)
            return outs

        def proj_split_end(s, slot, outs, evacs):
            w = W3(slot)
            for jj in range(4):
                for kc in range(8):
                    mm(outs[jj][:, SPL:s.N], w[:, kc, jj * 128:(jj + 1) * 128], xTc(s, kc)[:, SPL:s.N],
                       kc == 0, kc == 7)
                evacs[jj](outs[jj])

        def load_x(s, src_rows):
            for i in range(s.nsub):
                dma("sp", Xs(s, i), src_rows(i), rd=False)

        def mix_u(s, slot, hist_to=None):
            for g in range(4):
                if hist_to is None:
                    dst = Uv(s, g, 16, 16 + s.L)
                    proj_chunk(s, slot, g, lambda ps, dst=dst: act(dst, tok3(ps, s.nseq, s.L), AF.Copy))
                else:
                    dst = Uv(hist_to, g, 0, 16)
                    proj_chunk(s, slot, g, lambda ps, dst=dst: act(dst, ps, AF.Copy))

        def mix_C(s, slot):
            for cc in range(4):
                dst = s.f32v("Cs", cc * s.N, s.N)
                proj_chunk(s, slot, cc, lambda ps, dst=dst: act(dst, ps, AF.Copy))

        def mix_h(s, slot, hist_to=None):
            for cc in range(4):
                cs = s.f32v("Cs", cc * s.N, s.N)
                if hist_to is None:
                    dst = Vv(s, cc, 2, 2 + s.L)
                    proj_chunk(s, slot, cc, lambda ps, dst=dst, cs=cs: tt(dst, tok3(ps, s.nseq, s.L),
                                                                         tok3(cs, s.nseq, s.L), ALU.mult))
                else:
                    dst = Vv(hist_to, cc, 0, 2)
                    proj_chunk(s, slot, cc, lambda ps, dst=dst, cs=cs: tt(dst, ps[:, 14:16], cs[:, 14:16], ALU.mult))

        def pool_ops(s, first_tile_fix):
            W_ = s.WU
            L = s.L
            for g, w in enumerate((2, 4, 8, 16)):
                Ug = s.U[:, g * s.nseq * W_:(g + 1) * s.nseq * W_]
                sA = s.f32v("sA", 0, s.nseq * W_)
                sB = s.f32v("sB", 0, s.nseq * W_)
                cur = Ug
                bufs = [sA, sB]
                k = 1
                bi = 0
                while k < w:
                    nxt = bufs[bi]
                    lo = 2 * k - 1 if k > 1 else 1
                    (ptt if USE_POOL_ADDS else tt)(seq3(nxt, s.nseq, W_, lo, W_), seq3(cur, s.nseq, W_, lo, W_),
                       seq3(cur, s.nseq, W_, lo - k, W_ - k), ALU.add)
                    cur = nxt
                    bi ^= 1
                    k *= 2
                pooled = s.bfv("pooled", g * s.N, s.N)
                stt(tok3(pooled, s.nseq, L), seq3(cur, s.nseq, W_, 16, 16 + L), 1.0 / w,
                    seq3(Ug, s.nseq, W_, 16, 16 + L), ALU.mult, ALU.subtract)
                if first_tile_fix:
                    tmp = s.f32v("rstdF", 0, 16)
                    tt(tmp, cur[:, 16:32], invcs[:, g * 16:(g + 1) * 16], ALU.mult)
                    tt(pooled[:, 0:16], tmp, Ug[:, 16:32], ALU.subtract)

        def pool_map(s):
            pend = None
            for g in range(4):
                b = fbank()
                out = b[:, 0:s.N]
                mm(out, wpool[:, g * 128:(g + 1) * 128], s.bfv("pooled", g * s.N, s.N), True, True)
                ym = s.f32v("Ym", g * s.N, s.N)
                sq = s.bfv("sq", (g % 2) * s.N, s.N)
                act(ym, out, AF.Copy, scale=PS(g))
                act(sq, out, AF.Square, scale=PS(g))
                if pend is not None:
                    pend()

                def fin(g=g, ym=ym, sq=sq):
                    b2 = fbank()
                    o2 = b2[:, 0:s.N]
                    mm(o2, ones[:, :], sq, True, True)
                    rf = s.f32v("rstdF", (g % 2) * s.N, s.N)
                    act(rf, o2, AF.Ln, scale=1.0 / 128, bias=epsb[:, 0:1])
                    act(rf, rf, AF.Exp, scale=-0.5)
                    stt(s.merged(g), ym, GPO(g), rf, ALU.mult, ALU.mult)
                pend = fin
            pend()

        def conv_ops(s):
            L = s.L
            for cc in range(4):
                z = s.f32v("Z", cc * s.N, s.N)
                z3 = tok3(z, s.nseq, L)
                act(z3, Vv(s, cc, 0, L), AF.Copy, scale=WC(0, cc))
                stt(z3, Vv(s, cc, 1, 1 + L), WC(1, cc), z3, ALU.mult, ALU.add)
                stt(z3, Vv(s, cc, 2, 2 + L), WC(2, cc), z3, ALU.mult, ALU.add)

        def mix_B(s, slot):
            pend = None
            for cc in range(4):
                z = s.f32v("Z", cc * s.N, s.N)
                y = s.f32v("Y", cc * s.N, s.N)
                sq = s.bfv("sq", (cc % 2) * s.N, s.N)
                rf = s.f32v("rstdF", (cc % 2) * s.N, s.N)

                def ev(ps, z=z, y=y, sq=sq):
                    tt(y, ps, z, ALU.mult)
                    act(sq, y, AF.Square)
                proj_chunk(s, slot, cc, ev)
                if pend is not None:
                    pend()

                def fin(y=y, sq=sq, rf=rf, cc=cc):
                    b2 = fbank()
                    o2 = b2[:, 0:s.N]
                    mm(o2, bd64[:, :], sq, True, True)
                    act(rf, o2, AF.Ln, scale=1.0 / 64, bias=epsb[:, 0:1])
                    act(rf, rf, AF.Exp, scale=-0.5)
                    stt(s.merged(4 + cc), y, GCO(cc), rf, ALU.mult, ALU.mult)
                pend = fin
            pend()

        def evac_tok(s, i, h, ps, gp):
            r = s.rows
            act(junk[0:r, 0:512], ps, AF.Square, accum=s.st[0:r, i * 16 + 1 + h:i * 16 + 2 + h])
            O = s.f32v("O", i * D + h * 512, 512)[0:r, :]
            act(O, ps, AF.Copy)
            tt(O, O, gp[0:r, h * 512:(h + 1) * 512], ALU.mult)

        def post(s, i):
            r = s.rows
            ss = s.st[0:r, i * 16 + 3:i * 16 + 4]
            act(ss, s.st[0:r, i * 16 + 1:i * 16 + 2], AF.Identity, bias=s.st[0:r, i * 16 + 2:i * 16 + 3])
            rs = rstd_rows(s, i, ss, r, 6, 8)
            X = Xs(s, i)
            O = s.f32v("O", i * D, D)[0:r, :]
            stt(X, O, rs, X, ALU.mult, ALU.add)

        tb_rr = [0]

        def tok_out1(segs, src_chunks, gpost_name, next_pre, mid_tail=None):
            gp = gpost[gpost_name]
            slots = [stream.next(), stream.next()]
            pend = None
            pend1 = None
            pend2 = []
            for s in reversed(segs):
                for i in range(s.nsub):
                    r = s.rows
                    for h in range(2):
                        w = W3(slots[h])
                        bk = TB[tb_rr[0] % 4]
                        tb_rr[0] += 1
                        out = bk[0:r, :]
                        for kc in range(8):
                            mm(out, src_chunks(s, kc)[:, i * 128:i * 128 + r], w[:, kc, :], kc == 0, kc == 7)
                        evac_tok(s, i, h, out, gp)
                    post(s, i)
                    if pend1 is not None:
                        pend1()
                    if pend is not None:
                        pend()
                        pend = None
                    pend_next = (lambda s=s, i=i: pre2(s, i, next_pre))

                    def p1(s=s, i=i, pn=pend_next):
                        pre1(s, i, next_pre)
                        pend2.append(pn)
                    pend1 = p1
                    if pend2:
                        pend = pend2.pop(0)
            if pend1 is not None:
                pend1()
            if mid_tail is not None:
                mid_tail()
            if pend is not None:
                pend()
            while pend2:
                pend2.pop(0)()
            stream.done_one()
            stream.done_one()

        def tok_out4(segs, src_chunks, gpost_name, hooks):
            gp = gpost[gpost_name]
            nq = 4
            fb_wide[0] = False
            for h in range(2):
                banks_used = {}
                for q in range(nq):
                    slot = stream.next()
                    w = W3(slot)
                    for s in segs:
                        for i in range(s.nsub):
                            if s is prm:
                                bk = TB[i]
                            else:
                                key = (s.name, i)
                                if key not in banks_used:
                                    banks_used[key] = fbank()
                                bk = banks_used[key]
                            out = bk[0:s.rows, :]
                            for kc in range(8):
                                lhsT = src_chunks(s, q * 8 + kc)[:, i * 128:i * 128 + s.rows]
                                mm(out, lhsT, w[:, kc, :], q == 0 and kc == 0, q == nq - 1 and kc == 7)
                    stream.done_one()
                    if (h, q) in hooks:
                        hooks[(h, q)]()
                for s in segs:
                    for i in range(s.nsub):
                        bk = TB[i] if s is prm else banks_used[(s.name, i)]
                        evac_tok(s, i, h, bk[0:s.rows, :], gp)
            for s in segs:
                for i in range(s.nsub):
                    post(s, i)
            fb_wide[0] = True

        def attention(s, kTs, vSs):
            nb = len(kTs)
            cols = s.N // nb
            steps = [(hh, bi) for hh in range(4) for bi in range(nb)]
            pend = None
            for si, (hh, bi) in enumerate(steps):
                par = si % 2
                kT = kTs[bi]; vS = vSs[bi]
                c0 = bi * cols
                q = [s.bfv("qT", (2 * hh + e2) * s.N + c0, cols) for e2 in range(2)]
                pts = []
                for mc in range(2):
                    b = fbank()
                    out = b[:, 0:cols]
                    for e2 in range(2):
                        j = 2 * hh + e2
                        mm(out, kT[:, j * 256 + mc * 128: j * 256 + (mc + 1) * 128], q[e2], e2 == 0, e2 == 1)
                    pt = s.bfv("pT", (mc * 2 + par) * s.N + c0, cols)
                    act(pt, out, AF.Exp, scale=1.0 / 16.0)
                    pts.append(pt)
                if pend is not None:
                    pend()

                def fin(hh=hh, c0=c0, vS=vS, pts=pts, par=par):
                    b = fbank()
                    den = b[:, 0:cols]
                    for mc in range(2):
                        mm(den, ones[:, :], pts[mc], mc == 0, mc == 1)
                    rd = s.f32v("rden", par * s.N + c0, cols)
                    act(rd, den, AF.Ln)
                    act(rd, rd, AF.Exp, scale=-1.0)
                    for e2 in range(2):
                        j = 2 * hh + e2
                        b = fbank()
                        out = b[:, 0:cols]
                        for mc in range(2):
                            mm(out, vS[:, mc * D + j * 128: mc * D + (j + 1) * 128], pts[mc], mc == 0, mc == 1)
                        tt(s.merged(j)[:, c0:c0 + cols], out, rd, ALU.mult)
                pend = fin
            pend()

        Xb1b = prm.Xb[1].bitcast(BF16)
        kT_s = [Xb1b[:, b * 2048:(b + 1) * 2048] for b in range(2)]
        vS_s = [Xb1b[:, 4096 + b * 2048:4096 + (b + 1) * 2048] for b in range(2)]
        memXb = memX.bitcast(BF16)
        ckbs = [memXb[:, 0:2048], memXb[:, 2048:4096]]

        def cast_block(blk):
            wn, r, c = blk
            src = wd[wn][r * 1024:(r + 1) * 1024, c * 512:(c + 1) * 512].rearrange("(kc p) n -> p kc n", p=128)
            i = bidx[blk]
            dst = wsc[i].rearrange("p (kc n) -> p kc n", kc=8)
            P.add("pool", lambda e, src=src, dst=dst: e.dma_start(out=dst, in_=src), writes=[wsc[i]], dma=True)

        memseg = Seg()
        memseg.rows = 128; memseg.nsub = 2; memseg.N = 256; memseg.name = 'mem'
        memseg.X = memX; memseg.Xb = [memX]; memseg.cur = 0
        memseg.xs = xs_g; memseg.xT = mTt; memseg.st = sb("mst", [128, 32])
        stream.prefetch(RING - 1)
        cast_wpool()
        def sample_cache_casts():
            for b in range(2):
                ckb = ckbs[b]
                P.add("pool", lambda e, b=b, ckb=ckb: e.dma_start(out=ckb.rearrange("p (mc n) -> p mc n", mc=2),
                                                         in_=ck[b].rearrange("(mc p) n -> p mc n", p=128)),
                      writes=[ckb], dma=True)
                P.add("pool", lambda e, b=b: e.dma_start(out=vS_s[b].rearrange("p (mc n) -> p mc n", mc=2),
                                                         in_=cv[b].rearrange("(mc p) n -> p mc n", p=128)),
                      writes=[vS_s[b]], dma=True)
        stT = prm.un[0:32, 6144:6656]; scT = prm.un[0:4, 6656:7168]
        vmemset(stT[:, :], 0.0)
        for b in range(2):
            dma("sp", stT[b * 16 + 1:b * 16 + 16, :], spool[b], rd=False)
        dma("sp", scT[:, :], sconv.rearrange("b r c -> (b r) c"), rd=False)
        for g in range(4):
            tr(RB[:, 0:32], stT[0:32, g * 128:(g + 1) * 128], idf[0:32, 0:32])
            vcopy(Uv(smp, g, 0, 16), RB[:, 0:32].rearrange("p (b c) -> p b c", b=2))
            tr(RB[:, 32:36], scT[0:4, g * 128:(g + 1) * 128], idf[0:4, 0:4])
            vcopy(Vv(smp, g, 0, 2), RB[:, 32:36].rearrange("p (b c) -> p b c", b=2))

        def kv_phase():
            for b in range(2):
                ckb = ckbs[b]
                for mc in range(2):
                    for half in range(2):
                        for jj in range(4):
                            j = half * 4 + jj
                            tr(RBb[:, jj * 128:(jj + 1) * 128], ckb[:, mc * D + j * 128: mc * D + (j + 1) * 128], idb[:, :])
                        dst = kT_s[b].rearrange("p (j m) -> p j m", j=8)[:, half * 4:(half + 1) * 4, mc * 128:(mc + 1) * 128]
                        src = RBb[:, 0:512].rearrange("p (j m) -> p j m", j=4)
                        vcopy(dst, src)
            KO = prm.un[:, 2 * D:3 * D]
            VO = prm.un[:, 3 * D:4 * D]
            mT = lambda kc: memseg.xT[:, kc * 256:(kc + 1) * 256]
            for half in range(2):
                slot = stream.next()
                w = W3(slot)
                for jj in range(4):
                    j = half * 4 + jj
                    b = fbank()
                    out = b[:, 0:256]
                    for kc in range(8):
                        mm(out, w[:, kc, jj * 128:(jj + 1) * 128], mT(kc), kc == 0, kc == 7)
                    act(kT_p[:, j * 256:(j + 1) * 256], out, AF.Copy)
                b = fbank()
                for kc in range(8):
                    mm(b[:, :], mT(kc)[:, 0:128], w[:, kc, :], kc == 0, kc == 7)
                vcopy(KO[:, half * 512:(half + 1) * 512], b[:, :])
                stream.done_one()
            dma("sp", o_mk[:, :], KO, wr=False)
            for half in range(2):
                slot = stream.next()
                w = W3(slot)
                for mc in range(2):
                    b = fbank()
                    for kc in range(8):
                        mm(b[:, :], mT(kc)[:, mc * 128:(mc + 1) * 128], w[:, kc, :], kc == 0, kc == 7)
                    if mc == 0:
                        act(VO[:, half * 512:(half + 1) * 512], b[:, :], AF.Copy)
                        vcopy(vS_p[:, mc * D + half * 512: mc * D + (half + 1) * 512], VO[:, half * 512:(half + 1) * 512])
                    else:
                        vcopy(vS_p[:, mc * D + half * 512: mc * D + (half + 1) * 512], b[:, :])
                stream.done_one()
            dma("sp", o_mv[:, :], VO, wr=False)

        ckpt(3)
        for s in (prm, smp):
            vmemset(s.U[:, :], 0.0) if s is prm else None
        vmemset(prm.V[:, :], 0.0)

        tp_rr = [0]
        def stage_A_loads(t):
            prm.cur = t % 2
            load_x(prm, lambda i, t=t: xp[t * TN + i * 128: t * TN + (i + 1) * 128, :])
            if t == 0:
                load_x(hal, lambda i: xh[:, :])

        for t in range(NT):
            segs = [prm] + ([smp] if t == 0 else [])
            prm.cur = t % 2
            if t == 0:
                stage_A_loads(0)
                prenorm(hal, "g_mix_pre")
                load_x(smp, lambda i: xsm[:, :])
                load_gpost()
                prenorm(prm, "g_mix_pre")
                prenorm(smp, "g_mix_pre")
                load_x(memseg, lambda i: mem[i * 128:(i + 1) * 128, :])
            else:
                for g in range(4):
                    vcopy(Uv(prm, g, 0, 16), Uv(prm, g, TN, TN + 16))
                    vcopy(Vv(prm, g, 0, 2), Vv(prm, g, TN, TN + 2))
            ckpt(4)
            slot = stream.next()
            if t == 0:
                mix_u(hal, slot, hist_to=prm)
            for s in segs:
                mix_u(s, slot)
            stream.done_one()
            for s in segs:
                pool_ops(s, first_tile_fix=(t == 0 and s is prm))
            slot = stream.next()
            if t == 0:
                mix_C(hal, slot)
            for s in segs:
                mix_C(s, slot)
            stream.done_one()
            slot = stream.next()
            if t == 0:
                mix_h(hal, slot, hist_to=prm)
            for s in segs:
                mix_h(s, slot)
            stream.done_one()
            for s in segs:
                conv_ops(s)
            for s in segs:
                pool_map(s)
            slot = stream.next()
            for s in segs:
                mix_B(s, slot)
            stream.done_one()
            ckpt(5)
            if t == 0:
                prenorm(memseg, "g_mem")
                sample_cache_casts()
            if t == NT - 1:
                stp = misc[0:16, :]; scp = misc[0:2, :]
                for g in range(4):
                    tr(RB[0:16, g * 128:(g + 1) * 128], Uv(prm, g, TN, TN + 16), idf[:, :])
                vcopy(stp[:, :], RB[0:16, :])
                dma("sp", o_pool_p[:, :], stp[:, :], wr=False)
                for g in range(4):
                    tr(RB[0:2, g * 128:(g + 1) * 128], Vv(prm, g, TN, TN + 2), idf[:, :])
                vcopy(scp[:, :], RB[0:2, :])
                dma("sp", o_conv_p[:, :], scp[:, :], wr=False)
            if t == 0:
                sts = misc[0:32, :]; scs = misc[0:4, :]
                utmp = sb("utmp", [128, 4 * 32]); vtmp = sb("vtmp", [128, 4 * 4])
                for g in range(4):
                    vcopy(utmp[:, g * 32:(g + 1) * 32].rearrange("p (b c) -> p b c", b=2), Uv(smp, g, 16, 32))
                    vcopy(vtmp[:, g * 4:(g + 1) * 4].rearrange("p (b c) -> p b c", b=2), Vv(smp, g, 16, 18))
                for g in range(4):
                    tr(RB[0:32, g * 128:(g + 1) * 128], utmp[:, g * 32:(g + 1) * 32], idf[:, :])
                vcopy(sts[:, :], RB[0:32, :])
                dma("sp", o_pool_s[:, :], sts[:, :], wr=False)
                for g in range(4):
                    tr(RB[0:4, g * 128:(g + 1) * 128], vtmp[:, g * 4:(g + 1) * 4], idf[:, :])
                vcopy(scs[:, :], RB[0:4, :])
                dma("sp", o_conv_s[:, :], scs[:, :], wr=False)
            split = {}

            def mid_first_block():
                split["slot"] = stream.next()
                split["outs"] = proj_split_begin(prm, split["slot"])
            tok_out1(segs, lambda s, kc: s.merged(kc), "g_mix_post", "g_x_pre",
                     mid_tail=(mid_first_block if t > 0 else None))
            ckpt(6)
            if t == 0:
                kv_phase()
            for half in range(2):
                pre = split.pop("slot", None) if half == 0 else None
                slot = pre if pre is not None else stream.next()
                for s in segs:
                    if pre is not None and s is prm:
                        evs = [(lambda ps, dst=s.bfv("qT", (half * 4 + jj) * s.N, s.N): act(dst, ps, AF.Copy))
                               for jj in range(4)]
                        proj_split_end(s, slot, split.pop("outs"), evs)
                        continue
                    for jj in range(4):
                        dst = s.bfv("qT", (half * 4 + jj) * s.N, s.N)
                        proj_chunk(s, slot, jj, lambda ps, dst=dst: act(dst, ps, AF.Copy))
                stream.done_one()
            if smp in segs:
                attention(smp, kT_s, vS_s)
            attention(prm, [kT_p], [vS_p])
            tok_out1(segs, lambda s, kc: s.merged(kc), "g_x_post", "g_ff_pre", mid_tail=mid_first_block)
            ckpt(7)
            if t + 1 < NT:
                stage_A_loads(t + 1)
                prm.cur = t % 2
            for c in range(8):
                pre = split.pop("slot", None) if c == 0 else None
                slot = pre if pre is not None else stream.next()
                for s in segs:
                    evs = []
                    for jj in range(4):
                        ch = c * 4 + jj
                        hid = s.bfv("hid", ch * s.N, s.N)
                        rt = s.bfv("rtmp", (ch % 2) * s.N, s.N)

                        def ev(ps, hid=hid, rt=rt):
                            act(rt, ps, AF.Relu)
                            tt(hid, rt, rt, ALU.mult)
                        evs.append(ev)
                    if pre is not None and s is prm:
                        proj_split_end(s, slot, split.pop("outs"), evs)
                    else:
                        for jj in range(4):
                            proj_chunk(s, slot, jj, evs[jj])
                stream.done_one()
            hooks = {}
            cur_t = t
            if t + 1 < NT:
                nt_ = t + 1

                def mk_hooks(nt_=nt_):
                    def h00():
                        prm.cur = nt_ % 2
                        pre1(prm, 0, "g_mix_pre")
                        prm.cur = cur_t % 2

                    def mid(i):
                        def f():
                            prm.cur = nt_ % 2
                            pre2(prm, i - 1, "g_mix_pre")
                            if i < 4:
                                pre1(prm, i, "g_mix_pre")
                            prm.cur = cur_t % 2
                        return f
                    return {(0, 0): h00, (0, 1): mid(1), (0, 2): mid(2), (0, 3): mid(3), (1, 0): mid(4)}
                hooks = mk_hooks()
            prm.cur = t % 2
            tok_out4(segs, lambda s, kc: s.bfv("hid", kc * s.N, s.N), "g_ff_post", hooks)
            ckpt(8)
            prm.cur = t % 2
            for i in range(4):
                dma("sp", yp[t * TN + i * 128: t * TN + (i + 1) * 128, :], Xs(prm, i), wr=False)
            if smp in segs:
                dma("sp", ysm[:, :], smp.X[0:32, 0:D], wr=False)
    except _Stop:
        pass
    with nc.allow_low_precision("bf16 matmul operands, fp32 accumulation"):
        P.emit()
    return nc, P


_CACHE = {}


def _get_program():
    if "nc" not in _CACHE:
        _CACHE["nc"], _CACHE["P"] = build_program()
    return _CACHE["nc"]


def kernel(x_prompt, x_sample, state_pool, state_conv, cache_mem_k, cache_mem_v, mem_prompt,
           g_mix_pre, w_in, w_pool, pool_scale, w_conv, g_pool_out, g_conv_out, w_out,
           g_mix_post, g_mem, w_k, w_v, g_x_pre, w_q, w_co, g_x_post,
           g_ff_pre, w_up, w_down, g_ff_post):
    f = lambda a: np.ascontiguousarray(np.asarray(a, dtype=np.float32))
    x_prompt = f(x_prompt); x_sample = f(x_sample)
    nc = _get_program()
    col = lambda v: f(v).reshape(-1, 128).T
    pvec = np.concatenate([col(pool_scale[0]), col(w_conv[0, 0]), col(w_conv[0, 1]), col(w_conv[0, 2]),
                           col(g_pool_out[0]), col(g_conv_out[0])], axis=1)
    gcolv = np.concatenate([col(g_mix_pre[0]), col(g_x_pre[0]), col(g_ff_pre[0]), col(g_mem[0])], axis=1)
    shared = {
        "pvec": f(pvec), "gcol": f(gcolv), "ident": np.eye(128, dtype=np.float32),
        "g_mix_pre": f(g_mix_pre), "g_mix_post": f(g_mix_post), "g_x_pre": f(g_x_pre), "g_x_post": f(g_x_post),
        "g_ff_pre": f(g_ff_pre), "g_ff_post": f(g_ff_post), "g_mem": f(g_mem),
        "w_in": f(w_in[0]), "w_out": f(w_out[0]), "w_k": f(w_k[0]).reshape(D, D), "w_v": f(w_v[0]).reshape(D, D),
        "w_q": f(w_q[0]).reshape(D, D), "w_co": f(w_co[0]).reshape(D, D), "w_up": f(w_up[0]),
        "w_down": f(w_down[0]), "w_pool": f(w_pool[0]),
    }
    pos = np.arange(16)
    inv_start = np.stack([1.0 / np.minimum(pos + 1, w) for w in (2, 4, 8, 16)]).astype(np.float32)
    inv_mid = np.stack([np.full(16, 1.0 / w) for w in (2, 4, 8, 16)]).astype(np.float32)
    in_maps = []
    for c in range(NCORES):
        b, half = c // 2, c % 2
        t0 = half * TP
        m = f(mem_prompt[b])
        if half == 1:
            m = np.concatenate([m[128:], m[:128]], axis=0)
        d = dict(shared)
        d["xp"] = f(x_prompt[b, t0:t0 + TP])
        d["xh"] = f(x_prompt[b, t0 - XH:t0]) if half == 1 else np.zeros((XH, D), np.float32)
        d["xsm"] = f(x_sample[2 * c:2 * c + 2]).reshape(32, D)
        d["spool"] = f(state_pool[0, 2 * c:2 * c + 2]); d["sconv"] = f(state_conv[0, 2 * c:2 * c + 2])
        d["ck"] = f(cache_mem_k[0, 2 * c:2 * c + 2]).reshape(2, 256, D)
        d["cv"] = f(cache_mem_v[0, 2 * c:2 * c + 2]).reshape(2, 256, D)
        d["mem"] = f(m)
        tab = inv_start if half == 0 else inv_mid
        d["invc"] = np.ascontiguousarray(np.broadcast_to(tab.reshape(1, 64), (128, 64)))
        in_maps.append(d)
    res = run_bass_kernel_spmd(nc, in_maps, core_ids=list(range(NCORES)))
    R = res.results
    yp = np.zeros((4, 4096, D), np.float32); ys = np.zeros((16, 16, D), np.float32)
    pool_p = np.zeros((1, 4, 15, 512), np.float32); conv_p = np.zeros((1, 4, 2, 512), np.float32)
    mk = np.zeros((1, 4, 256, 4, 256), np.float32); mv = np.zeros((1, 4, 256, 4, 256), np.float32)
    pool_s = np.zeros((1, 16, 15, 512), np.float32); conv_s = np.zeros((1, 16, 2, 512), np.float32)
    for c in range(NCORES):
        b, half = c // 2, c % 2
        r = R[c]
        yp[b, half * TP:(half + 1) * TP] = r["yp"]
        ys[2 * c:2 * c + 2] = r["ysm"].reshape(2, 16, D)
        if half == 1:
            pool_p[0, b] = r["pool_p"][1:16]
            conv_p[0, b] = r["conv_p"]
        mk[0, b, half * 128:(half + 1) * 128] = r["mk"].reshape(128, 4, 256)
        mv[0, b, half * 128:(half + 1) * 128] = r["mv"].reshape(128, 4, 256)
        pool_s[0, 2 * c:2 * c + 2] = r["pool_s"].reshape(2, 16, 512)[:, 1:16]
        conv_s[0, 2 * c:2 * c + 2] = r["conv_s"].reshape(2, 2, 512)
    return yp, ys, pool_p, conv_p, mk, mv, pool_s, conv_s
```

```python
import numpy as np
import concourse.bass as bass
import concourse.mybir as mybir
from concourse.bass_utils import run_bass_kernel_spmd

F32 = mybir.dt.float32
BF16 = mybir.dt.bfloat16
AF = mybir.ActivationFunctionType
ALU = mybir.AluOpType

_ESZ = {F32: 4, BF16: 2, mybir.dt.int32: 4, mybir.dt.uint8: 1}


def _esz(dt):
    return _ESZ[dt]


def ap_rects(ap):
    t = ap.tensor
    name = t.name
    esz = _esz(ap.dtype)
    a = list(ap.ap)
    off = int(ap.offset)
    space = str(t.space)
    if "DRAM" in space.upper() or "HBM" in space.upper():
        lo = off
        hi = off
        for st, cnt in a:
            if st >= 0:
                hi += st * (cnt - 1)
            else:
                lo += st * (cnt - 1)
        return name, [(0, 1, lo * esz, (hi + 1) * esz)]
    rowlen = 1
    for s in list(t.shape)[1:]:
        rowlen *= int(s)
    if "PSUM" in space.upper():
        p0 = off // rowlen
        pst, pcnt = a[0]
        p1 = p0 + (pcnt if pst != 0 else 1)
        return name, [((p0 // 32) * 32, ((p1 + 31) // 32) * 32, 0, rowlen * esz)]
    p0 = off // rowlen
    c0 = off % rowlen
    pst, pcnt = a[0]
    p1 = p0 + (pcnt if pst != 0 else 1)
    free = a[1:]
    if not free:
        return name, [(p0, p1, c0 * esz, (c0 + 1) * esz)]
    inner_st, inner_cnt = free[-1]
    outer = free[:-1]
    nouter = 1
    for st, cnt in outer:
        nouter *= cnt
    inner_len = (abs(inner_st) * (inner_cnt - 1) + 1)
    if nouter > 64:
        lo = c0
        hi = c0
        for st, cnt in free:
            if st >= 0:
                hi += st * (cnt - 1)
            else:
                lo += st * (cnt - 1)
        return name, [(p0, p1, lo * esz, (hi + 1) * esz)]
    starts = [c0]
    for st, cnt in outer:
        starts = [s + i * st for s in starts for i in range(cnt)]
    rs = sorted((s * esz, (s + inner_len) * esz) for s in starts)
    merged = []
    for lo, hi in rs:
        if merged and lo <= merged[-1][1]:
            merged[-1][1] = max(merged[-1][1], hi)
        else:
            merged.append([lo, hi])
    return name, [(p0, p1, lo, hi) for lo, hi in merged]


def _ov(r, q):
    return r[0] < q[1] and q[0] < r[1] and r[2] < q[3] and q[2] < r[3]


def _covers(big, small):
    return big[0] <= small[0] and big[1] >= small[1] and big[2] <= small[2] and big[3] >= small[3]


class Op:
    __slots__ = ("eng", "idx", "fn", "is_dma", "deps", "needs_signal", "sigval",
                 "sem", "semval", "name")

    def __init__(self, eng, fn, is_dma, name):
        self.eng = eng
        self.fn = fn
        self.is_dma = is_dma
        self.deps = {}
        self.needs_signal = False
        self.sigval = None
        self.sem = None
        self.semval = None
        self.name = name


class Rec:
    __slots__ = ("rect", "writer", "readers")

    def __init__(self, rect, writer):
        self.rect = rect
        self.writer = writer
        self.readers = {}


class Prog:
    ENGS = ["pe", "act", "dve", "pool", "sp"]

    def __init__(self, nc, n_dma_sems=None):
        self.nc = nc
        self.ops = {e: [] for e in self.ENGS}
        self.state = {}
        self.n_dma_sems = n_dma_sems or {"sp": 20, "pool": 10, "act": 6}
        self.dma_rr = {e: 0 for e in self.ENGS}
        self.dma_last = {}

    def add(self, eng, fn, reads=(), writes=(), dma=False, name=""):
        op = Op(eng, fn, dma, name)
        op.idx = len(self.ops[eng])
        self.ops[eng].append(op)
        rkey = ("dma", id(op)) if dma else eng
        for ap in reads:
            tn, rects = ap_rects(ap)
            recs = self.state.setdefault(tn, [])
            for rect in rects:
                for rec in recs:
                    if _ov(rec.rect, rect):
                        if rec.writer is not None and rec.writer is not op:
                            op.deps.setdefault(rec.writer, "raw")
                            if op.deps[rec.writer] != "raw":
                                op.deps[rec.writer] = "raw"
                        rec.readers[rkey] = op
        for ap in writes:
            tn, rects = ap_rects(ap)
            recs = self.state.setdefault(tn, [])
            for rect in rects:
                keep = []
                for rec in recs:
                    if _ov(rec.rect, rect):
                        if rec.writer is not None and rec.writer is not op:
                            op.deps.setdefault(rec.writer, "war")
                        for rd in rec.readers.values():
                            if rd is not op:
                                op.deps.setdefault(rd, "war")
                        if _covers(rect, rec.rect):
                            continue
                    keep.append(rec)
                keep.append(Rec(rect, op))
                recs[:] = keep
        if dma:
            n = self.n_dma_sems[eng]
            slot = self.dma_rr[eng] % n
            self.dma_rr[eng] += 1
            prev = self.dma_last.get((eng, slot))
            op.sem = (eng, slot)
            op.semval = 16 if prev is None else prev.semval + 16
            if prev is not None:
                op.deps.setdefault(prev, "war")
            self.dma_last[(eng, slot)] = op
        return op

    def emit(self, final_wait_eng="sp"):
        nc = self.nc
        plan = {}
        for e in self.ENGS:
            for op in self.ops[e]:
                need = []
                for p, kind in op.deps.items():
                    if p.is_dma:
                        need.append(p)
                    elif p.eng == op.eng:
                        if op.eng in ("act", "dve", "pool"):
                            p.needs_signal = True
                            need.append(p)
                    else:
                        p.needs_signal = True
                        need.append(p)
                plan[op] = need
        for e in self.ENGS:
            c = 0
            for op in self.ops[e]:
                if op.needs_signal and not op.is_dma:
                    c += 1
                    op.sigval = c
        import contextlib
        stack = contextlib.ExitStack()
        esem = {}
        for e in ("pe", "act", "dve", "pool"):
            esem[e] = stack.enter_context(nc.semaphore("s_" + e))
        dsem = {}
        for e, n in self.n_dma_sems.items():
            for s in range(n):
                if (e, s) in self.dma_last:
                    dsem[(e, s)] = stack.enter_context(nc.semaphore("d_%s_%d" % (e, s)))

        def run(e, eng):
            seen = {}
            for op in self.ops[e]:
                waits = {}
                for p in plan[op]:
                    if p.is_dma:
                        k = dsem[p.sem]
                        v = p.semval
                    else:
                        k = esem[p.eng]
                        v = p.sigval
                    kk = id(k)
                    if seen.get(kk, 0) >= v:
                        continue
                    if kk not in waits or waits[kk][1] < v:
                        waits[kk] = (k, v)
                for kk, (k, v) in waits.items():
                    eng.wait_ge(k, v)
                    seen[kk] = v
                ins = op.fn(eng)
                if op.is_dma:
                    ins.then_inc(dsem[op.sem], 16)
                elif op.needs_signal:
                    ins.then_inc(esem[e], 1)
            if e == final_wait_eng:
                for key, last in self.dma_last.items():
                    k = dsem[key]
                    if seen.get(id(k), 0) < last.semval:
                        eng.wait_ge(k, last.semval)

        with nc.Block() as block:
            @block.tensor
            def _(eng):
                run("pe", eng)

            @block.scalar
            def _(eng):
                run("act", eng)

            @block.vector
            def _(eng):
                run("dve", eng)

            @block.gpsimd
            def _(eng):
                run("pool", eng)

            @block.sync
            def _(eng):
                run("sp", eng)
        stack.close()

    def stats(self):
        return {e: len(self.ops[e]) for e in self.ENGS}


NCORES = 8
D = 1024
TP = 2048
NT = 4
TN = 512
EPS = 1e-6
RING = 4
USE_POOL_ADDS = False
XH = 16


def _blocks():
    kv = [("w_k", 0, 0), ("w_k", 0, 1), ("w_v", 0, 0), ("w_v", 0, 1)]
    tile = [("w_in", 0, 0), ("w_in", 0, 2), ("w_in", 0, 3), ("w_in", 0, 1),
            ("w_out", 0, 0), ("w_out", 0, 1), ("w_q", 0, 0), ("w_q", 0, 1),
            ("w_co", 0, 0), ("w_co", 0, 1)]
    tile += [("w_up", 0, c) for c in range(8)]
    tile += [("w_down", q, h) for h in range(2) for q in range(4)]
    tile0 = tile[:6] + kv + tile[6:]
    return tile0, tile


class Seg:
    pass


class _Stop(Exception):
    pass


def ckpt(k):
    return None


def build_program():
    nc = bass.Bass("TRN2", target_bir_lowering=False)
    P = Prog(nc, n_dma_sems={"sp": 24, "pool": 12, "act": 2})

    def din(name, shape, dt=F32):
        return nc.dram_tensor(name, list(shape), dt, kind="ExternalInput")

    def dout(name, shape, dt=F32):
        return nc.dram_tensor(name, list(shape), dt, kind="ExternalOutput")

    xp = din("xp", [TP, D]); xh = din("xh", [XH, D]); xsm = din("xsm", [32, D])
    spool = din("spool", [2, 15, 512]); sconv = din("sconv", [2, 2, 512])
    ck = din("ck", [2, 256, D]); cv = din("cv", [2, 256, D]); mem = din("mem", [256, D])
    invc = din("invc", [128, 64]); pvec = din("pvec", [128, 24]); ident = din("ident", [128, 128])
    gnames = ["g_mix_pre", "g_mix_post", "g_x_pre", "g_x_post", "g_ff_pre", "g_ff_post", "g_mem"]
    gd = {n: din(n, [1, D]) for n in gnames}
    gcol = din("gcol", [128, 32])
    wd = {"w_in": din("w_in", [D, 2048]), "w_out": din("w_out", [D, D]), "w_k": din("w_k", [D, D]),
          "w_v": din("w_v", [D, D]), "w_q": din("w_q", [D, D]), "w_co": din("w_co", [D, D]),
          "w_up": din("w_up", [D, 4096]), "w_down": din("w_down", [4096, D])}
    w_pool = din("w_pool", [4, 128, 128])
    yp = dout("yp", [TP, D]); ysm = dout("ysm", [32, D])
    o_pool_p = dout("pool_p", [16, 512]); o_conv_p = dout("conv_p", [2, 512])
    o_mk = dout("mk", [128, D]); o_mv = dout("mv", [128, D])
    o_pool_s = dout("pool_s", [32, 512]); o_conv_s = dout("conv_s", [4, 512])

    tile0_blocks, tile_blocks = _blocks()
    uniq = list(tile0_blocks)
    wsc = nc.dram_tensor("wsc", [len(uniq), 128, 4096], BF16, kind="Internal")
    bidx = {b: i for i, b in enumerate(uniq)}

    def sb(name, shape, dt=F32):
        return nc.alloc_sbuf_tensor(name, list(shape), dt)

    idf = sb("idf", [128, 128]); idb = sb("idb", [128, 128], BF16)
    ones = sb("ones", [128, 128], BF16); bd64 = sb("bd64", [128, 128], BF16)
    epsb = sb("epsb", [128, 1])
    gpost = {n: sb("bc_" + n, [128, D]) for n in ("g_mix_post", "g_x_post", "g_ff_post")}
    gcs = sb("gcs", [128, 32]); gcb = sb("gcb", [128, 32], BF16)
    pv = sb("pv", [128, 24]); invcs = sb("invcs", [128, 64])
    wpool = sb("wpool", [128, 4 * 128], BF16)
    junk = sb("junk", [128, D], BF16)
    xs_g = sb("xs_g", [128, 2 * D], BF16)
    misc = sb("misc", [128, 512])
    kT_p = sb("kT_p", [128, 8 * 256], BF16); vS_p = sb("vS_p", [128, 2 * D], BF16)
    memX = sb("memX", [128, 2 * D]); mTt = sb("mTt", [128, 8 * 256], BF16)
    ring = [sb("ring%d" % i, [128, 4096], BF16) for i in range(RING)]
    banks = [nc.alloc_psum_tensor("bank%d" % i, [128, 512], F32) for i in range(8)]
    TB = banks[0:4]
    FB = banks[4:7]
    RB = banks[7]
    RBb = RB.bitcast(BF16)
    fb_rr = [0]

    ALLB = [banks[4], banks[5], banks[6], banks[0], banks[1], banks[2], banks[3]]
    fb_wide = [True]

    def fbank():
        pool = ALLB if fb_wide[0] else FB
        b = pool[fb_rr[0] % len(pool)]
        fb_rr[0] += 1
        return b

    def mm(out, lhsT, rhs, start, stop):
        P.add("pe", lambda e: e.matmul(out, lhsT=lhsT, rhs=rhs, start=start, stop=stop),
              reads=[lhsT, rhs], writes=[out])

    def tr(out, in_, idn):
        P.add("pe", lambda e: e.transpose(out=out, in_=in_, identity=idn), reads=[in_, idn], writes=[out])

    def act(out, in_, func, scale=1.0, bias=None, accum=None):
        rd = [in_]
        wr = [out]
        kw = {}
        if bias is not None:
            kw["bias"] = bias
            rd.append(bias)
        if not isinstance(scale, float):
            rd.append(scale)
        if accum is not None:
            kw["accum_out"] = accum
            wr.append(accum)
        P.add("act", lambda e: e.activation(out=out, in_=in_, func=func, scale=scale, **kw), reads=rd, writes=wr)

    def tt(out, in0, in1, op):
        P.add("dve", lambda e: e.tensor_tensor(out=out, in0=in0, in1=in1, op=op), reads=[in0, in1], writes=[out])

    def ptt(out, in0, in1, op):
        P.add("pool", lambda e: e.tensor_tensor(out=out, in0=in0, in1=in1, op=op), reads=[in0, in1], writes=[out])

    def ts(out, in0, s1, op0, s2=None, op1=None):
        rd = [in0] + [s for s in (s1, s2) if s is not None and not isinstance(s, float)]
        if op1 is None:
            P.add("dve", lambda e: e.tensor_scalar(out=out, in0=in0, scalar1=s1, scalar2=None, op0=op0),
                  reads=rd, writes=[out])
        else:
            P.add("dve", lambda e: e.tensor_scalar(out=out, in0=in0, scalar1=s1, scalar2=s2, op0=op0, op1=op1),
                  reads=rd, writes=[out])

    def stt(out, in0, scalar, in1, op0, op1):
        rd = [in0, in1] + ([] if isinstance(scalar, float) else [scalar])
        P.add("dve", lambda e: e.scalar_tensor_tensor(out=out, in0=in0, scalar=scalar, in1=in1, op0=op0, op1=op1),
              reads=rd, writes=[out])

    def vcopy(out, in_):
        P.add("dve", lambda e: e.tensor_copy(out=out, in_=in_), reads=[in_], writes=[out])

    def vrecip(out, in_):
        P.add("dve", lambda e: e.reciprocal(out=out, in_=in_), reads=[in_], writes=[out])

    def vmemset(ap, v):
        P.add("dve", lambda e: e.memset(ap, v), writes=[ap])

    def dma(q, out, in_, rd=True, wr=True):
        P.add(q, lambda e: e.dma_start(out=out, in_=in_), reads=[in_] if rd else [], writes=[out] if wr else [],
              dma=True)

    try:
        dma("sp", idf[:, :], ident[:, :], rd=False)
        vcopy(idb[:, :], idf[:, :])
        vmemset(ones[:, :], 1.0)
        vmemset(bd64[:, :], 0.0)
        vmemset(bd64[0:64, 0:64], 1.0)
        vmemset(bd64[64:128, 64:128], 1.0)
        vmemset(epsb[:, :], EPS)
        def load_gpost():
            for n in gpost:
                dma("sp", gpost[n][:, :], gd[n][0:1, :].broadcast_to([128, D]), rd=False)
        dma("sp", gcs[:, :], gcol[:, :], rd=False)
        vcopy(gcb[:, :], gcs[:, :])
        dma("sp", pv[:, :], pvec[:, :], rd=False)
        dma("sp", invcs[:, :], invc[:, :], rd=False)
        PS = lambda g: pv[:, g:g + 1]
        WC = lambda k, cc: pv[:, 4 + k * 4 + cc: 5 + k * 4 + cc]
        GPO = lambda g: pv[:, 16 + g:17 + g]
        GCO = lambda cc: pv[:, 20 + cc:21 + cc]
        GCOL = {"g_mix_pre": 0, "g_x_pre": 1, "g_ff_pre": 2, "g_mem": 3}

        def cast_wpool():
            P.add("pool", lambda e: e.dma_start(out=wpool[:, :].rearrange("p (g d) -> p g d", g=4),
                                                in_=w_pool.rearrange("g c d -> c g d")),
                  writes=[wpool[:, :]], dma=True)

        ckpt(1)
        wseq = [0]
        wb_pending = {}

        def load_block(blk):
            k = wseq[0]
            slot = ring[k % RING]
            wseq[0] += 1
            if k < len(tile0_blocks):
                wn, r, c = blk
                src = wd[wn][r * 1024:(r + 1) * 1024, c * 512:(c + 1) * 512].rearrange("(kc p) n -> p kc n", p=128)
                dst = slot[:, :].rearrange("p (kc n) -> p kc n", kc=8)
                P.add("pool", lambda e, src=src, dst=dst: e.dma_start(out=dst, in_=src), writes=[slot[:, :]], dma=True)
                if wn not in ("w_k", "w_v"):
                    wb_pending[k] = (lambda blk=blk, slot=slot: dma("sp", wsc[bidx[blk]], slot[:, :]))
            else:
                dma("sp", slot[:, :], wsc[bidx[blk]])
            return slot

        class Stream:
            def __init__(self, blocks):
                self.blocks = list(blocks)
                self.loaded = []
                self.pos = 0
                self.issued = 0

            def prefetch(self, depth):
                while self.issued < len(self.blocks) and self.issued - self.pos < depth:
                    self.loaded.append(load_block(self.blocks[self.issued]))
                    self.issued += 1

            def next(self):
                self.prefetch(RING - 1)
                s = self.loaded[self.pos]
                if self.pos in wb_pending:
                    wb_pending.pop(self.pos)()
                self.pos += 1
                return s

            def done_one(self):
                self.prefetch(RING - 1)

        all_blocks = list(tile0_blocks)
        for t in range(NT - 1):
            all_blocks += tile_blocks
        stream = Stream(all_blocks)

        def W3(slot):
            return slot[:, :].rearrange("p (kc n) -> p kc n", kc=8)

        def make_seg(name, nseq, L, rows, nsub, share=None):
            s = Seg()
            s.name = name; s.nseq = nseq; s.L = L; s.N = nseq * L; s.rows = rows; s.nsub = nsub
            n = s.N
            s.WU = 16 + L; s.WV = 2 + L
            s.xs = xs_g
            s.cur = 0
            if share is None:
                s.X = sb(name + "_X", [128, nsub * D])
                s.Xb = [s.X] + ([sb(name + "_X1", [128, nsub * D])] if name == "pr" else [])
                s.xT = sb(name + "_xT", [128, 8 * n], BF16)
                s.U = sb(name + "_U", [128, 4 * nseq * s.WU])
                s.V = sb(name + "_V", [128, 4 * nseq * s.WV])
                s.st = sb(name + "_st", [128, 16 * nsub])
            else:
                s.X = share.X; s.Xb = share.Xb; s.xT = sb(name + "_xT", [128, 8 * n], BF16)
                s.U = share.U; s.V = share.V; s.st = share.st
            szO = nsub * 4096 if share is None else 0
            offA = 16 * n
            o = offA
            lay = {}
            for nm, sz in (("Cs", 16 * n), ("Z", 16 * n), ("Y", 16 * n), ("Ym", 16 * n),
                           ("sA", 4 * nseq * s.WU), ("sB", 4 * nseq * s.WU), ("pooled", 8 * n), ("sq", 4 * n),
                           ("rstdF", 8 * n)):
                lay[nm] = o
                o += sz
            endB = o
            lay["O"] = offA
            oD = offA + szO
            lay["qT"] = oD; lay["pT"] = oD + 16 * n; lay["rden"] = oD + 24 * n
            endD = oD + 32 * n
            lay["hid"] = oD; lay["rtmp"] = oD + 64 * n
            endE = oD + 68 * n
            tot = max(endB, endD, endE, offA + szO)
            tot = (tot + 63) // 64 * 64
            s.un = sb(name + "_un", [128, tot // 4])
            s.unb = s.un.bitcast(BF16)
            s.lay = lay

            def f32v(nm, off_elems, ncols):
                b = lay[nm] // 4 + off_elems
                return s.un[:, b:b + ncols]

            def bfv(nm, off_elems, ncols):
                b = lay[nm] // 2 + off_elems
                return s.unb[:, b:b + ncols]
            s.f32v = f32v; s.bfv = bfv
            s.merged = lambda kc: s.unb[:, kc * n:(kc + 1) * n]
            return s

        prm = make_seg("pr", 1, TN, 128, 4)
        smp = make_seg("sm", 2, 16, 32, 1)
        hal = make_seg("ha", 1, 16, 16, 1, share=smp)

        def seq3(ap2d, nseq, width, lo, hi):
            if nseq == 1:
                return ap2d[:, lo:hi]
            return ap2d.rearrange("p (b c) -> p b c", b=nseq)[:, :, lo:hi]

        def tok3(ap2d, nseq, L):
            if nseq == 1:
                return ap2d
            return ap2d.rearrange("p (b c) -> p b c", b=nseq)

        def Uv(s, g, lo, hi):
            return seq3(s.U[:, g * s.nseq * s.WU:(g + 1) * s.nseq * s.WU], s.nseq, s.WU, lo, hi)

        def Vv(s, cc, lo, hi):
            return seq3(s.V[:, cc * s.nseq * s.WV:(cc + 1) * s.nseq * s.WV], s.nseq, s.WV, lo, hi)

        def rstd_rows(s, i, ss_ap, r, tc=6, oc=8):
            t = s.st[0:r, i * 16 + tc:i * 16 + tc + 1]
            o = s.st[0:r, i * 16 + oc:i * 16 + oc + 1]
            act(t, ss_ap, AF.Ln, scale=1.0 / D, bias=epsb[0:r, 0:1])
            act(o, t, AF.Exp, scale=-0.5)
            return o

        def Xs(s, i):
            return s.Xb[s.cur][0:s.rows, i * D:(i + 1) * D]

        xs_rr = [0]
        xs_slot = {}

        def pre1(s, i, gname):
            r = s.rows
            X = Xs(s, i)
            ss = s.st[0:r, i * 16:i * 16 + 1]
            act(junk[0:r, :], X, AF.Square, accum=ss)
            rs = rstd_rows(s, i, ss, r, 7, 9)
            k = xs_rr[0] % 2
            xs_rr[0] += 1
            xs_slot[(s.name, i)] = k
            xs = s.xs[0:r, k * D:(k + 1) * D]
            ts(xs, X, rs, ALU.mult)

        def pre2(s, i, gname):
            r = s.rows
            gc = GCOL[gname]
            k = xs_slot[(s.name, i)]
            xs = s.xs[0:r, k * D:(k + 1) * D]
            for c in range(8):
                tr(RBb[:, c * 128:c * 128 + r], xs[:, c * 128:(c + 1) * 128], idb[0:r, 0:r])
            src = RBb[:, :].rearrange("p (c t) -> p c t", c=8)[:, :, 0:r]
            dst = s.xT[:, :].rearrange("p (c t) -> p c t", c=8)[:, :, i * 128:i * 128 + r]
            gb = gcs[:, gc * 8:(gc + 1) * 8].unsqueeze(2).to_broadcast([128, 8, r])
            tt(dst, src, gb, ALU.mult)

        def prenorm(s, gname):
            for i in range(s.nsub):
                pre1(s, i, gname)
                pre2(s, i, gname)

        def xTc(s, kc):
            return s.xT[:, kc * s.N:(kc + 1) * s.N]

        def proj_chunk(s, slot, jj, evac):
            w = W3(slot)
            b = fbank()
            out = b[:, 0:s.N]
            for kc in range(8):
                mm(out, w[:, kc, jj * 128:(jj + 1) * 128], xTc(s, kc), kc == 0, kc == 7)
            evac(out)

        SPL = 384

        def proj_split_begin(s, slot):
            w = W3(slot)
            outs = []
            for jj in range(4):
                b = fbank()
                out = b[:, 0:s.N]
                for kc in range(8):
                    mm(out[:, 0:SPL], w[:, kc, jj * 128:(jj + 1) * 128], xTc(s, kc)[:, 0:SPL], kc == 0, kc == 7)
                outs.append(out)
            return outs

        def proj_split_end(s, slot, outs, evacs):
            w = W3(slot)
            for jj in range(4):
                for kc in range(8):
                    mm(outs[jj][:, SPL:s.N], w[:, kc, jj * 128:(jj + 1) * 128], xTc(s, kc)[:, SPL:s.N],
                       kc == 0, kc == 7)
                evacs[jj](outs[jj])

        def load_x(s, src_rows):
            for i in range(s.nsub):
                dma("sp", Xs(s, i), src_rows(i), rd=False)

        def mix_u(s, slot, hist_to=None):
            for g in range(4):
                if hist_to is None:
                    dst = Uv(s, g, 16, 16 + s.L)
                    proj_chunk(s, slot, g, lambda ps, dst=dst: act(dst, tok3(ps, s.nseq, s.L), AF.Copy))
                else:
                    dst = Uv(hist_to, g, 0, 16)
                    proj_chunk(s, slot, g, lambda ps, dst=dst: act(dst, ps, AF.Copy))

        def mix_C(s, slot):
            for cc in range(4):
                dst = s.f32v("Cs", cc * s.N, s.N)
                proj_chunk(s, slot, cc, lambda ps, dst=dst: act(dst, ps, AF.Copy))

        def mix_h(s, slot, hist_to=None):
            for cc in range(4):
                cs = s.f32v("Cs", cc * s.N, s.N)
                if hist_to is None:
                    dst = Vv(s, cc, 2, 2 + s.L)
                    proj_chunk(s, slot, cc, lambda ps, dst=dst, cs=cs: tt(dst, tok3(ps, s.nseq, s.L),
                                                                         tok3(cs, s.nseq, s.L), ALU.mult))
                else:
                    dst = Vv(hist_to, cc, 0, 2)
                    proj_chunk(s, slot, cc, lambda ps, dst=dst, cs=cs: tt(dst, ps[:, 14:16], cs[:, 14:16], ALU.mult))

        def pool_ops(s, first_tile_fix):
            W_ = s.WU
            L = s.L
            for g, w in enumerate((2, 4, 8, 16)):
                Ug = s.U[:, g * s.nseq * W_:(g + 1) * s.nseq * W_]
                sA = s.f32v("sA", 0, s.nseq * W_)
                sB = s.f32v("sB", 0, s.nseq * W_)
                cur = Ug
                bufs = [sA, sB]
                k = 1
                bi = 0
                while k < w:
                    nxt = bufs[bi]
                    lo = 2 * k - 1 if k > 1 else 1
                    (ptt if USE_POOL_ADDS else tt)(seq3(nxt, s.nseq, W_, lo, W_), seq3(cur, s.nseq, W_, lo, W_),
                       seq3(cur, s.nseq, W_, lo - k, W_ - k), ALU.add)
                    cur = nxt
                    bi ^= 1
                    k *= 2
                pooled = s.bfv("pooled", g * s.N, s.N)
                stt(tok3(pooled, s.nseq, L), seq3(cur, s.nseq, W_, 16, 16 + L), 1.0 / w,
                    seq3(Ug, s.nseq, W_, 16, 16 + L), ALU.mult, ALU.subtract)
                if first_tile_fix:
                    tmp = s.f32v("rstdF", 0, 16)
                    tt(tmp, cur[:, 16:32], invcs[:, g * 16:(g + 1) * 16], ALU.mult)
                    tt(pooled[:, 0:16], tmp, Ug[:, 16:32], ALU.subtract)

        def pool_map(s):
            pend = None
            for g in range(4):
                b = fbank()
                out = b[:, 0:s.N]
                mm(out, wpool[:, g * 128:(g + 1) * 128], s.bfv("pooled", g * s.N, s.N), True, True)
                ym = s.f32v("Ym", g * s.N, s.N)
                sq = s.bfv("sq", (g % 2) * s.N, s.N)
                act(ym, out, AF.Copy, scale=PS(g))
                act(sq, out, AF.Square, scale=PS(g))
                if pend is not None:
                    pend()

                def fin(g=g, ym=ym, sq=sq):
                    b2 = fbank()
                    o2 = b2[:, 0:s.N]
                    mm(o2, ones[:, :], sq, True, True)
                    rf = s.f32v("rstdF", (g % 2) * s.N, s.N)
                    act(rf, o2, AF.Ln, scale=1.0 / 128, bias=epsb[:, 0:1])
                    act(rf, rf, AF.Exp, scale=-0.5)
                    stt(s.merged(g), ym, GPO(g), rf, ALU.mult, ALU.mult)
                pend = fin
            pend()

        def conv_ops(s):
            L = s.L
            for cc in range(4):
                z = s.f32v("Z", cc * s.N, s.N)
                z3 = tok3(z, s.nseq, L)
                act(z3, Vv(s, cc, 0, L), AF.Copy, scale=WC(0, cc))
                stt(z3, Vv(s, cc, 1, 1 + L), WC(1, cc), z3, ALU.mult, ALU.add)
                stt(z3, Vv(s, cc, 2, 2 + L), WC(2, cc), z3, ALU.mult, ALU.add)

        def mix_B(s, slot):
            pend = None
            for cc in range(4):
                z = s.f32v("Z", cc * s.N, s.N)
                y = s.f32v("Y", cc * s.N, s.N)
                sq = s.bfv("sq", (cc % 2) * s.N, s.N)
                rf = s.f32v("rstdF", (cc % 2) * s.N, s.N)

                def ev(ps, z=z, y=y, sq=sq):
                    tt(y, ps, z, ALU.mult)
                    act(sq, y, AF.Square)
                proj_chunk(s, slot, cc, ev)
                if pend is not None:
                    pend()

                def fin(y=y, sq=sq, rf=rf, cc=cc):
                    b2 = fbank()
                    o2 = b2[:, 0:s.N]
                    mm(o2, bd64[:, :], sq, True, True)
                    act(rf, o2, AF.Ln, scale=1.0 / 64, bias=epsb[:, 0:1])
                    act(rf, rf, AF.Exp, scale=-0.5)
                    stt(s.merged(4 + cc), y, GCO(cc), rf, ALU.mult, ALU.mult)
                pend = fin
            pend()

        def evac_tok(s, i, h, ps, gp):
            r = s.rows
            act(junk[0:r, 0:512], ps, AF.Square, accum=s.st[0:r, i * 16 + 1 + h:i * 16 + 2 + h])
            O = s.f32v("O", i * D + h * 512, 512)[0:r, :]
            act(O, ps, AF.Copy)
            tt(O, O, gp[0:r, h * 512:(h + 1) * 512], ALU.mult)

        def post(s, i):
            r = s.rows
            ss = s.st[0:r, i * 16 + 3:i * 16 + 4]
            act(ss, s.st[0:r, i * 16 + 1:i * 16 + 2], AF.Identity, bias=s.st[0:r, i * 16 + 2:i * 16 + 3])
            rs = rstd_rows(s, i, ss, r, 6, 8)
            X = Xs(s, i)
            O = s.f32v("O", i * D, D)[0:r, :]
            stt(X, O, rs, X, ALU.mult, ALU.add)

        tb_rr = [0]

        def tok_out1(segs, src_chunks, gpost_name, next_pre, mid_tail=None):
            gp = gpost[gpost_name]
            slots = [stream.next(), stream.next()]
            pend = None
            pend1 = None
            pend2 = []
            for s in reversed(segs):
                for i in range(s.nsub):
                    r = s.rows
                    for h in range(2):
                        w = W3(slots[h])
                        bk = TB[tb_rr[0] % 4]
                        tb_rr[0] += 1
                        out = bk[0:r, :]
                        for kc in range(8):
                            mm(out, src_chunks(s, kc)[:, i * 128:i * 128 + r], w[:, kc, :], kc == 0, kc == 7)
                        evac_tok(s, i, h, out, gp)
                    post(s, i)
                    if pend is not None:
                        pend()
                        pend = None
                    if pend1 is not None:
                        pend1()
                    pend_next = (lambda s=s, i=i: pre2(s, i, next_pre))

                    def p1(s=s, i=i, pn=pend_next):
                        pre1(s, i, next_pre)
                        pend2.append(pn)
                    pend1 = p1
                    if pend2:
                        pend = pend2.pop(0)
            if pend is not None:
                pend()
            if pend1 is not None:
                pend1()
            if mid_tail is not None:
                mid_tail()
            while pend2:
                pend2.pop(0)()
            stream.done_one()
            stream.done_one()

        def tok_out4(segs, src_chunks, gpost_name, hooks):
            gp = gpost[gpost_name]
            nq = 4
            fb_wide[0] = False
            for h in range(2):
                banks_used = {}
                for q in range(nq):
                    slot = stream.next()
                    w = W3(slot)
                    for s in segs:
                        for i in range(s.nsub):
                            if s is prm:
                                bk = TB[i]
                            else:
                                key = (s.name, i)
                                if key not in banks_used:
                                    banks_used[key] = fbank()
                                bk = banks_used[key]
                            out = bk[0:s.rows, :]
                            for kc in range(8):
                                lhsT = src_chunks(s, q * 8 + kc)[:, i * 128:i * 128 + s.rows]
                                mm(out, lhsT, w[:, kc, :], q == 0 and kc == 0, q == nq - 1 and kc == 7)
                    stream.done_one()
                    if (h, q) in hooks:
                        hooks[(h, q)]()
                for s in segs:
                    for i in range(s.nsub):
                        bk = TB[i] if s is prm else banks_used[(s.name, i)]
                        evac_tok(s, i, h, bk[0:s.rows, :], gp)
            for s in segs:
                for i in range(s.nsub):
                    post(s, i)
            fb_wide[0] = True

        def attention(s, kTs, vSs):
            nb = len(kTs)
            cols = s.N // nb
            steps = [(hh, bi) for hh in range(4) for bi in range(nb)]
            pend = None
            for si, (hh, bi) in enumerate(steps):
                par = si % 2
                kT = kTs[bi]; vS = vSs[bi]
                c0 = bi * cols
                q = [s.bfv("qT", (2 * hh + e2) * s.N + c0, cols) for e2 in range(2)]
                pts = []
                for mc in range(2):
                    b = fbank()
                    out = b[:, 0:cols]
                    for e2 in range(2):
                        j = 2 * hh + e2
                        mm(out, kT[:, j * 256 + mc * 128: j * 256 + (mc + 1) * 128], q[e2], e2 == 0, e2 == 1)
                    pt = s.bfv("pT", (mc * 2 + par) * s.N + c0, cols)
                    act(pt, out, AF.Exp, scale=1.0 / 16.0)
                    pts.append(pt)
                if pend is not None:
                    pend()

                def fin(hh=hh, c0=c0, vS=vS, pts=pts, par=par):
                    b = fbank()
                    den = b[:, 0:cols]
                    for mc in range(2):
                        mm(den, ones[:, :], pts[mc], mc == 0, mc == 1)
                    rd = s.f32v("rden", par * s.N + c0, cols)
                    act(rd, den, AF.Ln)
                    act(rd, rd, AF.Exp, scale=-1.0)
                    for e2 in range(2):
                        j = 2 * hh + e2
                        b = fbank()
                        out = b[:, 0:cols]
                        for mc in range(2):
                            mm(out, vS[:, mc * D + j * 128: mc * D + (j + 1) * 128], pts[mc], mc == 0, mc == 1)
                        tt(s.merged(j)[:, c0:c0 + cols], out, rd, ALU.mult)
                pend = fin
            pend()

        Xb1b = prm.Xb[1].bitcast(BF16)
        kT_s = [Xb1b[:, b * 2048:(b + 1) * 2048] for b in range(2)]
        vS_s = [Xb1b[:, 4096 + b * 2048:4096 + (b + 1) * 2048] for b in range(2)]
        memXb = memX.bitcast(BF16)
        ckbs = [memXb[:, 0:2048], memXb[:, 2048:4096]]

        def cast_block(blk):
            wn, r, c = blk
            src = wd[wn][r * 1024:(r + 1) * 1024, c * 512:(c + 1) * 512].rearrange("(kc p) n -> p kc n", p=128)
            i = bidx[blk]
            dst = wsc[i].rearrange("p (kc n) -> p kc n", kc=8)
            P.add("pool", lambda e, src=src, dst=dst: e.dma_start(out=dst, in_=src), writes=[wsc[i]], dma=True)

        memseg = Seg()
        memseg.rows = 128; memseg.nsub = 2; memseg.N = 256; memseg.name = 'mem'
        memseg.X = memX; memseg.Xb = [memX]; memseg.cur = 0
        memseg.xs = xs_g; memseg.xT = mTt; memseg.st = sb("mst", [128, 32])
        stream.prefetch(RING - 1)
        cast_wpool()
        def sample_cache_casts():
            for b in range(2):
                ckb = ckbs[b]
                P.add("pool", lambda e, b=b, ckb=ckb: e.dma_start(out=ckb.rearrange("p (mc n) -> p mc n", mc=2),
                                                         in_=ck[b].rearrange("(mc p) n -> p mc n", p=128)),
                      writes=[ckb], dma=True)
                P.add("pool", lambda e, b=b: e.dma_start(out=vS_s[b].rearrange("p (mc n) -> p mc n", mc=2),
                                                         in_=cv[b].rearrange("(mc p) n -> p mc n", p=128)),
                      writes=[vS_s[b]], dma=True)
        stT = prm.un[0:32, 6144:6656]; scT = prm.un[0:4, 6656:7168]
        vmemset(stT[:, :], 0.0)
        for b in range(2):
            dma("sp", stT[b * 16 + 1:b * 16 + 16, :], spool[b], rd=False)
        dma("sp", scT[:, :], sconv.rearrange("b r c -> (b r) c"), rd=False)
        for g in range(4):
            tr(RB[:, 0:32], stT[0:32, g * 128:(g + 1) * 128], idf[0:32, 0:32])
            vcopy(Uv(smp, g, 0, 16), RB[:, 0:32].rearrange("p (b c) -> p b c", b=2))
            tr(RB[:, 32:36], scT[0:4, g * 128:(g + 1) * 128], idf[0:4, 0:4])
            vcopy(Vv(smp, g, 0, 2), RB[:, 32:36].rearrange("p (b c) -> p b c", b=2))

        def kv_phase():
            for b in range(2):
                ckb = ckbs[b]
                for mc in range(2):
                    for half in range(2):
                        for jj in range(4):
                            j = half * 4 + jj
                            tr(RBb[:, jj * 128:(jj + 1) * 128], ckb[:, mc * D + j * 128: mc * D + (j + 1) * 128], idb[:, :])
                        dst = kT_s[b].rearrange("p (j m) -> p j m", j=8)[:, half * 4:(half + 1) * 4, mc * 128:(mc + 1) * 128]
                        src = RBb[:, 0:512].rearrange("p (j m) -> p j m", j=4)
                        vcopy(dst, src)
            KO = prm.un[:, 2 * D:3 * D]
            VO = prm.un[:, 3 * D:4 * D]
            mT = lambda kc: memseg.xT[:, kc * 256:(kc + 1) * 256]
            for half in range(2):
                slot = stream.next()
                w = W3(slot)
                for jj in range(4):
                    j = half * 4 + jj
                    b = fbank()
                    out = b[:, 0:256]
                    for kc in range(8):
                        mm(out, w[:, kc, jj * 128:(jj + 1) * 128], mT(kc), kc == 0, kc == 7)
                    act(kT_p[:, j * 256:(j + 1) * 256], out, AF.Copy)
                b = fbank()
                for kc in range(8):
                    mm(b[:, :], mT(kc)[:, 0:128], w[:, kc, :], kc == 0, kc == 7)
                vcopy(KO[:, half * 512:(half + 1) * 512], b[:, :])
                stream.done_one()
            dma("sp", o_mk[:, :], KO, wr=False)
            for half in range(2):
                slot = stream.next()
                w = W3(slot)
                for mc in range(2):
                    b = fbank()
                    for kc in range(8):
                        mm(b[:, :], mT(kc)[:, mc * 128:(mc + 1) * 128], w[:, kc, :], kc == 0, kc == 7)
                    if mc == 0:
                        act(VO[:, half * 512:(half + 1) * 512], b[:, :], AF.Copy)
                        vcopy(vS_p[:, mc * D + half * 512: mc * D + (half + 1) * 512], VO[:, half * 512:(half + 1) * 512])
                    else:
                        vcopy(vS_p[:, mc * D + half * 512: mc * D + (half + 1) * 512], b[:, :])
                stream.done_one()
            dma("sp", o_mv[:, :], VO, wr=False)

        ckpt(3)
        for s in (prm, smp):
            vmemset(s.U[:, :], 0.0) if s is prm else None
        vmemset(prm.V[:, :], 0.0)

        tp_rr = [0]
        def stage_A_loads(t):
            prm.cur = t % 2
            load_x(prm, lambda i, t=t: xp[t * TN + i * 128: t * TN + (i + 1) * 128, :])
            if t == 0:
                load_x(hal, lambda i: xh[:, :])

        for t in range(NT):
            segs = [prm] + ([smp] if t == 0 else [])
            prm.cur = t % 2
            if t == 0:
                stage_A_loads(0)
                prenorm(hal, "g_mix_pre")
                load_x(smp, lambda i: xsm[:, :])
                load_gpost()
                prenorm(prm, "g_mix_pre")
                prenorm(smp, "g_mix_pre")
                load_x(memseg, lambda i: mem[i * 128:(i + 1) * 128, :])
            else:
                for g in range(4):
                    vcopy(Uv(prm, g, 0, 16), Uv(prm, g, TN, TN + 16))
                    vcopy(Vv(prm, g, 0, 2), Vv(prm, g, TN, TN + 2))
            ckpt(4)
            slot = stream.next()
            if t == 0:
                mix_u(hal, slot, hist_to=prm)
            for s in segs:
                mix_u(s, slot)
            stream.done_one()
            for s in segs:
                pool_ops(s, first_tile_fix=(t == 0 and s is prm))
            slot = stream.next()
            if t == 0:
                mix_C(hal, slot)
            for s in segs:
                mix_C(s, slot)
            stream.done_one()
            slot = stream.next()
            if t == 0:
                mix_h(hal, slot, hist_to=prm)
            for s in segs:
                mix_h(s, slot)
            stream.done_one()
            for s in segs:
                conv_ops(s)
            for s in segs:
                pool_map(s)
            slot = stream.next()
            for s in segs:
                mix_B(s, slot)
            stream.done_one()
            ckpt(5)
            if t == 0:
                prenorm(memseg, "g_mem")
                sample_cache_casts()
            if t == NT - 1:
                stp = misc[0:16, :]; scp = misc[0:2, :]
                for g in range(4):
                    tr(RB[0:16, g * 128:(g + 1) * 128], Uv(prm, g, TN, TN + 16), idf[:, :])
                vcopy(stp[:, :], RB[0:16, :])
                dma("sp", o_pool_p[:, :], stp[:, :], wr=False)
                for g in range(4):
                    tr(RB[0:2, g * 128:(g + 1) * 128], Vv(prm, g, TN, TN + 2), idf[:, :])
                vcopy(scp[:, :], RB[0:2, :])
                dma("sp", o_conv_p[:, :], scp[:, :], wr=False)
            if t == 0:
                sts = misc[0:32, :]; scs = misc[0:4, :]
                utmp = sb("utmp", [128, 4 * 32]); vtmp = sb("vtmp", [128, 4 * 4])
                for g in range(4):
                    vcopy(utmp[:, g * 32:(g + 1) * 32].rearrange("p (b c) -> p b c", b=2), Uv(smp, g, 16, 32))
                    vcopy(vtmp[:, g * 4:(g + 1) * 4].rearrange("p (b c) -> p b c", b=2), Vv(smp, g, 16, 18))
                for g in range(4):
                    tr(RB[0:32, g * 128:(g + 1) * 128], utmp[:, g * 32:(g + 1) * 32], idf[:, :])
                vcopy(sts[:, :], RB[0:32, :])
                dma("sp", o_pool_s[:, :], sts[:, :], wr=False)
                for g in range(4):
                    tr(RB[0:4, g * 128:(g + 1) * 128], vtmp[:, g * 4:(g + 1) * 4], idf[:, :])
                vcopy(scs[:, :], RB[0:4, :])
                dma("sp", o_conv_s[:, :], scs[:, :], wr=False)
            split = {}

            def mid_first_block():
                split["slot"] = stream.next()
                split["outs"] = proj_split_begin(prm, split["slot"])
            tok_out1(segs, lambda s, kc: s.merged(kc), "g_mix_post", "g_x_pre",
                     mid_tail=(mid_first_block if t > 0 else None))
            ckpt(6)
            if t == 0:
                kv_phase()
            for half in range(2):
                pre = split.pop("slot", None) if half == 0 else None
                slot = pre if pre is not None else stream.next()
                for s in segs:
                    if pre is not None and s is prm:
                        evs = [(lambda ps, dst=s.bfv("qT", (half * 4 + jj) * s.N, s.N): act(dst, ps, AF.Copy))
                               for jj in range(4)]
                        proj_split_end(s, slot, split.pop("outs"), evs)
                        continue
                    for jj in range(4):
                        dst = s.bfv("qT", (half * 4 + jj) * s.N, s.N)
                        proj_chunk(s, slot, jj, lambda ps, dst=dst: act(dst, ps, AF.Copy))
                stream.done_one()
            if smp in segs:
                attention(smp, kT_s, vS_s)
            attention(prm, [kT_p], [vS_p])
            tok_out1(segs, lambda s, kc: s.merged(kc), "g_x_post", "g_ff_pre", mid_tail=mid_first_block)
            ckpt(7)
            if t + 1 < NT:
                stage_A_loads(t + 1)
                prm.cur = t % 2
            for c in range(8):
                pre = split.pop("slot", None) if c == 0 else None
                slot = pre if pre is not None else stream.next()
                for s in segs:
                    evs = []
                    for jj in range(4):
                        ch = c * 4 + jj
                        hid = s.bfv("hid", ch * s.N, s.N)
                        rt = s.bfv("rtmp", (ch % 2) * s.N, s.N)

                        def ev(ps, hid=hid, rt=rt):
                            act(rt, ps, AF.Relu)
                            tt(hid, rt, rt, ALU.mult)
                        evs.append(ev)
                    if pre is not None and s is prm:
                        proj_split_end(s, slot, split.pop("outs"), evs)
                    else:
                        for jj in range(4):
                            proj_chunk(s, slot, jj, evs[jj])
                stream.done_one()
            hooks = {}
            cur_t = t
            if t + 1 < NT:
                nt_ = t + 1

                def mk_hooks(nt_=nt_):
                    def h00():
                        prm.cur = nt_ % 2
                        pre1(prm, 0, "g_mix_pre")
                        prm.cur = cur_t % 2

                    def mid(i):
                        def f():
                            prm.cur = nt_ % 2
                            pre2(prm, i - 1, "g_mix_pre")
                            if i < 4:
                                pre1(prm, i, "g_mix_pre")
                            prm.cur = cur_t % 2
                        return f
                    return {(0, 0): h00, (0, 1): mid(1), (0, 2): mid(2), (0, 3): mid(3), (1, 0): mid(4)}
                hooks = mk_hooks()
            prm.cur = t % 2
            tok_out4(segs, lambda s, kc: s.bfv("hid", kc * s.N, s.N), "g_ff_post", hooks)
            ckpt(8)
            prm.cur = t % 2
            for i in range(4):
                dma("sp", yp[t * TN + i * 128: t * TN + (i + 1) * 128, :], Xs(prm, i), wr=False)
            if smp in segs:
                dma("sp", ysm[:, :], smp.X[0:32, 0:D], wr=False)
    except _Stop:
        pass
    with nc.allow_low_precision("bf16 matmul operands, fp32 accumulation"):
        P.emit()
    return nc, P


_CACHE = {}


def _get_program():
    if "nc" not in _CACHE:
        _CACHE["nc"], _CACHE["P"] = build_program()
    return _CACHE["nc"]


def kernel(x_prompt, x_sample, state_pool, state_conv, cache_mem_k, cache_mem_v, mem_prompt,
           g_mix_pre, w_in, w_pool, pool_scale, w_conv, g_pool_out, g_conv_out, w_out,
           g_mix_post, g_mem, w_k, w_v, g_x_pre, w_q, w_co, g_x_post,
           g_ff_pre, w_up, w_down, g_ff_post):
    f = lambda a: np.ascontiguousarray(np.asarray(a, dtype=np.float32))
    x_prompt = f(x_prompt); x_sample = f(x_sample)
    nc = _get_program()
    col = lambda v: f(v).reshape(-1, 128).T
    pvec = np.concatenate([col(pool_scale[0]), col(w_conv[0, 0]), col(w_conv[0, 1]), col(w_conv[0, 2]),
                           col(g_pool_out[0]), col(g_conv_out[0])], axis=1)
    gcolv = np.concatenate([col(g_mix_pre[0]), col(g_x_pre[0]), col(g_ff_pre[0]), col(g_mem[0])], axis=1)
    shared = {
        "pvec": f(pvec), "gcol": f(gcolv), "ident": np.eye(128, dtype=np.float32),
        "g_mix_pre": f(g_mix_pre), "g_mix_post": f(g_mix_post), "g_x_pre": f(g_x_pre), "g_x_post": f(g_x_post),
        "g_ff_pre": f(g_ff_pre), "g_ff_post": f(g_ff_post), "g_mem": f(g_mem),
        "w_in": f(w_in[0]), "w_out": f(w_out[0]), "w_k": f(w_k[0]).reshape(D, D), "w_v": f(w_v[0]).reshape(D, D),
        "w_q": f(w_q[0]).reshape(D, D), "w_co": f(w_co[0]).reshape(D, D), "w_up": f(w_up[0]),
        "w_down": f(w_down[0]), "w_pool": f(w_pool[0]),
    }
    pos = np.arange(16)
    inv_start = np.stack([1.0 / np.minimum(pos + 1, w) for w in (2, 4, 8, 16)]).astype(np.float32)
    inv_mid = np.stack([np.full(16, 1.0 / w) for w in (2, 4, 8, 16)]).astype(np.float32)
    in_maps = []
    for c in range(NCORES):
        b, half = c // 2, c % 2
        t0 = half * TP
        m = f(mem_prompt[b])
        if half == 1:
            m = np.concatenate([m[128:], m[:128]], axis=0)
        d = dict(shared)
        d["xp"] = f(x_prompt[b, t0:t0 + TP])
        d["xh"] = f(x_prompt[b, t0 - XH:t0]) if half == 1 else np.zeros((XH, D), np.float32)
        d["xsm"] = f(x_sample[2 * c:2 * c + 2]).reshape(32, D)
        d["spool"] = f(state_pool[0, 2 * c:2 * c + 2]); d["sconv"] = f(state_conv[0, 2 * c:2 * c + 2])
        d["ck"] = f(cache_mem_k[0, 2 * c:2 * c + 2]).reshape(2, 256, D)
        d["cv"] = f(cache_mem_v[0, 2 * c:2 * c + 2]).reshape(2, 256, D)
        d["mem"] = f(m)
        tab = inv_start if half == 0 else inv_mid
        d["invc"] = np.ascontiguousarray(np.broadcast_to(tab.reshape(1, 64), (128, 64)))
        in_maps.append(d)
    res = run_bass_kernel_spmd(nc, in_maps, core_ids=list(range(NCORES)))
    R = res.results
    yp = np.zeros((4, 4096, D), np.float32); ys = np.zeros((16, 16, D), np.float32)
    pool_p = np.zeros((1, 4, 15, 512), np.float32); conv_p = np.zeros((1, 4, 2, 512), np.float32)
    mk = np.zeros((1, 4, 256, 4, 256), np.float32); mv = np.zeros((1, 4, 256, 4, 256), np.float32)
    pool_s = np.zeros((1, 16, 15, 512), np.float32); conv_s = np.zeros((1, 16, 2, 512), np.float32)
    for c in range(NCORES):
        b, half = c // 2, c % 2
        r = R[c]
        yp[b, half * TP:(half + 1) * TP] = r["yp"]
        ys[2 * c:2 * c + 2] = r["ysm"].reshape(2, 16, D)
        if half == 1:
            pool_p[0, b] = r["pool_p"][1:16]
            conv_p[0, b] = r["conv_p"]
        mk[0, b, half * 128:(half + 1) * 128] = r["mk"].reshape(128, 4, 256)
        mv[0, b, half * 128:(half + 1) * 128] = r["mv"].reshape(128, 4, 256)
        pool_s[0, 2 * c:2 * c + 2] = r["pool_s"].reshape(2, 16, 512)[:, 1:16]
        conv_s[0, 2 * c:2 * c + 2] = r["conv_s"].reshape(2, 2, 512)
    return yp, ys, pool_p, conv_p, mk, mv, pool_s, conv_s
```
